# Optimizing a Trainium2 kernel written in Bass

```python
import jax
import jax.numpy as jnp
from jax import lax
import numpy as np

D_MODEL = 1024
BATCH = 16
SEQ = 2048
DEPTH = 2

N_BRANCH = 4
BRANCH_W = 512
EPS = 1e-6
ROPE_THETA = 10000.0
CHUNK = 128

CONV_DIM = 512
CONV_W = 3
RET_HEADS = 4
RET_DK = 64
RET_DV = 128
SG_DIM = 512
SG_GROUPS = 4
SG_GROUP_DIM = SG_DIM // SG_GROUPS
MLA_HEADS = 8
MLA_NOPE = 64
MLA_ROPE = 32
MLA_V = 64
MLA_Q_RANK = 384
MLA_KV_RANK = 256
Q_BLOCK = 128
D_FF = ((8 * D_MODEL // 3 + 255) // 256) * 256

SPLIT_SIZES = (CONV_DIM, CONV_DIM, CONV_DIM,
               RET_HEADS * RET_DK, RET_HEADS * RET_DK, RET_HEADS * RET_DV, RET_HEADS * RET_DV,
               SG_DIM, SG_DIM,
               MLA_Q_RANK, MLA_KV_RANK, MLA_ROPE,
               N_BRANCH * D_MODEL)
N_IN = int(sum(SPLIT_SIZES))
SPLIT_POINTS = tuple(int(v) for v in np.cumsum(SPLIT_SIZES)[:-1])

kernel_name = 'hybrid_gated_mixer_block'


def rms_norm(x, g):
    xf = x.astype(jnp.float32)
    y = xf * lax.rsqrt(jnp.mean(xf * xf, axis=-1, keepdims=True) + EPS)
    return (y * g.astype(jnp.float32)).astype(x.dtype)


def ln_plain(x):
    xf = x.astype(jnp.float32)
    mu = jnp.mean(xf, axis=-1, keepdims=True)
    var = jnp.mean(jnp.square(xf - mu), axis=-1, keepdims=True)
    return (xf - mu) * lax.rsqrt(var + EPS)


def layer_norm(x, g, b):
    return (ln_plain(x) * g.astype(jnp.float32) + b.astype(jnp.float32)).astype(x.dtype)


def rope_tables(positions, dim):
    inv = ROPE_THETA ** (-jnp.arange(0, dim, 2, dtype=jnp.float32) / dim)
    ang = positions.astype(jnp.float32)[..., None] * inv
    return jnp.cos(ang), jnp.sin(ang)


def apply_rope(x, cos, sin):
    c = cos[:, :, None, :]
    s = sin[:, :, None, :]
    x1, x2 = jnp.split(x, 2, axis=-1)
    out = jnp.concatenate([x1 * c - x2 * s, x1 * s + x2 * c], axis=-1)
    return out.astype(x.dtype)


def causal_conv(x, w):
    seq = x.shape[1]
    xp = jnp.pad(x, ((0, 0), (CONV_W - 1, 0), (0, 0)))
    return sum(w[i] * xp[:, i:i + seq] for i in range(CONV_W))


def retention(q, k, v, cos, sin):
    bsz, seq = q.shape[0], q.shape[1]
    n_chunks = seq // CHUNK
    q = apply_rope(q, cos, sin)
    k = apply_rope(k, cos, sin) * (RET_DK ** -0.5)
    q = q.reshape(bsz, n_chunks, CHUNK, RET_HEADS, RET_DK)
    k = k.reshape(bsz, n_chunks, CHUNK, RET_HEADS, RET_DK)
    v = v.reshape(bsz, n_chunks, CHUNK, RET_HEADS, RET_DV)
    log_gamma = jnp.log1p(-jnp.exp2(-5.0 - jnp.arange(RET_HEADS, dtype=jnp.float32)))
    idx = jnp.arange(CHUNK, dtype=jnp.float32)
    diff = idx[:, None] - idx[None, :]
    decay = jnp.where(diff >= 0, jnp.exp(jnp.maximum(diff, 0.0)[None] * log_gamma[:, None, None]), 0.0)
    scores = jnp.einsum('bnihd,bnjhd->bnhij', q, k) * decay
    o_inner = jnp.einsum('bnhij,bnjhe->bnihe', scores, v)
    zeta = jnp.exp((CHUNK - 1 - idx)[:, None] * log_gamma[None, :])
    xi = jnp.exp((idx + 1.0)[:, None] * log_gamma[None, :])
    chunk_decay = jnp.exp(CHUNK * log_gamma)
    kv = jnp.einsum('bnjhd,bnjhe,jh->nbhde', k, v, zeta).astype(jnp.float32)

    def step(state, kv_n):
        return chunk_decay[None, :, None, None] * state + kv_n, state

    init = jnp.zeros((bsz, RET_HEADS, RET_DK, RET_DV), jnp.float32)
    _, prev_states = lax.scan(step, init, kv)
    o_cross = jnp.einsum('bnihd,nbhde,ih->bnihe', q, prev_states, xi)
    o = ln_plain(o_inner + o_cross).astype(v.dtype)
    return o.reshape(bsz, seq, RET_HEADS * RET_DV)


def spatial_gating(u, v, ln_g, ln_b, ws, bs):
    bsz, seq = u.shape[0], u.shape[1]
    n_chunks = seq // CHUNK
    v = layer_norm(v, ln_g, ln_b).reshape(bsz, n_chunks, CHUNK, SG_GROUPS, SG_GROUP_DIM)
    w = jnp.tril(ws)
    s = jnp.einsum('gij,bnjgd->bnigd', w, v) + bs.T[None, None, :, :, None]
    return u * s.reshape(bsz, seq, SG_DIM)


def mla(c_q, c_kv, k_pe, q_norm, w_uq, kv_norm, w_ukv, cos, sin):
    bsz, seq = c_q.shape[0], c_q.shape[1]
    q = (rms_norm(c_q, q_norm) @ w_uq).reshape(bsz, seq, MLA_HEADS, MLA_NOPE + MLA_ROPE)
    q_nope, q_pe = q[..., :MLA_NOPE], apply_rope(q[..., MLA_NOPE:], cos, sin)
    kv = (rms_norm(c_kv, kv_norm) @ w_ukv).reshape(bsz, seq, MLA_HEADS, MLA_NOPE + MLA_V)
    k_nope, v = kv[..., :MLA_NOPE], kv[..., MLA_NOPE:]
    k_pe = apply_rope(k_pe[:, :, None, :], cos, sin)[:, :, 0]
    scale = (MLA_NOPE + MLA_ROPE) ** -0.5
    outs = []
    for blk in range(seq // Q_BLOCK):
        q0 = blk * Q_BLOCK
        kend = q0 + Q_BLOCK
        s = (jnp.einsum('bqhd,bkhd->bhqk', q_nope[:, q0:kend], k_nope[:, :kend])
             + jnp.einsum('bqhd,bkd->bhqk', q_pe[:, q0:kend], k_pe[:, :kend]))
        s = s.astype(jnp.float32) * scale
        mask = jnp.arange(kend)[None, :] <= (q0 + jnp.arange(Q_BLOCK))[:, None]
        s = jnp.where(mask, s, jnp.float32(-1e30))
        p = jax.nn.softmax(s, axis=-1).astype(v.dtype)
        outs.append(jnp.einsum('bhqk,bkhd->bqhd', p, v[:, :kend]))
    o = jnp.concatenate(outs, axis=1)
    return o.reshape(bsz, seq, MLA_HEADS * MLA_V)


def hybrid_layer(x, cos_r, sin_r, cos_m, sin_m, norm_mix, w_in, b_gate, conv_w,
                 sg_ln_g, sg_ln_b, sg_ws, sg_b, mla_q_norm, mla_w_uq, mla_kv_norm, mla_w_ukv,
                 w_branch, w_out, norm_ffn, w_ffn_in, w_ffn_out):
    bsz, seq = x.shape[0], x.shape[1]
    h = rms_norm(x, norm_mix)
    proj = h @ w_in
    (a_b, a_c, a_x, r_q, r_k, r_v, r_g, s_u, s_v,
     m_cq, m_ckv, m_kpe, gate_logits) = jnp.split(proj, SPLIT_POINTS, axis=-1)
    y_a = a_b * causal_conv(a_c * a_x, conv_w)
    y_r = jax.nn.silu(r_g) * retention(
        r_q.reshape(bsz, seq, RET_HEADS, RET_DK), r_k.reshape(bsz, seq, RET_HEADS, RET_DK),
        r_v.reshape(bsz, seq, RET_HEADS, RET_DV), cos_r, sin_r)
    y_s = spatial_gating(jax.nn.gelu(s_u), jax.nn.gelu(s_v), sg_ln_g, sg_ln_b, sg_ws, sg_b)
    y_m = mla(m_cq, m_ckv, m_kpe, mla_q_norm, mla_w_uq, mla_kv_norm, mla_w_ukv, cos_m, sin_m)
    gates = jax.nn.sigmoid(gate_logits.reshape(bsz, seq, N_BRANCH, D_MODEL) + b_gate)
    merged = sum(gates[:, :, i] * (y @ w_branch[i]) for i, y in enumerate((y_a, y_r, y_s, y_m)))
    x = x + merged @ w_out
    g, u = jnp.split(rms_norm(x, norm_ffn) @ w_ffn_in, 2, axis=-1)
    return x + (jax.nn.silu(g) * u) @ w_ffn_out


def setup_inputs(seed: int = 0) -> dict:
    key = jax.random.key(seed)
    ks = jax.random.split(key, 24)
    f32 = jnp.float32

    def nrm(k, shape, scale):
        return jax.random.normal(k, shape, f32) * scale

    def gain(k, shape):
        return 1.0 + 0.02 * jax.random.normal(k, shape, f32)

    offsets = jax.random.randint(ks[1], (BATCH, 1), 0, 4096, dtype=jnp.int32)
    positions = offsets + jnp.arange(SEQ, dtype=jnp.int32)[None, :]
    return {
        'x': jax.random.normal(ks[0], (BATCH, SEQ, D_MODEL), f32),
        'positions': positions,
        'norm_mix': gain(ks[2], (DEPTH, D_MODEL)),
        'w_in': nrm(ks[3], (DEPTH, D_MODEL, N_IN), D_MODEL ** -0.5),
        'b_gate': nrm(ks[4], (DEPTH, N_BRANCH, D_MODEL), 0.02),
        'conv_w': nrm(ks[5], (DEPTH, CONV_W, CONV_DIM), CONV_W ** -0.5),
        'sg_ln_g': gain(ks[6], (DEPTH, SG_DIM)),
        'sg_ln_b': nrm(ks[7], (DEPTH, SG_DIM), 0.02),
        'sg_ws': nrm(ks[8], (DEPTH, SG_GROUPS, CHUNK, CHUNK), CHUNK ** -0.5),
        'sg_b': gain(ks[9], (DEPTH, SG_GROUPS, CHUNK)),
        'mla_q_norm': gain(ks[10], (DEPTH, MLA_Q_RANK)),
        'mla_w_uq': nrm(ks[11], (DEPTH, MLA_Q_RANK, MLA_HEADS * (MLA_NOPE + MLA_ROPE)), MLA_Q_RANK ** -0.5),
        'mla_kv_norm': gain(ks[12], (DEPTH, MLA_KV_RANK)),
        'mla_w_ukv': nrm(ks[13], (DEPTH, MLA_KV_RANK, MLA_HEADS * (MLA_NOPE + MLA_V)), MLA_KV_RANK ** -0.5),
        'w_branch': nrm(ks[14], (DEPTH, N_BRANCH, BRANCH_W, D_MODEL), BRANCH_W ** -0.5),
        'w_out': nrm(ks[15], (DEPTH, D_MODEL, D_MODEL), D_MODEL ** -0.5),
        'norm_ffn': gain(ks[16], (DEPTH, D_MODEL)),
        'w_ffn_in': nrm(ks[17], (DEPTH, D_MODEL, 2 * D_FF), D_MODEL ** -0.5),
        'w_ffn_out': nrm(ks[18], (DEPTH, D_FF, D_MODEL), D_FF ** -0.5),
        'final_norm': gain(ks[19], (D_MODEL,)),
    }


def reference(x, positions, norm_mix, w_in, b_gate, conv_w, sg_ln_g, sg_ln_b, sg_ws, sg_b,
              mla_q_norm, mla_w_uq, mla_kv_norm, mla_w_ukv, w_branch, w_out, norm_ffn,
              w_ffn_in, w_ffn_out, final_norm):
    cos_r, sin_r = rope_tables(positions, RET_DK)
    cos_m, sin_m = rope_tables(positions, MLA_ROPE)
    for l in range(DEPTH):
        x = hybrid_layer(x, cos_r, sin_r, cos_m, sin_m, norm_mix[l], w_in[l], b_gate[l], conv_w[l],
                         sg_ln_g[l], sg_ln_b[l], sg_ws[l], sg_b[l], mla_q_norm[l], mla_w_uq[l],
                         mla_kv_norm[l], mla_w_ukv[l], w_branch[l], w_out[l], norm_ffn[l],
                         w_ffn_in[l], w_ffn_out[l])
    return rms_norm(x, final_norm)
```

```python
import math
import contextlib
import numpy as np
import concourse.bass as bass
import concourse.mybir as mybir
from concourse.bass_utils import run_bass_kernel_spmd

dt = mybir.dt
F32, BF16, I32 = dt.float32, dt.bfloat16, dt.int32
AF = mybir.ActivationFunctionType
ALU = mybir.AluOpType

ENGS = ("pe", "act", "dve", "pool", "sp")
NDMASEM = 8


class Buf:
    __slots__ = ("name", "lw", "rd", "excl")

    def __init__(self, name="", excl=False):
        self.name = name
        self.lw = None
        self.rd = []
        self.excl = excl


def bufs(n, name=""):
    return [Buf(f"{name}{i}") for i in range(n)]


class Prog:
    def __init__(self, nc):
        self.nc = nc
        self.ops = []

    def add(self, eng, fn, reads=(), writes=(), dma=False):
        self.ops.append((eng, fn, tuple(reads), tuple(writes), dma))

    def pe(self, fn, r=(), w=()): self.add("pe", fn, r, w)
    def act(self, fn, r=(), w=()): self.add("act", fn, r, w)
    def dve(self, fn, r=(), w=()): self.add("dve", fn, r, w)
    def pool(self, fn, r=(), w=()): self.add("pool", fn, r, w)

    def dma(self, q, out, in_, r=(), w=()):
        self.add(q, lambda e: e.dma_start(out=out, in_=in_), r, w, dma=True)

    def fence(self):
        self.ops.append(("fence", None, (), (), False))

    def emit(self):
        nc = self.nc
        ops = self.ops
        n = len(ops)
        deps = [None] * n
        signaling = [False] * n
        last_eng = {}
        last_slot = {}
        dcount = {}
        fence_deps = []
        for k, (eng, fn, reads, writes, is_dma) in enumerate(ops):
            if eng == "fence":
                fence_deps = list(last_eng.values()) + list(last_slot.values())
                deps[k] = []
                continue
            raw = set()
            oth = set()
            if fence_deps and any((b.lw is None and not b.rd) for b in writes):
                oth.update(fence_deps)
            if is_dma:
                j = dcount.get(eng, 0)
                dcount[eng] = j + 1
                last_slot[(eng, j % NDMASEM)] = k
            else:
                last_eng[eng] = k
            for b in reads:
                if b.lw is not None:
                    raw.add(b.lw)
                if b.excl:
                    oth.update(b.rd)
            for b in writes:
                if b.lw is not None:
                    oth.add(b.lw)
                for r_ in b.rd:
                    oth.add(r_)
            for b in reads:
                b.rd.append(k)
            for b in writes:
                b.lw = k
                b.rd = []
            best = {}
            dl = []
            for p in raw | oth:
                if p == k:
                    continue
                peng, _, _, _, pdma = ops[p]
                if pdma:
                    dl.append(p)
                    continue
                if (not is_dma) and peng == eng:
                    if eng == "pe" or p not in raw:
                        continue
                if peng not in best or best[peng] < p:
                    best[peng] = p
            dl.extend(best.values())
            for p in dl:
                if not ops[p][4]:
                    signaling[p] = True
            deps[k] = dl
        cnt = {e: 0 for e in ENGS}
        sigval = [None] * n
        dmacount = {}
        slotuses = {}
        dma_prev = [None] * n
        for k, (eng, fn, reads, writes, is_dma) in enumerate(ops):
            if eng == "fence":
                continue
            if is_dma:
                j = dmacount.get(eng, 0)
                dmacount[eng] = j + 1
                slot = j % NDMASEM
                u = slotuses.get((eng, slot), 0)
                if u > 0:
                    dma_prev[k] = (("dma", eng, slot), 16 * u)
                slotuses[(eng, slot)] = u + 1
                sigval[k] = (("dma", eng, slot), 16 * (u + 1))
            elif signaling[k]:
                cnt[eng] += 1
                sigval[k] = (("eng", eng), cnt[eng])
        per = {e: [] for e in ENGS}
        seen = {e: {} for e in ENGS}
        for k, (eng, fn, reads, writes, is_dma) in enumerate(ops):
            if eng == "fence":
                continue
            waits = []
            cand = [sigval[p] for p in deps[k]]
            if dma_prev[k] is not None:
                cand.append(dma_prev[k])
            for (sk, v) in cand:
                if seen[eng].get(sk, 0) >= v:
                    continue
                seen[eng][sk] = v
                waits.append((sk, v))
            per[eng].append((waits, fn, sigval[k], is_dma))
        final_waits = [(("dma", q, s), 16 * u) for (q, s), u in slotuses.items()]
        self.stats = (n, sum(signaling), {e: len(per[e]) for e in ENGS})

        stack = contextlib.ExitStack()
        sems = {}
        with stack:
            for e in ENGS:
                sems[("eng", e)] = stack.enter_context(nc.semaphore("s_" + e))
            for q in dmacount:
                for s in range(NDMASEM):
                    sems[("dma", q, s)] = stack.enter_context(nc.semaphore(f"d_{q}_{s}"))
            block = stack.enter_context(nc.Block())

            def run(engname, e):
                for waits, fn, inc, is_dma in per[engname]:
                    for (sk, v) in waits:
                        e.wait_ge(sems[sk], v)
                    ins = fn(e)
                    if inc is not None:
                        ins.then_inc(sems[inc[0]], 16 if is_dma else 1)
                if engname == "sp":
                    for (sk, v) in final_waits:
                        e.wait_ge(sems[sk], v)

            @block.tensor
            def _(e): run("pe", e)

            @block.scalar
            def _(e): run("act", e)

            @block.vector
            def _(e): run("dve", e)

            @block.gpsimd
            def _(e): run("pool", e)

            @block.sync
            def _(e): run("sp", e)


class Arena:
    def __init__(self, t, nbytes, prog=None):
        self.t = t
        self.nbytes = nbytes
        self.off = 0
        self.marks = []
        self.peak = 0
        self.prog = prog

    def alloc(self, shape_free, dtype, parts=128):
        if isinstance(shape_free, int):
            shape_free = (shape_free,)
        esz = mybir.dt.size(dtype)
        nel = int(np.prod(shape_free))
        nb = (nel * esz + 63) // 64 * 64
        assert self.off + nb <= self.nbytes, f"arena overflow {self.off}+{nb}>{self.nbytes}"
        o = self.off
        self.off += nb
        self.peak = max(self.peak, self.off)
        ap = self.t[0:parts, o // 2:(o + nel * esz) // 2]
        if dtype != BF16:
            ap = ap.bitcast(dtype)
        if len(shape_free) > 1:
            names = " ".join(f"d{i}" for i in range(len(shape_free)))
            kw = {f"d{i}": int(s) for i, s in enumerate(shape_free)}
            ap = ap.rearrange(f"p ({names}) -> p {names}", **kw)
        return ap

    def mark(self):
        self.marks.append(self.off)

    def release(self):
        self.off = self.marks.pop()
        if self.prog is not None:
            self.prog.fence()


class Ring:
    def __init__(self, items):
        self.items = items
        self.i = 0

    def next(self):
        it = self.items[self.i % len(self.items)]
        self.i += 1
        return it


L = 2
D = 1024
S = 2048
NT = 16
NIN = 8864
DFF = 2816
C_AB, C_AC, C_AX = 0, 512, 1024
C_RQ, C_RK, C_RV, C_RG = 1536, 1792, 2048, 2560
C_SU, C_SV = 3072, 3584
C_MQ, C_MKV, C_MPE = 4096, 4480, 4736
C_GATE = 4768
EPS = 1e-6
R_NMIX, R_NFFN, R_QN, R_KVN, R_LNG, R_LNB, R_BS = 0, 1024, 2048, 2432, 2688, 3200, 3712
R_NNEXT = 4224
NROW = 5248
K_ID, K_NEG, K_M01, K_DT, K_ZS, K_XI, K_IFR, K_IFM = 0, 128, 256, 384, 896, 900, 1156, 1188
NCONST = 1204
NPV = 44
MAGIC = 12582912.0
TWO_PI = 2.0 * math.pi
C1 = float(np.float32(6.28125))
C2 = float(np.float32(TWO_PI - 6.28125))
PI_LO = 3.1415925


def host_consts():
    c = np.zeros((128, NCONST), np.float64)
    idx = np.arange(128)
    c[:, K_ID:K_ID + 128] = np.eye(128)
    kk, qq = np.meshgrid(idx, idx, indexing="ij")
    c[:, K_NEG:K_NEG + 128] = np.where(kk <= qq, 0.0, -30000.0)
    c[:, K_M01:K_M01 + 128] = np.where(kk <= qq, 1.0, 0.0)
    lg = np.log1p(-np.exp2(-5.0 - np.arange(4, dtype=np.float64)))
    scale = 64 ** -0.5
    for h in range(4):
        diff = (qq - kk).astype(np.float64)
        dtm = np.where(diff >= 0, np.exp(np.maximum(diff, 0) * lg[h]), 0.0) * scale
        c[:, K_DT + h * 128:K_DT + (h + 1) * 128] = dtm
        c[:, K_ZS + h] = np.exp((127 - idx) * lg[h]) * scale
    for cc in range(2):
        for p in range(128):
            h = 2 * cc + p // 64
            c[p, K_XI + cc * 128:K_XI + (cc + 1) * 128] = np.exp((idx + 1.0) * lg[h])
    ifr = (np.float32(10000.0) ** (-(np.arange(0, 64, 2, dtype=np.float32) / np.float32(64)))).astype(np.float32)
    ifm = (np.float32(10000.0) ** (-(np.arange(0, 32, 2, dtype=np.float32) / np.float32(32)))).astype(np.float32)
    c[:, K_IFR:K_IFR + 32] = ifr[None, :]
    c[:, K_IFM:K_IFM + 16] = ifm[None, :]
    cd = [float(np.exp(128 * lg[h])) for h in range(4)]
    return c.astype(np.float32), cd


class _Stop(Exception):
    pass


def build(nseq=2, nlayers=L, debug=False, upto=99):
    nc = bass.Bass("TRN2", target_bir_lowering=False)
    din = lambda name, shape, d=F32: nc.dram_tensor(name, list(shape), d, kind="ExternalInput").ap()
    x_in = din("x", [nseq, S, D])
    pos_in = din("pos", [nseq, 128, NT], I32)
    w_in = din("w_in", [L, D, NIN])
    w_br = din("w_branch", [L, 4, 512, D])
    w_o = din("w_out", [L, D, D])
    w_f1 = din("w_ffn_in", [L, D, 2 * DFF])
    w_f2 = din("w_ffn_out", [L, DFF, D])
    w_uq = din("w_uq", [L, 384, 768])
    w_ukv = din("w_ukv", [L, 256, 1024])
    wsT_in = din("wsT", [L, 128, 4, 128])
    rowv = din("rowvec", [L, NROW])
    fnorm = din("fnorm", [1, D])
    pv_in = din("pvec", [L, 128, NPV])
    cst_in = din("consts", [128, NCONST])
    out = nc.dram_tensor("out", [nseq, S, D], F32, kind="ExternalOutput").ap()
    dbg = {}
    if debug:
        for nm, shp in (("d_hT", [128, 8, S]), ("d_ym", [128, 4, S]), ("d_ya", [128, 4, S]), ("d_yr", [128, 4, S]),
                        ("d_ys", [128, 4, S]), ("d_mg", [128, 8, S])):
            dbg[nm] = nc.dram_tensor(nm, shp, BF16, kind="ExternalOutput").ap()
        dbg["d_x1"] = nc.dram_tensor("d_x1", [S, D], F32, kind="ExternalOutput").ap()
        dbg["d_x2"] = nc.dram_tensor("d_x2", [S, D], F32, kind="ExternalOutput").ap()
        dbg["d_cos"] = nc.dram_tensor("d_cos", [128, NT, 32], F32, kind="ExternalOutput").ap()
        dbg["d_sin"] = nc.dram_tensor("d_sin", [128, NT, 32], F32, kind="ExternalOutput").ap()

    _, CD = host_consts()
    NB = 207 * 1024
    stack = contextlib.ExitStack()
    with stack:
        at = stack.enter_context(nc.sbuf_tensor("arena", [128, NB // 2], BF16))
        pst = stack.enter_context(nc.psum_tensor("ps", [128, 8, 512], F32))
        P = Prog(nc)
        A = Arena(at, NB, P)
        PB = [Buf(f"bank{i}", excl=True) for i in range(8)]
        ps = [pst[:, i, :] for i in range(8)]
        psb = [pst[:, i, :].bitcast(BF16) for i in range(8)]

        def MM(out_ap, lhsT, rhs, start, stop, r, w):
            P.pe(lambda e: e.matmul(out_ap, lhsT=lhsT, rhs=rhs, start=start, stop=stop), r, w)

        def TR(out_ap, in_ap, ident, r, w):
            P.pe(lambda e: e.transpose(out=out_ap, in_=in_ap, identity=ident), r, w)

        cst = A.alloc(NCONST, F32); b_cst = Buf("cst")
        idb = A.alloc(128, BF16); negb = A.alloc(128, BF16); b_cb = Buf("cb")
        rowb = A.alloc(NROW, F32); b_rowb = Buf("rowb")
        fnb = A.alloc(D, F32); b_fnb = Buf("fnb")
        pvt = A.alloc(NPV, F32); b_pv = Buf("pv")
        cos_r = A.alloc((NT, 32), F32); sin_r = A.alloc((NT, 32), F32)
        cos_m = A.alloc((NT, 16), F32); sin_m = A.alloc((NT, 16), F32)
        b_rope = Buf("rope")
        hT = A.alloc((8, S), BF16); b_hT = bufs(NT, "hT")
        yT = A.alloc((4, S), BF16); b_y = bufs(NT, "y")
        b_mg = bufs(NT, "mg")
        MG = {}
        ss = A.alloc(NT, F32); sd = A.alloc(NT, F32); rs = A.alloc(NT, F32)
        b_ss = bufs(NT, "ss"); b_sd = bufs(NT, "sd"); b_rs = bufs(NT, "rs")
        junk = A.alloc(D, BF16); b_junk = Buf("junk")
        xt_ring = Ring([(A.alloc(D, F32), Buf(f"xt{i}")) for i in range(3)])
        hb_ring = Ring([(A.alloc(D, BF16), Buf(f"hb{i}")) for i in range(3)])
        ident = idb
        b_xd = bufs(NT, "xd")

        P.dma("sp", cst, cst_in, w=[b_cst])
        P.dma("sp", fnb, fnorm[0:1, :].partition_broadcast(128).squeeze(1), w=[b_fnb])
        P.dve(lambda e: e.tensor_copy(out=idb, in_=cst[:, K_ID:K_ID + 128]), r=[b_cst], w=[b_cb])
        P.dve(lambda e: e.tensor_copy(out=negb, in_=cst[:, K_NEG:K_NEG + 128]), r=[b_cst], w=[b_cb])

        tr_ring = Ring([4, 5])

        def load_w(dst, src2d, r=(), w=()):
            P.dma("pool", dst, src2d.rearrange("(k p) n -> p k n", p=128), r=r, w=w)

        def norm_tile(xt, bx, gcol, tt, dstT, b_dst, final_dst=None, defer=False):
            P.act(lambda e: e.activation(out=junk, in_=xt, func=AF.Square, accum_out=ss[:, tt:tt + 1]),
                  r=[bx], w=[b_junk, b_ss[tt]])
            P.act(lambda e: e.activation(out=sd[:, tt:tt + 1], in_=ss[:, tt:tt + 1], func=AF.Sqrt, scale=1.0 / D, bias=EPS),
                  r=[b_ss[tt]], w=[b_sd[tt]])
            P.dve(lambda e: e.reciprocal(out=rs[:, tt:tt + 1], in_=sd[:, tt:tt + 1]), r=[b_sd[tt]], w=[b_rs[tt]])
            if final_dst is not None:
                ot, bo = xt_ring.next()
                P.dve(lambda e: e.scalar_tensor_tensor(out=ot, in0=xt, scalar=rs[:, tt:tt + 1], in1=fnb, op0=ALU.mult, op1=ALU.mult),
                      r=[bx, b_rs[tt], b_fnb], w=[bo])
                P.dma("pool", final_dst, ot, r=[bo], w=[b_xd[tt]])
                return
            hb, bh = hb_ring.next()
            P.dve(lambda e: e.scalar_tensor_tensor(out=hb, in0=xt, scalar=rs[:, tt:tt + 1], in1=rowb[:, gcol:gcol + D],
                                                   op0=ALU.mult, op1=ALU.mult),
                  r=[bx, b_rs[tt], b_rowb], w=[bh])

            def part2():
                bk = tr_ring.next()
                for kc in range(8):
                    TR(psb[bk][:, kc * 128:(kc + 1) * 128], hb[:, kc * 128:(kc + 1) * 128], ident, r=[bh, b_cb], w=[PB[bk]])
                P.act(lambda e: e.copy(out=dstT[:, :, tt * 128:(tt + 1) * 128], in_=psb[bk].rearrange("p (k n) -> p k n", k=8)),
                      r=[PB[bk]], w=[b_dst[tt]])
            if defer:
                return part2
            part2()

        def zero_ss():
            P.dve(lambda e: e.memset(ss, 0.0), w=b_ss)

        def finalize(l, bi, yT, b_y, first, prefetch=None):
            mgT = MG["t"]
            A.mark()
            wb = A.alloc((4, D), BF16); b_wb = Buf("wb")
            wg = A.alloc((8, D), BF16); b_wg = Buf("wg")
            sg_ring = Ring([(A.alloc(512, F32), Buf(f"sg{i}")) for i in range(2)])
            tp_ring = Ring([(A.alloc(512, F32), Buf(f"tp{i}")) for i in range(2)])
            b_wbp = bufs(2, "wbp"); b_wgp = bufs(4, "wgp")
            for pz in range(2):
                load_w(wb[:, :, pz * 512:(pz + 1) * 512], w_br[l, bi][:, pz * 512:(pz + 1) * 512], w=[b_wbp[pz]])
                for pq in range(2):
                    g0 = (pz * 2 + pq) * 256
                    load_w(wg[:, :, g0:g0 + 256], w_in[l, :, C_GATE + bi * D + g0:C_GATE + bi * D + g0 + 256], w=[b_wgp[pz * 2 + pq]])
            if prefetch is not None:
                prefetch()
            zr = Ring([0, 1]); gr = Ring([2, 3])
            for oc in range(8):
                for tg in range(4):
                    tsl = slice(tg * 512, (tg + 1) * 512)
                    tb = [4 * tg + i for i in range(4)]
                    zb = zr.next(); gb = gr.next()
                    for c in range(4):
                        MM(ps[zb], wb[:, c, oc * 128:(oc + 1) * 128], yT[:, c, tsl], c == 0, c == 3,
                           r=[b_wbp[oc // 4]] + [b_y[t] for t in tb], w=[PB[zb]])
                    for kc in range(8):
                        MM(ps[gb], wg[:, kc, oc * 128:(oc + 1) * 128], hT[:, kc, tsl], kc == 0, kc == 7,
                           r=[b_wgp[oc // 2]] + [b_hT[t] for t in tb], w=[PB[gb]])
                    sg, bsg = sg_ring.next()
                    P.act(lambda e, sg=sg, gb=gb, oc=oc: e.activation(out=sg, in_=ps[gb], func=AF.Sigmoid,
                                                                     bias=pvt[:, bi * 8 + oc:bi * 8 + oc + 1]),
                          r=[PB[gb], b_pv], w=[bsg])
                    if first:
                        P.dve(lambda e, sg=sg, zb=zb, oc=oc, tsl=tsl: e.tensor_tensor(out=mgT[:, oc, tsl], in0=ps[zb], in1=sg, op=ALU.mult),
                              r=[PB[zb], bsg], w=[b_mg[t] for t in tb])
                    else:
                        tp, btp = tp_ring.next()
                        P.dve(lambda e, sg=sg, zb=zb, tp=tp: e.tensor_tensor(out=tp, in0=ps[zb], in1=sg, op=ALU.mult),
                              r=[PB[zb], bsg], w=[btp])
                        P.pool(lambda e, tp=tp, oc=oc, tsl=tsl: e.tensor_tensor(out=mgT[:, oc, tsl], in0=mgT[:, oc, tsl], in1=tp, op=ALU.add),
                               r=[btp] + [b_mg[t] for t in tb], w=[b_mg[t] for t in tb])
            A.release()

        def dump(name, ap, rb):
            if debug and name in dbg:
                P.dma("sp", dbg[name], ap, r=rb)

        def rope_tables(s):
            A.mark()
            posi = A.alloc(NT, I32); b_pi = Buf("posi")
            posf = A.alloc(NT, F32); b_pf = Buf("posf")
            P.dma("sp", posi, pos_in[s], w=[b_pi])
            P.dve(lambda e: e.tensor_copy(out=posf, in_=posi), r=[b_pi], w=[b_pf])
            for (nf, koff, ctab, stab) in ((32, K_IFR, cos_r, sin_r), (16, K_IFM, cos_m, sin_m)):
                ang = A.alloc((NT, nf), F32); t1 = A.alloc((NT, nf), F32); t2 = A.alloc((NT, nf), F32)
                b_a = Buf("ang"); b_t1 = Buf("t1"); b_t2 = Buf("t2")
                P.dve(lambda e, ang=ang, nf=nf, koff=koff: e.tensor_tensor(
                    out=ang, in0=posf.unsqueeze(2).broadcast_to([128, NT, nf]),
                    in1=cst[:, koff:koff + nf].unsqueeze(1).broadcast_to([128, NT, nf]), op=ALU.mult),
                    r=[b_pf, b_cst], w=[b_a])
                P.dve(lambda e, ang=ang, t1=t1: e.tensor_scalar(out=t1, in0=ang, scalar1=1.0 / TWO_PI, scalar2=MAGIC, op0=ALU.mult, op1=ALU.add),
                      r=[b_a], w=[b_t1])
                P.dve(lambda e, t1=t1, t2=t2: e.tensor_scalar(out=t2, in0=t1, scalar1=-MAGIC, scalar2=None, op0=ALU.add),
                      r=[b_t1], w=[b_t2])
                P.dve(lambda e, t1=t1, t2=t2, ang=ang: e.scalar_tensor_tensor(out=t1, in0=t2, scalar=-C1, in1=ang, op0=ALU.mult, op1=ALU.add),
                      r=[b_t2, b_a], w=[b_t1])
                P.dve(lambda e, t1=t1, t2=t2, ang=ang: e.scalar_tensor_tensor(out=ang, in0=t2, scalar=-C2, in1=t1, op0=ALU.mult, op1=ALU.add),
                      r=[b_t2, b_t1], w=[b_a])
                P.dve(lambda e, ang=ang: e.tensor_scalar(out=ang, in0=ang, scalar1=-PI_LO, scalar2=PI_LO, op0=ALU.max, op1=ALU.min),
                      r=[b_a], w=[b_a])
                P.act(lambda e, ang=ang, stab=stab: e.activation(out=stab, in_=ang, func=AF.Sin), r=[b_a], w=[b_rope])
                P.act(lambda e, ang=ang, t1=t1: e.activation(out=t1, in_=ang, func=AF.Abs), r=[b_a], w=[b_t1])
                P.act(lambda e, t1=t1, ctab=ctab: e.activation(out=ctab, in_=t1, func=AF.Sin, scale=-1.0, bias=math.pi / 2),
                      r=[b_t1], w=[b_rope])
            A.release()
            if debug and s == 0:
                dump("d_cos", cos_r, [b_rope]); dump("d_sin", sin_r, [b_rope])

        def rope_apply(dst1, dst2, x1, x2, ct, st, shape, tmp, r, w):
            ta, tb_ = tmp
            b_ta = Buf("ta"); b_tb = Buf("tb")
            P.dve(lambda e: e.tensor_tensor(out=ta, in0=x1, in1=ct, op=ALU.mult), r=r, w=[b_ta])
            P.dve(lambda e: e.tensor_tensor(out=tb_, in0=x2, in1=st, op=ALU.mult), r=r, w=[b_tb])
            P.dve(lambda e: e.tensor_tensor(out=dst1, in0=ta, in1=tb_, op=ALU.subtract), r=[b_ta, b_tb], w=w)
            P.dve(lambda e: e.tensor_tensor(out=ta, in0=x1, in1=st, op=ALU.mult), r=r, w=[b_ta])
            P.dve(lambda e: e.tensor_tensor(out=tb_, in0=x2, in1=ct, op=ALU.mult), r=r, w=[b_tb])
            P.dve(lambda e: e.tensor_tensor(out=dst2, in0=ta, in1=tb_, op=ALU.add), r=[b_ta, b_tb], w=w)

        def ck(n):
            if upto < n:
                raise _Stop()

        def body():
          for s in range(nseq):
            rope_tables(s)
            ck(0)
            for l in range(nlayers):
                dbg_on = debug and s == 0 and l == 0
                P.dma("sp", rowb, rowv[l:l + 1, :].partition_broadcast(128).squeeze(1), w=[b_rowb])
                P.dma("sp", pvt, pv_in[l], w=[b_pv])
                def m0_tile(tt, defer=False):
                    xt, bx = xt_ring.next()
                    P.dma("sp", xt, x_in[s, tt * 128:(tt + 1) * 128, :], w=[bx])
                    return norm_tile(xt, bx, R_NMIX, tt, hT, b_hT, defer=defer)
                if l == 0:
                    zero_ss()

                ck(1)
                A.mark()
                qnT = A.alloc((3, S), BF16); b_qnT = bufs(NT, "qnT")
                KT = A.alloc((8, S), BF16); b_KT = bufs(NT, "KT")
                VP = A.alloc((NT, 8, 65), BF16); b_VP = bufs(NT, "VP")
                ymT = yT; b_ym = b_y
                A.mark()
                wm = A.alloc((8, 672), BF16); b_wm = Buf("wm")
                wkv = A.alloc((2, 1024), BF16); b_wkv = Buf("wkv")
                kvnT = A.alloc((2, S), BF16); b_kvnT = bufs(NT, "kvnT")
                qn_ring = Ring([(A.alloc(384, BF16), Buf(f"qn{i}")) for i in range(2)])
                kvn_ring = Ring([(A.alloc(256, BF16), Buf(f"kvn{i}")) for i in range(2)])
                kpe_ring = Ring([(A.alloc(96, BF16), Buf(f"kpe{i}")) for i in range(2)])
                st8 = A.alloc((NT, 8), F32); b_st8 = bufs(NT, "st8")
                rtmp = (A.alloc(16, F32), A.alloc(16, F32))
                load_w(wm, w_in[l, :, C_MQ:C_MQ + 672], w=[b_wm])
                load_w(wkv, w_ukv[l], w=[b_wkv])
                for (kp, bkp) in kpe_ring.items:
                    P.dve(lambda e, kp=kp: e.memset(kp, 0.0), w=[bkp])
                P.dve(lambda e: e.memset(st8, 0.0), w=b_st8)
                P.pool(lambda e: e.memset(VP, 1.0), w=b_VP)
                pr_ring = Ring([(0, 1), (2, 3)])
                MPB = {}

                def mp_A(tt):
                    tsl = slice(tt * 128, (tt + 1) * 128)
                    ba, bb = pr_ring.next()
                    for kc in range(8):
                        MM(ps[ba], hT[:, kc, tsl], wm[:, kc, 0:512], kc == 0, kc == 7, r=[b_hT[tt], b_wm], w=[PB[ba]])
                    for kc in range(8):
                        MM(ps[bb][:, 0:160], hT[:, kc, tsl], wm[:, kc, 512:672], kc == 0, kc == 7, r=[b_hT[tt], b_wm], w=[PB[bb]])
                    c0 = st8[:, tt, 0:1]; c1 = st8[:, tt, 1:2]; c2 = st8[:, tt, 2:3]
                    P.act(lambda e, ba=ba, c0=c0: e.activation(out=junk[:, 0:384], in_=ps[ba][:, 0:384], func=AF.Square, accum_out=c0),
                          r=[PB[ba]], w=[b_junk, b_st8[tt]])
                    P.act(lambda e, ba=ba, c1=c1: e.activation(out=junk[:, 0:128], in_=ps[ba][:, 384:512], func=AF.Square, accum_out=c1),
                          r=[PB[ba]], w=[b_junk, b_st8[tt]])
                    P.act(lambda e, bb=bb, c2=c2: e.activation(out=junk[:, 0:128], in_=ps[bb][:, 0:128], func=AF.Square, accum_out=c2),
                          r=[PB[bb]], w=[b_junk, b_st8[tt]])
                    MPB[tt] = (ba, bb)

                def mp_D(tt):
                    P.dve(lambda e, tt=tt: e.tensor_tensor(out=st8[:, tt, 3:4], in0=st8[:, tt, 1:2], in1=st8[:, tt, 2:3], op=ALU.add),
                          r=[b_st8[tt]], w=[b_st8[tt]])

                def mp_B(tt):
                    ba, bb = MPB.pop(tt)
                    P.act(lambda e, tt=tt: e.activation(out=st8[:, tt, 4:5], in_=st8[:, tt, 0:1], func=AF.Sqrt, scale=1.0 / 384, bias=EPS),
                          r=[b_st8[tt]], w=[b_st8[tt]])
                    P.act(lambda e, tt=tt: e.activation(out=st8[:, tt, 5:6], in_=st8[:, tt, 3:4], func=AF.Sqrt, scale=1.0 / 256, bias=EPS),
                          r=[b_st8[tt]], w=[b_st8[tt]])
                    P.dve(lambda e, tt=tt: e.reciprocal(out=st8[:, tt, 6:8], in_=st8[:, tt, 4:6]), r=[b_st8[tt]], w=[b_st8[tt]])
                    qn, bqn = qn_ring.next(); kvn, bkvn = kvn_ring.next(); kp, bkp = kpe_ring.next()
                    P.dve(lambda e, qn=qn, ba=ba, tt=tt: e.scalar_tensor_tensor(
                        out=qn, in0=ps[ba][:, 0:384], scalar=st8[:, tt, 6:7], in1=rowb[:, R_QN:R_QN + 384], op0=ALU.mult, op1=ALU.mult),
                        r=[PB[ba], b_st8[tt], b_rowb], w=[bqn])
                    P.dve(lambda e, kvn=kvn, ba=ba, tt=tt: e.scalar_tensor_tensor(
                        out=kvn[:, 0:128], in0=ps[ba][:, 384:512], scalar=st8[:, tt, 7:8], in1=rowb[:, R_KVN:R_KVN + 128], op0=ALU.mult, op1=ALU.mult),
                        r=[PB[ba], b_st8[tt], b_rowb], w=[bkvn])
                    P.dve(lambda e, kvn=kvn, bb=bb, tt=tt: e.scalar_tensor_tensor(
                        out=kvn[:, 128:256], in0=ps[bb][:, 0:128], scalar=st8[:, tt, 7:8], in1=rowb[:, R_KVN + 128:R_KVN + 256], op0=ALU.mult, op1=ALU.mult),
                        r=[PB[bb], b_st8[tt], b_rowb], w=[bkvn])
                    rope_apply(kp[:, 64:80], kp[:, 80:96], ps[bb][:, 128:144], ps[bb][:, 144:160],
                               cos_m[:, tt, :], sin_m[:, tt, :], None, rtmp, r=[PB[bb], b_rope], w=[bkp])
                    return qn, bqn, kvn, bkvn, kp, bkp

                def mp_P2(tt, qn, bqn, kvn, bkvn, kp, bkp):
                    tsl = slice(tt * 128, (tt + 1) * 128)
                    bk = tr_ring.next()
                    for c in range(3):
                        TR(psb[bk][:, c * 128:(c + 1) * 128], qn[:, c * 128:(c + 1) * 128], ident, r=[bqn, b_cb], w=[PB[bk]])
                    for c in range(2):
                        TR(psb[bk][:, (3 + c) * 128:(4 + c) * 128], kvn[:, c * 128:(c + 1) * 128], ident, r=[bkvn, b_cb], w=[PB[bk]])
                    TR(psb[bk][0:96, 640:768], kp, ident, r=[bkp, b_cb], w=[PB[bk]])
                    P.act(lambda e, bk=bk, tsl=tsl: e.copy(out=qnT[:, :, tsl], in_=psb[bk][:, 0:384].rearrange("p (k n) -> p k n", k=3)),
                          r=[PB[bk]], w=[b_qnT[tt]])
                    P.act(lambda e, bk=bk, tsl=tsl: e.copy(out=kvnT[:, :, tsl], in_=psb[bk][:, 384:640].rearrange("p (k n) -> p k n", k=2)),
                          r=[PB[bk]], w=[b_kvnT[tt]])
                    P.dve(lambda e, bk=bk, tsl=tsl: e.tensor_copy(out=KT[64:96, :, tsl],
                                                                  in_=psb[bk][64:96, 640:768].unsqueeze(1).broadcast_to([32, 8, 128])),
                          r=[PB[bk]], w=[b_KT[tt]])
                    vb = 6 + (tt % 2)
                    for kc in range(2):
                        MM(ps[vb], kvnT[:, kc, tsl], wkv[:, kc, 512:1024], kc == 0, kc == 1, r=[b_kvnT[tt], b_wkv], w=[PB[vb]])
                    P.act(lambda e, vb=vb, tt=tt: e.copy(out=VP[:, tt, :, 0:64], in_=ps[vb].rearrange("p (h d) -> p h d", h=8)),
                          r=[PB[vb]], w=[b_VP[tt]])

                m0p = None
                if l == 0:
                    m0_tile(0)
                    m0_tile(1)
                    m0p = m0_tile(2, defer=True)
                mres = {}
                for k in range(NT + 2):
                    m0n = None
                    if l == 0 and k + 3 < NT:
                        m0n = m0_tile(k + 3, defer=True)
                    if k < NT:
                        mp_A(k)
                    if m0p is not None:
                        m0p()
                    m0p = m0n
                    if 0 <= k - 1 < NT:
                        mres[k - 1] = mp_B(k - 1)
                    if 0 <= k - 2 < NT:
                        mp_P2(k - 2, *mres.pop(k - 2))
                    if k < NT:
                        mp_D(k)
                if dbg_on:
                    dump("d_hT", hT, b_hT)
                kr = Ring([0, 1, 2, 3])
                for h in range(8):
                    for tg in range(4):
                        tsl = slice(tg * 512, (tg + 1) * 512)
                        tb = [4 * tg + i for i in range(4)]
                        kb = kr.next()
                        for kc in range(2):
                            MM(ps[kb][0:64, :], wkv[:, kc, h * 64:(h + 1) * 64], kvnT[:, kc, tsl], kc == 0, kc == 1,
                               r=[b_wkv] + [b_kvnT[t] for t in tb], w=[PB[kb]])
                        if (h * 4 + tg) % 2 == 0:
                            P.act(lambda e, kb=kb, h=h, tsl=tsl: e.copy(out=KT[0:64, h, tsl], in_=ps[kb][0:64, :]),
                                  r=[PB[kb]], w=[b_KT[t] for t in tb])
                        else:
                            P.dve(lambda e, kb=kb, h=h, tsl=tsl: e.tensor_copy(out=KT[0:64, h, tsl], in_=ps[kb][0:64, :]),
                                  r=[PB[kb]], w=[b_KT[t] for t in tb])
                A.release()
                ck(2)
                A.mark()
                wq = A.alloc((3, 768), BF16); b_wq = Buf("wq")
                load_w(wq, w_uq[l], w=[b_wq])
                qf_ring = Ring([(A.alloc((8, 96), BF16), Buf(f"qf{i}")) for i in range(3)])
                qi_ring = Ring([(A.alloc((8, 128), BF16), Buf(f"qi{i}")) for i in range(4)])
                p_ring = Ring([(A.alloc(512, BF16), Buf(f"p{i}")) for i in range(4)])
                yt_ring = Ring([(A.alloc(512, BF16), Buf(f"yt{i}")) for i in range(2)])
                rc_ring = Ring([(A.alloc(8, F32), Buf(f"rc{i}")) for i in range(2)])
                rtq = (A.alloc((8, 16), F32), A.alloc((8, 16), F32))
                qpe = A.alloc((8, 32), F32); bqpe = Buf("qpe")
                sc_ring = Ring([3, 4, 5])
                ASCALE = 96 ** -0.5
                def q_stage(i):
                    tsl = slice(i * 128, (i + 1) * 128)
                    qa, qb = (0, 1) if i % 2 == 0 else (6, 7)
                    for kc in range(3):
                        MM(ps[qa], qnT[:, kc, tsl], wq[:, kc, 0:512], kc == 0, kc == 2, r=[b_qnT[i], b_wq], w=[PB[qa]])
                    for kc in range(3):
                        MM(ps[qb][:, 0:256], qnT[:, kc, tsl], wq[:, kc, 512:768], kc == 0, kc == 2, r=[b_qnT[i], b_wq], w=[PB[qb]])
                    qf, bqf = qf_ring.next()
                    qff = qf.rearrange("p h d -> p (h d)")
                    P.dve(lambda e: e.tensor_copy(out=qff[:, 0:512], in_=ps[qa]), r=[PB[qa]], w=[bqf])
                    P.dve(lambda e: e.tensor_copy(out=qff[:, 512:768], in_=ps[qb][:, 0:256]), r=[PB[qb]], w=[bqf])
                    cb = cos_m[:, i, :].unsqueeze(1).broadcast_to([128, 8, 16])
                    sb_ = sin_m[:, i, :].unsqueeze(1).broadcast_to([128, 8, 16])
                    P.dve(lambda e: e.tensor_copy(out=qpe, in_=qf[:, :, 64:96]), r=[bqf], w=[bqpe])
                    rope_apply(qf[:, :, 64:80], qf[:, :, 80:96], qpe[:, :, 0:16], qpe[:, :, 16:32], cb, sb_, None, rtq,
                               r=[bqpe, b_rope], w=[bqf])
                    for h in range(8):
                        TR(psb[2][0:96, h * 128:(h + 1) * 128], qf[:, h, :], ident, r=[bqf, b_cb], w=[PB[2]])
                    qi, bqi = qi_ring.next()
                    P.dve(lambda e: e.tensor_copy(out=qi[0:96], in_=psb[2][0:96, :].rearrange("p (h n) -> p h n", h=8)),
                          r=[PB[2]], w=[bqi])
                    return qi, bqi

                def qk_stage(i, qi, bqi, h, js):
                    sb2 = sc_ring.next()
                    for jl, j in enumerate(js):
                        diag = (j == i)
                        MM(ps[sb2][:, jl * 128:(jl + 1) * 128], KT[0:96, h, j * 128:(j + 1) * 128], qi[0:96, h, :],
                           True, not diag, r=[b_KT[j], bqi], w=[PB[sb2]])
                        if diag:
                            MM(ps[sb2][:, jl * 128:(jl + 1) * 128], ident, negb, False, True, r=[b_cb], w=[PB[sb2]])
                    pt, bpt = p_ring.next()
                    nj = len(js)
                    P.act(lambda e: e.activation(out=pt[:, 0:nj * 128], in_=ps[sb2][:, 0:nj * 128], func=AF.Exp, scale=ASCALE),
                          r=[PB[sb2]], w=[bpt])
                    return pt, bpt

                def pv_stage(i, h, js, pt, bpt):
                    ob = (6 if i % 2 == 0 else 0) + h // 4
                    ocol = (h % 4) * 65
                    for jl, j in enumerate(js):
                        MM(ps[ob][:, ocol:ocol + 65], pt[:, jl * 128:(jl + 1) * 128], VP[:, j, h, :], j == 0, j == i,
                           r=[bpt, b_VP[j]], w=[PB[ob]])

                def epilogue(i):
                    tsl = slice(i * 128, (i + 1) * 128)
                    rc, brc = rc_ring.next()
                    yt, byt = yt_ring.next()
                    for hb2 in range(2):
                        ob = (6 if i % 2 == 0 else 0) + hb2
                        pv4 = ps[ob][:, 0:260].rearrange("p (h d) -> p h d", h=4)
                        P.dve(lambda e, pv4=pv4, hb2=hb2: e.reciprocal(out=rc[:, hb2 * 4:(hb2 + 1) * 4], in_=pv4[:, :, 64]),
                              r=[PB[ob]], w=[brc])
                        P.dve(lambda e, pv4=pv4, hb2=hb2: e.tensor_tensor(
                            out=yt[:, hb2 * 256:(hb2 + 1) * 256].rearrange("p (h d) -> p h d", h=4), in0=pv4[:, :, 0:64],
                            in1=rc[:, hb2 * 4:(hb2 + 1) * 4].unsqueeze(2).broadcast_to([128, 4, 64]), op=ALU.mult),
                            r=[PB[ob], brc], w=[byt])
                    for c in range(4):
                        TR(psb[2][:, c * 128:(c + 1) * 128], yt[:, c * 128:(c + 1) * 128], ident, r=[byt, b_cb], w=[PB[2]])
                    P.dve(lambda e: e.tensor_copy(out=ymT[:, :, tsl], in_=psb[2][:, 0:512].rearrange("p (k n) -> p k n", k=4)),
                          r=[PB[2]], w=[b_ym[i]])

                LOOK = 2
                qs = {0: q_stage(0), 1: q_stage(1)}
                for i in range(NT):
                    qi, bqi = qs.pop(i)
                    chunks = [(h, list(range(j0, min(j0 + 4, i + 1)))) for h in range(8) for j0 in range(0, i + 1, 4)]
                    pend = []
                    for (h, js) in chunks:
                        pt, bpt = qk_stage(i, qi, bqi, h, js)
                        pend.append((h, js, pt, bpt))
                        if len(pend) > LOOK:
                            pv_stage(i, *pend.pop(0))
                    if i + 2 < NT:
                        qs[i + 2] = q_stage(i + 2)
                    while pend:
                        pv_stage(i, *pend.pop(0))
                    epilogue(i)
                A.release()
                if dbg_on:
                    dump("d_ym", ymT, b_ym)
                A.release()
                A.mark()
                mgT = A.alloc((8, S), BF16)
                MG["t"] = mgT
                ck(3)
                A.mark()
                wa = A.alloc((8, 1536), BF16); b_wa = Buf("wa")
                finalize(l, 3, ymT, b_ym, first=True, prefetch=lambda: load_w(wa, w_in[l, :, 0:1536], w=[b_wa]))

                ck(4)
                A.mark()
                yaT = yT; b_ya = b_y
                pb_ring = Ring([(A.alloc(S + 2, F32), Buf(f"pbuf{i}")) for i in range(2)])
                ab_ring = Ring([(A.alloc(S, F32), Buf(f"abuf{i}")) for i in range(2)])
                cacc = A.alloc(S, F32); b_cacc = Buf("cacc")
                axr = Ring([(A.alloc(512, F32), Buf(f"ax{i}")) for i in range(2)])
                for (pb_, bpb_) in pb_ring.items:
                    P.dve(lambda e, pb_=pb_: e.memset(pb_[:, 0:2], 0.0), w=[bpb_])
                cr = Ring([(0, 1, 2), (3, 4, 5)])
                for c in range(4):
                    pbuf, b_pbuf = pb_ring.next()
                    abuf, b_abuf = ab_ring.next()
                    for tg in range(4):
                        tsl = slice(tg * 512, (tg + 1) * 512)
                        tb = [4 * tg + i for i in range(4)]
                        b0, b1, b2 = cr.next()
                        for (bk, coff) in ((b0, C_AB), (b1, C_AC), (b2, C_AX)):
                            for kc in range(8):
                                MM(ps[bk], wa[:, kc, coff + c * 128:coff + (c + 1) * 128], hT[:, kc, tsl], kc == 0, kc == 7,
                                   r=[b_wa] + [b_hT[t] for t in tb], w=[PB[bk]])
                        ax, bax = axr.next()
                        P.act(lambda e, ax=ax, b2=b2: e.copy(out=ax, in_=ps[b2]), r=[PB[b2]], w=[bax])
                        P.act(lambda e, b0=b0, tsl=tsl, abuf=abuf: e.copy(out=abuf[:, tsl], in_=ps[b0]), r=[PB[b0]], w=[b_abuf])
                        P.dve(lambda e, ax=ax, b1=b1, tg=tg, pbuf=pbuf: e.tensor_tensor(out=pbuf[:, 2 + tg * 512:2 + (tg + 1) * 512], in0=ps[b1], in1=ax, op=ALU.mult),
                              r=[PB[b1], bax], w=[b_pbuf])
                    w0 = pvt[:, 32 + c * 3 + 0:32 + c * 3 + 1]; w1 = pvt[:, 32 + c * 3 + 1:32 + c * 3 + 2]; w2 = pvt[:, 32 + c * 3 + 2:32 + c * 3 + 3]
                    P.dve(lambda e, w0=w0, pbuf=pbuf: e.tensor_scalar(out=cacc, in0=pbuf[:, 0:S], scalar1=w0, scalar2=None, op0=ALU.mult),
                          r=[b_pbuf, b_pv], w=[b_cacc])
                    P.dve(lambda e, w1=w1, pbuf=pbuf: e.scalar_tensor_tensor(out=cacc, in0=pbuf[:, 1:S + 1], scalar=w1, in1=cacc, op0=ALU.mult, op1=ALU.add),
                          r=[b_pbuf, b_pv, b_cacc], w=[b_cacc])
                    P.dve(lambda e, w2=w2, pbuf=pbuf: e.scalar_tensor_tensor(out=cacc, in0=pbuf[:, 2:S + 2], scalar=w2, in1=cacc, op0=ALU.mult, op1=ALU.add),
                          r=[b_pbuf, b_pv, b_cacc], w=[b_cacc])
                    P.pool(lambda e, c=c, abuf=abuf: e.tensor_tensor(out=yaT[:, c, :], in0=cacc, in1=abuf, op=ALU.mult),
                           r=[b_cacc, b_abuf], w=b_ya)
                if dbg_on:
                    dump("d_ya", yaT, b_ya)
                A.release()
                A.release()
                A.mark()
                wr = A.alloc((8, 1536), BF16); b_wr = Buf("wr")
                finalize(l, 0, yaT, b_ya, first=False, prefetch=lambda: load_w(wr, w_in[l, :, C_RQ:C_RQ + 1536], w=[b_wr]))
                ck(4.5)

                ck(5)
                yrT = yT; b_yr = b_y
                A.mark()
                Sst = A.alloc((2, 128), F32); b_S = Buf("S")
                Sbf = A.alloc((2, 128), BF16); b_Sbf = Buf("Sbf")
                P.dve(lambda e: e.memset(Sst, 0.0), w=[b_S])
                P.dve(lambda e: e.memset(Sbf, 0.0), w=[b_Sbf])
                qk_ring = Ring([(A.alloc(512, BF16), Buf(f"qk{i}")) for i in range(3)])
                kz_ring = Ring([(A.alloc(256, BF16), Buf(f"kz{i}")) for i in range(5)])
                vb_ring = Ring([(A.alloc(512, BF16), Buf(f"vb{i}")) for i in range(5)])
                sgr_ring = Ring([(A.alloc(512, BF16), Buf(f"sgr{i}")) for i in range(5)])
                qT_ring = Ring([(A.alloc((2, 2, 128), BF16), Buf(f"qTz{i}")) for i in range(3)])
                kT_ring = Ring([(A.alloc((2, 128), BF16), Buf(f"kT{i}")) for i in range(3)])
                qx_ring = Ring([(A.alloc((2, 2, 128), BF16), Buf(f"qx{i}")) for i in range(4)])
                for (qz, bqz) in qT_ring.items:
                    P.dve(lambda e, qz=qz: e.memset(qz, 0.0), w=[bqz])
                at_ring = Ring([(A.alloc((4, 128), BF16), Buf(f"at{i}")) for i in range(3)])
                on_ring = Ring([(A.alloc(512, F32), Buf(f"on{i}")) for i in range(2)])
                yr_ring = Ring([(A.alloc(512, BF16), Buf(f"yrt{i}")) for i in range(3)])
                bst = A.alloc((4, 6), F32); b_bst = Buf("bst")
                mv = A.alloc((4, 2), F32); b_mv = Buf("mv")
                rsd = A.alloc(12, F32); b_rsd = Buf("rsd")
                rtr = (A.alloc((8, 32), F32), A.alloc((8, 32), F32))
                RST = {}

                def ret_A1(tt):
                    tsl = slice(tt * 128, (tt + 1) * 128)
                    for (bk, coff) in ((0, 0), (1, 512), (2, 1024)):
                        for kc in range(8):
                            MM(ps[bk], hT[:, kc, tsl], wr[:, kc, coff:coff + 512], kc == 0, kc == 7, r=[b_hT[tt], b_wr], w=[PB[bk]])
                    qk, bqk = qk_ring.next()
                    src = ps[0].rearrange("p (h d) -> p h d", h=8)
                    dst = qk.rearrange("p (h d) -> p h d", h=8)
                    cb = cos_r[:, tt, :].unsqueeze(1).broadcast_to([128, 8, 32])
                    sb_ = sin_r[:, tt, :].unsqueeze(1).broadcast_to([128, 8, 32])
                    rope_apply(dst[:, :, 0:32], dst[:, :, 32:64], src[:, :, 0:32], src[:, :, 32:64], cb, sb_, None, rtr,
                               r=[PB[0], b_rope], w=[bqk])
                    kz, bkz = kz_ring.next()
                    P.pool(lambda e: e.tensor_tensor(
                        out=kz.rearrange("p (h d) -> p h d", h=4), in0=qk[:, 256:512].rearrange("p (h d) -> p h d", h=4),
                        in1=cst[:, K_ZS:K_ZS + 4].unsqueeze(2).broadcast_to([128, 4, 64]), op=ALU.mult),
                        r=[bqk, b_cst], w=[bkz])
                    vbt, bvb = vb_ring.next()
                    P.act(lambda e: e.copy(out=vbt, in_=ps[1]), r=[PB[1]], w=[bvb])
                    sgr, bsgr = sgr_ring.next()
                    P.act(lambda e: e.activation(out=sgr, in_=ps[2], func=AF.Silu), r=[PB[2]], w=[bsgr])
                    RST[tt] = dict(qk=qk, bqk=bqk, kz=kz, bkz=bkz, vbt=vbt, bvb=bvb, sgr=sgr, bsgr=bsgr)

                def ret_A3(tt):
                    d = RST[tt]
                    qk, bqk = d["qk"], d["bqk"]
                    for c in range(4):
                        TR(psb[3][:, c * 128:(c + 1) * 128], qk[:, c * 128:(c + 1) * 128], ident, r=[bqk, b_cb], w=[PB[3]])
                    qTz, bqT = qT_ring.next()
                    kT, bkT = kT_ring.next()
                    P.act(lambda e: e.copy(out=qTz[0:64, :, 0, :], in_=psb[3][0:64, 0:256].rearrange("p (c n) -> p c n", c=2)),
                          r=[PB[3]], w=[bqT])
                    P.act(lambda e: e.copy(out=qTz[64:128, :, 1, :], in_=psb[3][64:128, 0:256].rearrange("p (c n) -> p c n", c=2)),
                          r=[PB[3]], w=[bqT])
                    P.act(lambda e: e.copy(out=kT, in_=psb[3][:, 256:512].rearrange("p (c n) -> p c n", c=2)),
                          r=[PB[3]], w=[bkT])
                    qx, bqx = qx_ring.next()
                    P.pool(lambda e: e.tensor_tensor(
                        out=qx, in0=qTz,
                        in1=cst[:, K_XI:K_XI + 256].rearrange("p (c n) -> p c n", c=2).unsqueeze(2).broadcast_to([128, 2, 2, 128]), op=ALU.mult),
                        r=[bqT, b_cst], w=[bqx])
                    d.update(qTz=qTz, bqT=bqT, kT=kT, bkT=bkT, qx=qx, bqx=bqx)

                def ret_B1(tt):
                    d = RST[tt]
                    qTz, bqT, kT, bkT = d["qTz"], d["bqT"], d["kT"], d["bkT"]
                    for h in range(4):
                        c = h // 2; hf = h % 2
                        MM(ps[4][:, h * 128:(h + 1) * 128], kT[:, c, :], qTz[:, c, hf, :], True, True, r=[bkT, bqT], w=[PB[4]])
                    att, bat = at_ring.next()
                    P.dve(lambda e: e.tensor_tensor(out=att.rearrange("p h n -> p (h n)"), in0=ps[4], in1=cst[:, K_DT:K_DT + 512], op=ALU.mult),
                          r=[PB[4], b_cst], w=[bat])
                    d.update(att=att, bat=bat)

                def ret_B2(tt):
                    d = RST[tt]
                    kz, bkz, vbt, bvb, sgr, bsgr = d["kz"], d["bkz"], d["vbt"], d["bvb"], d["sgr"], d["bsgr"]
                    qx, bqx, att, bat = d["qx"], d["bqx"], d["att"], d["bat"]
                    for h in range(4):
                        c = h // 2
                        MM(ps[5][:, h * 128:(h + 1) * 128], att[:, h, :], vbt[:, h * 128:(h + 1) * 128], True, False, r=[bat, bvb], w=[PB[5]])
                        MM(ps[5][:, h * 128:(h + 1) * 128], qx[:, c, h % 2, :], Sbf[:, c, :], False, True, r=[bqx, b_Sbf], w=[PB[5]])
                    for c in range(2):
                        MM(ps[6 + c], kz[:, c * 128:(c + 1) * 128], vbt, True, True, r=[bkz, bvb], w=[PB[6 + c]])
                    for h in range(4):
                        c = h // 2; po = (h % 2) * 64
                        P.dve(lambda e, h=h, c=c, po=po: e.scalar_tensor_tensor(
                            out=Sst[po:po + 64, c, :], in0=Sst[po:po + 64, c, :], scalar=CD[h], in1=ps[6 + c][po:po + 64, h * 128:(h + 1) * 128],
                            op0=ALU.mult, op1=ALU.add), r=[b_S, PB[6 + c]], w=[b_S])
                    P.act(lambda e: e.copy(out=Sbf, in_=Sst), r=[b_S], w=[b_Sbf])
                    for h in range(4):
                        P.dve(lambda e, h=h: e.bn_stats(out=bst[:, h, :], in_=ps[5][:, h * 128:(h + 1) * 128]), r=[PB[5]], w=[b_bst])
                    for h in range(4):
                        P.dve(lambda e, h=h: e.bn_aggr(out=mv[:, h, :], in_=bst[:, h, :]), r=[b_bst], w=[b_mv])
                    P.act(lambda e: e.activation(out=rsd[:, 0:4], in_=mv[:, :, 1], func=AF.Sqrt, bias=EPS), r=[b_mv], w=[b_rsd])
                    P.dve(lambda e: e.reciprocal(out=rsd[:, 4:8], in_=rsd[:, 0:4]), r=[b_rsd], w=[b_rsd])
                    on, bon = on_ring.next()
                    P.dve(lambda e: e.scalar_tensor_tensor(out=rsd[:, 8:12], in0=mv[:, :, 0], scalar=-1.0, in1=rsd[:, 4:8], op0=ALU.mult, op1=ALU.mult),
                          r=[b_mv, b_rsd], w=[b_rsd])
                    for h in range(4):
                        P.act(lambda e, h=h: e.activation(out=on[:, h * 128:(h + 1) * 128], in_=ps[5][:, h * 128:(h + 1) * 128],
                                                          func=AF.Identity, scale=rsd[:, 4 + h:5 + h], bias=rsd[:, 8 + h:9 + h]),
                              r=[PB[5], b_rsd], w=[bon])
                    yrt, byrt = yr_ring.next()
                    P.pool(lambda e: e.tensor_tensor(out=yrt, in0=on, in1=sgr, op=ALU.mult), r=[bon, bsgr], w=[byrt])
                    d.update(yrt=yrt, byrt=byrt)

                def ret_B3(tt):
                    d = RST.pop(tt)
                    tsl = slice(tt * 128, (tt + 1) * 128)
                    yrt, byrt = d["yrt"], d["byrt"]
                    for c in range(4):
                        TR(psb[3][:, c * 128:(c + 1) * 128], yrt[:, c * 128:(c + 1) * 128], ident, r=[byrt, b_cb], w=[PB[3]])
                    P.act(lambda e: e.copy(out=yrT[:, :, tsl], in_=psb[3][:, 0:512].rearrange("p (k n) -> p k n", k=4)),
                          r=[PB[3]], w=[b_yr[tt]])

                for k in range(NT + 4):
                    if k < NT:
                        ret_A1(k)
                    if 0 <= k - 1 < NT:
                        ret_A3(k - 1)
                    if 0 <= k - 2 < NT:
                        ret_B1(k - 2)
                    if 0 <= k - 3 < NT:
                        ret_B2(k - 3)
                    if 0 <= k - 4 < NT:
                        ret_B3(k - 4)
                A.release()
                if dbg_on:
                    dump("d_yr", yrT, b_yr)
                A.release()
                A.mark()
                wsg = A.alloc((8, 1024), BF16); b_wsg = Buf("wsg")
                finalize(l, 1, yrT, b_yr, first=False, prefetch=lambda: load_w(wsg, w_in[l, :, C_SU:C_SU + 1024], w=[b_wsg]))

                ck(6)
                ysT = yT; b_ys = b_y
                A.mark()
                guT = A.alloc((4, S), BF16); b_gu = bufs(NT, "gu")
                wsf = A.alloc((4, 128), F32); b_wsf = Buf("wsf")
                wsb = A.alloc((4, 128), BF16); b_wsb = Buf("wsb")
                P.dma("sp", wsf, wsT_in[l], w=[b_wsf])
                P.dve(lambda e: e.tensor_tensor(out=wsb, in0=wsf, in1=cst[:, K_M01:K_M01 + 128].unsqueeze(1).broadcast_to([128, 4, 128]), op=ALU.mult),
                      r=[b_wsf, b_cst], w=[b_wsb])
                g1_ring = Ring([(A.alloc(512, F32), Buf(f"g1{i}")) for i in range(5)])
                g2_ring = Ring([(A.alloc(512, F32), Buf(f"g2{i}")) for i in range(3)])
                gv_ring = Ring([(A.alloc(512, F32), Buf(f"gv{i}")) for i in range(3)])
                vn_ring = Ring([(A.alloc(512, BF16), Buf(f"vn{i}")) for i in range(3)])
                st6 = A.alloc(6, F32); b_st6 = Buf("st6")
                mv2 = A.alloc(4, F32); b_mv2 = Buf("mv2")
                GC = 2.0 * math.sqrt(2.0 / math.pi)

                def gelu_from_psum(bk, dst_ap, w_dst):
                    g1, bg1 = g1_ring.next(); g2, bg2 = g2_ring.next()
                    P.act(lambda e: e.activation(out=g1, in_=ps[bk], func=AF.Square), r=[PB[bk]], w=[bg1])
                    P.dve(lambda e: e.tensor_scalar(out=g1, in0=g1, scalar1=0.044715, scalar2=1.0, op0=ALU.mult, op1=ALU.add), r=[bg1], w=[bg1])
                    P.dve(lambda e: e.tensor_tensor(out=g1, in0=g1, in1=ps[bk], op=ALU.mult), r=[bg1, PB[bk]], w=[bg1])
                    P.act(lambda e: e.activation(out=g2, in_=g1, func=AF.Sigmoid, scale=GC), r=[bg1], w=[bg2])
                    P.dve(lambda e: e.tensor_tensor(out=dst_ap, in0=g2, in1=ps[bk], op=ALU.mult), r=[bg2, PB[bk]], w=w_dst)

                def gelu_G1(bk):
                    g1, bg1 = g1_ring.next()
                    P.act(lambda e: e.activation(out=g1, in_=ps[bk], func=AF.Square, scale=math.sqrt(0.044715)), r=[PB[bk]], w=[bg1])
                    P.dve(lambda e: e.scalar_tensor_tensor(out=g1, in0=g1, scalar=1.0, in1=ps[bk], op0=ALU.add, op1=ALU.mult), r=[bg1, PB[bk]], w=[bg1])
                    return g1, bg1

                def gelu_G2(bk, g1, bg1, dst_ap, w_dst):
                    g2, bg2 = g2_ring.next()
                    P.act(lambda e: e.activation(out=g2, in_=g1, func=AF.Sigmoid, scale=GC), r=[bg1], w=[bg2])
                    P.dve(lambda e: e.tensor_tensor(out=dst_ap, in0=g2, in1=ps[bk], op=ALU.mult), r=[bg2, PB[bk]], w=w_dst)

                sr = Ring([0, 1, 6])
                gpend = None
                for g in range(4):
                    for tg in range(4):
                        tsl = slice(tg * 512, (tg + 1) * 512)
                        tb = [4 * tg + i for i in range(4)]
                        bk = sr.next()
                        for kc in range(8):
                            MM(ps[bk], wsg[:, kc, g * 128:(g + 1) * 128], hT[:, kc, tsl], kc == 0, kc == 7,
                               r=[b_wsg] + [b_hT[t] for t in tb], w=[PB[bk]])
                        g1, bg1 = gelu_G1(bk)
                        if gpend is not None:
                            gelu_G2(*gpend)
                        gpend = (bk, g1, bg1, guT[:, g, tsl], [b_gu[t] for t in tb])
                gelu_G2(*gpend)
                vr = Ring([2, 3, 7]); orr = Ring([4, 5])
                mv_ring = Ring([(A.alloc(4, F32), Buf(f"mv2_{i}")) for i in range(3)])
                st_ring = Ring([(A.alloc(6, F32), Buf(f"st6_{i}")) for i in range(3)])
                SPT = {}

                def sp_S1a(tt):
                    tsl = slice(tt * 128, (tt + 1) * 128)
                    bk = vr.next()
                    for kc in range(8):
                        MM(ps[bk], hT[:, kc, tsl], wsg[:, kc, 512:1024], kc == 0, kc == 7, r=[b_hT[tt], b_wsg], w=[PB[bk]])
                    g1, bg1 = gelu_G1(bk)
                    SPT[tt] = dict(bk=bk, g1=g1, bg1=bg1)

                def sp_S1b(tt):
                    d = SPT[tt]
                    gv, bgv = gv_ring.next()
                    gelu_G2(d["bk"], d["g1"], d["bg1"], gv, [bgv])
                    st, bst_ = st_ring.next()
                    mvx, bmv = mv_ring.next()
                    P.dve(lambda e: e.bn_stats(out=st, in_=gv), r=[bgv], w=[bst_])
                    P.dve(lambda e: e.bn_aggr(out=mvx[:, 0:2], in_=st), r=[bst_], w=[bmv])
                    d.update(gv=gv, bgv=bgv, mvx=mvx, bmv=bmv)

                def sp_S1c(tt):
                    d = SPT[tt]
                    gv, bgv, mvx, bmv = d["gv"], d["bgv"], d["mvx"], d["bmv"]
                    P.act(lambda e: e.activation(out=mvx[:, 2:3], in_=mvx[:, 1:2], func=AF.Sqrt, bias=EPS), r=[bmv], w=[bmv])
                    P.dve(lambda e: e.reciprocal(out=mvx[:, 3:4], in_=mvx[:, 2:3]), r=[bmv], w=[bmv])
                    P.dve(lambda e: e.tensor_scalar(out=gv, in0=gv, scalar1=mvx[:, 0:1], scalar2=mvx[:, 3:4], op0=ALU.subtract, op1=ALU.mult),
                          r=[bgv, bmv], w=[bgv])
                    P.pool(lambda e: e.tensor_tensor(out=gv, in0=gv, in1=rowb[:, R_LNG:R_LNG + 512], op=ALU.mult), r=[bgv, b_rowb], w=[bgv])
                    vn, bvn = vn_ring.next()
                    P.pool(lambda e: e.tensor_tensor(out=vn, in0=gv, in1=rowb[:, R_LNB:R_LNB + 512], op=ALU.add), r=[bgv, b_rowb], w=[bvn])
                    d.update(vn=vn, bvn=bvn)

                def sp_S2(tt):
                    d = SPT.pop(tt)
                    vn, bvn = d["vn"], d["bvn"]
                    tsl = slice(tt * 128, (tt + 1) * 128)
                    ob = orr.next()
                    for g in range(4):
                        MM(ps[ob][:, g * 128:(g + 1) * 128], vn[:, g * 128:(g + 1) * 128], wsb[:, g, :], True, True, r=[bvn, b_wsb], w=[PB[ob]])
                    g1, bg1 = g1_ring.next()
                    P.dve(lambda e: e.tensor_tensor(out=g1, in0=ps[ob], in1=rowb[:, R_BS:R_BS + 512], op=ALU.add),
                          r=[PB[ob], b_rowb], w=[bg1])
                    P.dve(lambda e: e.tensor_tensor(out=ysT[:, :, tsl], in0=g1.rearrange("p (g n) -> p g n", g=4), in1=guT[:, :, tsl], op=ALU.mult),
                          r=[bg1, b_gu[tt]], w=[b_ys[tt]])

                for k in range(NT + 3):
                    if k < NT:
                        sp_S1a(k)
                    if 0 <= k - 1 < NT:
                        sp_S1b(k - 1)
                    if 0 <= k - 2 < NT:
                        sp_S1c(k - 2)
                    if 0 <= k - 3 < NT:
                        sp_S2(k - 3)
                A.release()
                if dbg_on:
                    dump("d_ys", ysT, b_ys)
                A.release()
                A.mark()
                wo = A.alloc((8, D), BF16); b_wo = Buf("wo")
                finalize(l, 2, ysT, b_ys, first=False, prefetch=lambda: load_w(wo, w_o[l], w=[b_wo]))
                if dbg_on:
                    dump("d_mg", mgT, b_mg)

                ck(7)
                def update_pass(l, s, lhs_chunks, wts, b_wts, nk, b_lhs, src_x, norm_col, final, dbgname=None):
                    if norm_col is not None or final:
                        zero_ss()
                    ur = Ring([(0, 1), (2, 3)])
                    pendq = []
                    for tt in range(NT):
                        tsl = slice(tt * 128, (tt + 1) * 128)
                        ba, bb = ur.next()
                        for (bk, coff) in ((ba, 0), (bb, 512)):
                            for kc in range(nk):
                                MM(ps[bk], lhs_chunks[:, kc, tsl], wts[:, kc, coff:coff + 512], kc == 0, kc == nk - 1,
                                   r=[b_lhs[tt], b_wts], w=[PB[bk]])
                        if len(pendq) >= 2:
                            pendq.pop(0)()
                        xt, bx = xt_ring.next()
                        P.dma("sp", xt, src_x[tt * 128:(tt + 1) * 128, :], r=[b_xd[tt]], w=[bx])
                        P.dve(lambda e, xt=xt, ba=ba: e.tensor_tensor(out=xt[:, 0:512], in0=xt[:, 0:512], in1=ps[ba], op=ALU.add), r=[bx, PB[ba]], w=[bx])
                        P.dve(lambda e, xt=xt, bb=bb: e.tensor_tensor(out=xt[:, 512:1024], in0=xt[:, 512:1024], in1=ps[bb], op=ALU.add), r=[bx, PB[bb]], w=[bx])
                        if not final:
                            P.dma("pool", out[s, tt * 128:(tt + 1) * 128, :], xt, r=[bx], w=[b_xd[tt]])
                        if dbgname is not None:
                            P.dma("sp", dbg[dbgname][tt * 128:(tt + 1) * 128, :], xt, r=[bx])
                        if final:
                            norm_tile(xt, bx, None, tt, None, None, final_dst=out[s, tt * 128:(tt + 1) * 128, :])
                        elif norm_col is not None:
                            pendq.append(norm_tile(xt, bx, norm_col, tt, hT, b_hT, defer=True))
                    while pendq:
                        pendq.pop(0)()

                src_x = x_in[s] if l == 0 else out[s]
                update_pass(l, s, mgT, wo, b_wo, 8, b_mg, src_x, R_NFFN, False, "d_x1" if dbg_on else None)
                A.release()
                A.release()

                ck(8)
                A.mark()
                actT = A.alloc((11, S), BF16); b_act = bufs(NT, "act")
                w2 = A.alloc((11, D), BF16); b_w2 = Buf("w2")
                wblk = Ring([(A.alloc((8, 256), BF16), Buf(f"wblk{i}")) for i in range(3)])
                sl_ring = Ring([(A.alloc(512, F32), Buf(f"sl{i}")) for i in range(2)])
                for half in range(2):
                    fr = Ring([(0, 2), (1, 3)])
                    pend = []

                    def issue(fc):
                        wt, bwt = wblk.next()
                        gc = half * 1408 + fc * 128
                        P.dma("pool", wt[:, :, 0:128], w_f1[l, :, gc:gc + 128].rearrange("(k p) n -> p k n", p=128), w=[bwt])
                        P.dma("pool", wt[:, :, 128:256], w_f1[l, :, DFF + gc:DFF + gc + 128].rearrange("(k p) n -> p k n", p=128), w=[bwt])
                        pend.append((wt, bwt))
                    issue(0); issue(1)
                    load_w(w2, w_f2[l, half * 1408:(half + 1) * 1408, :], r=(), w=[b_w2])
                    for fc in range(11):
                        if fc + 2 < 11:
                            issue(fc + 2)
                        wt, bwt = pend[fc]
                        for tg in range(4):
                            tsl = slice(tg * 512, (tg + 1) * 512)
                            tb = [4 * tg + i for i in range(4)]
                            gb, ub = fr.next()
                            for kc in range(8):
                                MM(ps[gb], wt[:, kc, 0:128], hT[:, kc, tsl], kc == 0, kc == 7, r=[bwt] + [b_hT[t] for t in tb], w=[PB[gb]])
                            for kc in range(8):
                                MM(ps[ub], wt[:, kc, 128:256], hT[:, kc, tsl], kc == 0, kc == 7, r=[bwt] + [b_hT[t] for t in tb], w=[PB[ub]])
                            sl, bsl = sl_ring.next()
                            P.act(lambda e, sl=sl, gb=gb: e.activation(out=sl, in_=ps[gb], func=AF.Silu), r=[PB[gb]], w=[bsl])
                            P.dve(lambda e, sl=sl, ub=ub, fc=fc, tsl=tsl: e.tensor_tensor(out=actT[:, fc, tsl], in0=sl, in1=ps[ub], op=ALU.mult),
                                  r=[bsl, PB[ub]], w=[b_act[t] for t in tb])
                    last_layer = (l == nlayers - 1)
                    if half == 0:
                        update_pass(l, s, actT, w2, b_w2, 11, b_act, out[s], None, False)
                    else:
                        update_pass(l, s, actT, w2, b_w2, 11, b_act, out[s], None if last_layer else R_NNEXT, last_layer,
                                    "d_x2" if dbg_on else None)
                A.release()
        try:
            body()
        except _Stop:
            pass
        P.emit()
        print("ops/sig/per-engine:", P.stats, "arena peak KiB", A.peak / 1024.0, flush=True)
    return nc


_CACHE = {}


def prep_shared(inputs):
    f = lambda a: np.ascontiguousarray(np.asarray(a), dtype=np.float32)
    w_ukv = f(inputs["mla_w_ukv"]).reshape(L, 256, 8, 128)
    w_ukv_p = np.ascontiguousarray(np.concatenate([w_ukv[..., 0:64].reshape(L, 256, 512), w_ukv[..., 64:128].reshape(L, 256, 512)], axis=-1))
    wsT = np.ascontiguousarray(f(inputs["sg_ws"]).transpose(0, 3, 1, 2))
    rowvec = np.concatenate([f(inputs["norm_mix"]), f(inputs["norm_ffn"]), f(inputs["mla_q_norm"]), f(inputs["mla_kv_norm"]),
                             f(inputs["sg_ln_g"]), f(inputs["sg_ln_b"]), f(inputs["sg_b"]).reshape(L, 512),
                             np.concatenate([f(inputs["norm_mix"])[1:], np.zeros((1, D), np.float32)], axis=0)], axis=1)
    assert rowvec.shape == (L, NROW)
    bg = f(inputs["b_gate"]).reshape(L, 4, 8, 128).transpose(0, 3, 1, 2).reshape(L, 128, 32)
    cw = f(inputs["conv_w"]).reshape(L, 3, 4, 128).transpose(0, 3, 2, 1).reshape(L, 128, 12)
    pvec = np.ascontiguousarray(np.concatenate([bg, cw], axis=2))
    consts, _ = host_consts()
    return {
        "w_in": f(inputs["w_in"]), "w_branch": f(inputs["w_branch"]), "w_out": f(inputs["w_out"]),
        "w_ffn_in": f(inputs["w_ffn_in"]), "w_ffn_out": f(inputs["w_ffn_out"]),
        "w_uq": f(inputs["mla_w_uq"]), "w_ukv": w_ukv_p, "wsT": wsT, "rowvec": np.ascontiguousarray(rowvec),
        "fnorm": f(inputs["final_norm"]).reshape(1, D), "pvec": pvec, "consts": consts,
    }


def kernel(**inputs):
    ncores = 8
    x = np.ascontiguousarray(np.asarray(inputs["x"]), dtype=np.float32)
    pos = np.asarray(inputs["positions"]).astype(np.int32)
    B = x.shape[0]
    nseq = B // ncores
    shared = prep_shared(inputs)
    if "nc" not in _CACHE:
        _CACHE["nc"] = build(nseq=nseq)
    nc = _CACHE["nc"]
    in_maps = []
    for c in range(ncores):
        m = dict(shared)
        m["x"] = np.ascontiguousarray(x[c * nseq:(c + 1) * nseq])
        p = pos[c * nseq:(c + 1) * nseq].reshape(nseq, NT, 128).transpose(0, 2, 1)
        m["pos"] = np.ascontiguousarray(p)
        in_maps.append(m)
    res = run_bass_kernel_spmd(nc, in_maps, core_ids=list(range(ncores)))
    return np.concatenate([r["out"] for r in res.results], axis=0)
```

```python
import math
import contextlib
import numpy as np
import concourse.bass as bass
import concourse.mybir as mybir
from concourse.bass_utils import run_bass_kernel_spmd

dt = mybir.dt
F32, BF16, I32 = dt.float32, dt.bfloat16, dt.int32
AF = mybir.ActivationFunctionType
ALU = mybir.AluOpType

ENGS = ("pe", "act", "dve", "pool", "sp")
NDMASEM = 8


class Buf:
    __slots__ = ("name", "lw", "rd", "excl")

    def __init__(self, name="", excl=False):
        self.name = name
        self.lw = None
        self.rd = []
        self.excl = excl


def bufs(n, name=""):
    return [Buf(f"{name}{i}") for i in range(n)]


class Prog:
    def __init__(self, nc):
        self.nc = nc
        self.ops = []

    def add(self, eng, fn, reads=(), writes=(), dma=False):
        self.ops.append((eng, fn, tuple(reads), tuple(writes), dma))

    def pe(self, fn, r=(), w=()): self.add("pe", fn, r, w)
    def act(self, fn, r=(), w=()): self.add("act", fn, r, w)
    def dve(self, fn, r=(), w=()): self.add("dve", fn, r, w)
    def pool(self, fn, r=(), w=()): self.add("pool", fn, r, w)

    def dma(self, q, out, in_, r=(), w=()):
        self.add(q, lambda e: e.dma_start(out=out, in_=in_), r, w, dma=True)

    def fence(self):
        self.ops.append(("fence", None, (), (), False))

    def emit(self):
        nc = self.nc
        ops = self.ops
        n = len(ops)
        deps = [None] * n
        signaling = [False] * n
        last_eng = {}
        last_slot = {}
        dcount = {}
        fence_deps = []
        for k, (eng, fn, reads, writes, is_dma) in enumerate(ops):
            if eng == "fence":
                fence_deps = list(last_eng.values()) + list(last_slot.values())
                deps[k] = []
                continue
            raw = set()
            oth = set()
            if fence_deps and any((b.lw is None and not b.rd) for b in writes):
                oth.update(fence_deps)
            if is_dma:
                j = dcount.get(eng, 0)
                dcount[eng] = j + 1
                last_slot[(eng, j % NDMASEM)] = k
            else:
                last_eng[eng] = k
            for b in reads:
                if b.lw is not None:
                    raw.add(b.lw)
                if b.excl:
                    oth.update(b.rd)
            for b in writes:
                if b.lw is not None:
                    oth.add(b.lw)
                for r_ in b.rd:
                    oth.add(r_)
            for b in reads:
                b.rd.append(k)
            for b in writes:
                b.lw = k
                b.rd = []
            best = {}
            dl = []
            for p in raw | oth:
                if p == k:
                    continue
                peng, _, _, _, pdma = ops[p]
                if pdma:
                    dl.append(p)
                    continue
                if (not is_dma) and peng == eng:
                    if eng == "pe" or p not in raw:
                        continue
                if peng not in best or best[peng] < p:
                    best[peng] = p
            dl.extend(best.values())
            for p in dl:
                if not ops[p][4]:
                    signaling[p] = True
            deps[k] = dl
        cnt = {e: 0 for e in ENGS}
        sigval = [None] * n
        dmacount = {}
        slotuses = {}
        dma_prev = [None] * n
        for k, (eng, fn, reads, writes, is_dma) in enumerate(ops):
            if eng == "fence":
                continue
            if is_dma:
                j = dmacount.get(eng, 0)
                dmacount[eng] = j + 1
                slot = j % NDMASEM
                u = slotuses.get((eng, slot), 0)
                if u > 0:
                    dma_prev[k] = (("dma", eng, slot), 16 * u)
                slotuses[(eng, slot)] = u + 1
                sigval[k] = (("dma", eng, slot), 16 * (u + 1))
            elif signaling[k]:
                cnt[eng] += 1
                sigval[k] = (("eng", eng), cnt[eng])
        per = {e: [] for e in ENGS}
        seen = {e: {} for e in ENGS}
        for k, (eng, fn, reads, writes, is_dma) in enumerate(ops):
            if eng == "fence":
                continue
            waits = []
            cand = [sigval[p] for p in deps[k]]
            if dma_prev[k] is not None:
                cand.append(dma_prev[k])
            for (sk, v) in cand:
                if seen[eng].get(sk, 0) >= v:
                    continue
                seen[eng][sk] = v
                waits.append((sk, v))
            per[eng].append((waits, fn, sigval[k], is_dma))
        final_waits = [(("dma", q, s), 16 * u) for (q, s), u in slotuses.items()]
        self.stats = (n, sum(signaling), {e: len(per[e]) for e in ENGS})

        stack = contextlib.ExitStack()
        sems = {}
        with stack:
            for e in ENGS:
                sems[("eng", e)] = stack.enter_context(nc.semaphore("s_" + e))
            for q in dmacount:
                for s in range(NDMASEM):
                    sems[("dma", q, s)] = stack.enter_context(nc.semaphore(f"d_{q}_{s}"))
            block = stack.enter_context(nc.Block())

            def run(engname, e):
                for waits, fn, inc, is_dma in per[engname]:
                    for (sk, v) in waits:
                        e.wait_ge(sems[sk], v)
                    ins = fn(e)
                    if inc is not None:
                        ins.then_inc(sems[inc[0]], 16 if is_dma else 1)
                if engname == "sp":
                    for (sk, v) in final_waits:
                        e.wait_ge(sems[sk], v)

            @block.tensor
            def _(e): run("pe", e)

            @block.scalar
            def _(e): run("act", e)

            @block.vector
            def _(e): run("dve", e)

            @block.gpsimd
            def _(e): run("pool", e)

            @block.sync
            def _(e): run("sp", e)


class Arena:
    def __init__(self, t, nbytes, prog=None):
        self.t = t
        self.nbytes = nbytes
        self.off = 0
        self.marks = []
        self.peak = 0
        self.prog = prog

    def alloc(self, shape_free, dtype, parts=128):
        if isinstance(shape_free, int):
            shape_free = (shape_free,)
        esz = mybir.dt.size(dtype)
        nel = int(np.prod(shape_free))
        nb = (nel * esz + 63) // 64 * 64
        assert self.off + nb <= self.nbytes, f"arena overflow {self.off}+{nb}>{self.nbytes}"
        o = self.off
        self.off += nb
        self.peak = max(self.peak, self.off)
        ap = self.t[0:parts, o // 2:(o + nel * esz) // 2]
        if dtype != BF16:
            ap = ap.bitcast(dtype)
        if len(shape_free) > 1:
            names = " ".join(f"d{i}" for i in range(len(shape_free)))
            kw = {f"d{i}": int(s) for i, s in enumerate(shape_free)}
            ap = ap.rearrange(f"p ({names}) -> p {names}", **kw)
        return ap

    def mark(self):
        self.marks.append(self.off)

    def release(self):
        self.off = self.marks.pop()
        if self.prog is not None:
            self.prog.fence()


class Ring:
    def __init__(self, items):
        self.items = items
        self.i = 0

    def next(self):
        it = self.items[self.i % len(self.items)]
        self.i += 1
        return it


L = 2
D = 1024
S = 2048
NT = 16
NIN = 8864
DFF = 2816
C_AB, C_AC, C_AX = 0, 512, 1024
C_RQ, C_RK, C_RV, C_RG = 1536, 1792, 2048, 2560
C_SU, C_SV = 3072, 3584
C_MQ, C_MKV, C_MPE = 4096, 4480, 4736
C_GATE = 4768
EPS = 1e-6
R_NMIX, R_NFFN, R_QN, R_KVN, R_LNG, R_LNB, R_BS = 0, 1024, 2048, 2432, 2688, 3200, 3712
R_NNEXT = 4224
NROW = 5248
K_ID, K_NEG, K_M01, K_DT, K_ZS, K_XI, K_IFR, K_IFM = 0, 128, 256, 384, 896, 900, 1156, 1188
NCONST = 1204
NPV = 44
MAGIC = 12582912.0
TWO_PI = 2.0 * math.pi
C1 = float(np.float32(6.28125))
C2 = float(np.float32(TWO_PI - 6.28125))
PI_LO = 3.1415925


def host_consts():
    c = np.zeros((128, NCONST), np.float64)
    idx = np.arange(128)
    c[:, K_ID:K_ID + 128] = np.eye(128)
    kk, qq = np.meshgrid(idx, idx, indexing="ij")
    c[:, K_NEG:K_NEG + 128] = np.where(kk <= qq, 0.0, -30000.0)
    c[:, K_M01:K_M01 + 128] = np.where(kk <= qq, 1.0, 0.0)
    lg = np.log1p(-np.exp2(-5.0 - np.arange(4, dtype=np.float64)))
    scale = 64 ** -0.5
    for h in range(4):
        diff = (qq - kk).astype(np.float64)
        dtm = np.where(diff >= 0, np.exp(np.maximum(diff, 0) * lg[h]), 0.0) * scale
        c[:, K_DT + h * 128:K_DT + (h + 1) * 128] = dtm
        c[:, K_ZS + h] = np.exp((127 - idx) * lg[h]) * scale
    for cc in range(2):
        for p in range(128):
            h = 2 * cc + p // 64
            c[p, K_XI + cc * 128:K_XI + (cc + 1) * 128] = np.exp((idx + 1.0) * lg[h])
    ifr = (np.float32(10000.0) ** (-(np.arange(0, 64, 2, dtype=np.float32) / np.float32(64)))).astype(np.float32)
    ifm = (np.float32(10000.0) ** (-(np.arange(0, 32, 2, dtype=np.float32) / np.float32(32)))).astype(np.float32)
    c[:, K_IFR:K_IFR + 32] = ifr[None, :]
    c[:, K_IFM:K_IFM + 16] = ifm[None, :]
    cd = [float(np.exp(128 * lg[h])) for h in range(4)]
    return c.astype(np.float32), cd


class _Stop(Exception):
    pass


def build(nseq=2, nlayers=L, debug=False, upto=99):
    nc = bass.Bass("TRN2", target_bir_lowering=False)
    din = lambda name, shape, d=F32: nc.dram_tensor(name, list(shape), d, kind="ExternalInput").ap()
    x_in = din("x", [nseq, S, D])
    pos_in = din("pos", [nseq, 128, NT], I32)
    w_in = din("w_in", [L, D, NIN])
    w_br = din("w_branch", [L, 4, 512, D])
    w_o = din("w_out", [L, D, D])
    w_f1 = din("w_ffn_in", [L, D, 2 * DFF])
    w_f2 = din("w_ffn_out", [L, DFF, D])
    w_uq = din("w_uq", [L, 384, 768])
    w_ukv = din("w_ukv", [L, 256, 1024])
    wsT_in = din("wsT", [L, 128, 4, 128])
    rowv = din("rowvec", [L, NROW])
    fnorm = din("fnorm", [1, D])
    pv_in = din("pvec", [L, 128, NPV])
    cst_in = din("consts", [128, NCONST])
    out = nc.dram_tensor("out", [nseq, S, D], F32, kind="ExternalOutput").ap()
    dbg = {}
    if debug:
        for nm, shp in (("d_hT", [128, 8, S]), ("d_ym", [128, 4, S]), ("d_ya", [128, 4, S]), ("d_yr", [128, 4, S]),
                        ("d_ys", [128, 4, S]), ("d_mg", [128, 8, S])):
            dbg[nm] = nc.dram_tensor(nm, shp, BF16, kind="ExternalOutput").ap()
        dbg["d_x1"] = nc.dram_tensor("d_x1", [S, D], F32, kind="ExternalOutput").ap()
        dbg["d_x2"] = nc.dram_tensor("d_x2", [S, D], F32, kind="ExternalOutput").ap()
        dbg["d_cos"] = nc.dram_tensor("d_cos", [128, NT, 32], F32, kind="ExternalOutput").ap()
        dbg["d_sin"] = nc.dram_tensor("d_sin", [128, NT, 32], F32, kind="ExternalOutput").ap()

    _, CD = host_consts()
    NB = 204 * 1024
    stack = contextlib.ExitStack()
    with stack:
        at = stack.enter_context(nc.sbuf_tensor("arena", [128, NB // 2], BF16))
        pst = stack.enter_context(nc.psum_tensor("ps", [128, 8, 512], F32))
        P = Prog(nc)
        A = Arena(at, NB, P)
        PB = [Buf(f"bank{i}", excl=True) for i in range(8)]
        ps = [pst[:, i, :] for i in range(8)]
        psb = [pst[:, i, :].bitcast(BF16) for i in range(8)]

        def MM(out_ap, lhsT, rhs, start, stop, r, w):
            P.pe(lambda e: e.matmul(out_ap, lhsT=lhsT, rhs=rhs, start=start, stop=stop), r, w)

        def TR(out_ap, in_ap, ident, r, w):
            P.pe(lambda e: e.transpose(out=out_ap, in_=in_ap, identity=ident), r, w)

        cst = A.alloc(NCONST, F32); b_cst = Buf("cst")
        idb = A.alloc(128, BF16); negb = A.alloc(128, BF16); b_cb = Buf("cb")
        rowb = A.alloc(NROW, F32); b_rowb = Buf("rowb")
        fnb = A.alloc(D, F32); b_fnb = Buf("fnb")
        pvt = A.alloc(NPV, F32); b_pv = Buf("pv")
        cos_r = A.alloc((NT, 32), F32); sin_r = A.alloc((NT, 32), F32)
        cos_m = A.alloc((NT, 16), F32); sin_m = A.alloc((NT, 16), F32)
        b_rope = Buf("rope")
        hT = A.alloc((8, S), BF16); b_hT = bufs(NT, "hT")
        yT = A.alloc((4, S), BF16); b_y = bufs(NT, "y")
        b_mg = bufs(NT, "mg")
        MG = {}
        ss = A.alloc(NT, F32); sd = A.alloc(NT, F32); rs = A.alloc(NT, F32)
        b_ss = bufs(NT, "ss"); b_sd = bufs(NT, "sd"); b_rs = bufs(NT, "rs")
        junk = A.alloc(D, BF16); b_junk = Buf("junk")
        nhalf = A.alloc(4, F32); b_nh = Buf("nhalf")
        P.pool(lambda e: e.memset(nhalf, -0.5), w=[b_nh])
        xt_ring = Ring([(A.alloc(D, F32), Buf(f"xt{i}")) for i in range(3)])
        hb_ring = Ring([(A.alloc(D, BF16), Buf(f"hb{i}")) for i in range(2)])
        ident = idb
        b_xd = bufs(NT, "xd")

        P.dma("sp", cst, cst_in, w=[b_cst])
        P.dma("sp", fnb, fnorm[0:1, :].partition_broadcast(128).squeeze(1), w=[b_fnb])
        P.dve(lambda e: e.tensor_copy(out=idb, in_=cst[:, K_ID:K_ID + 128]), r=[b_cst], w=[b_cb])
        P.dve(lambda e: e.tensor_copy(out=negb, in_=cst[:, K_NEG:K_NEG + 128]), r=[b_cst], w=[b_cb])

        tr_ring = Ring([4, 5])

        def load_w(dst, src2d, r=(), w=()):
            P.dma("pool", dst, src2d.rearrange("(k p) n -> p k n", p=128), r=r, w=w)

        def norm_tile(xt, bx, gcol, tt, dstT, b_dst, final_dst=None, defer=False):
            P.act(lambda e: e.activation(out=junk, in_=xt, func=AF.Square, accum_out=ss[:, tt:tt + 1]),
                  r=[bx], w=[b_junk, b_ss[tt]])
            P.act(lambda e: e.activation(out=sd[:, tt:tt + 1], in_=ss[:, tt:tt + 1], func=AF.Sqrt, scale=1.0 / D, bias=EPS),
                  r=[b_ss[tt]], w=[b_sd[tt]])
            P.dve(lambda e: e.reciprocal(out=rs[:, tt:tt + 1], in_=sd[:, tt:tt + 1]), r=[b_sd[tt]], w=[b_rs[tt]])
            if final_dst is not None:
                ot, bo = xt_ring.next()
                P.dve(lambda e: e.scalar_tensor_tensor(out=ot, in0=xt, scalar=rs[:, tt:tt + 1], in1=fnb, op0=ALU.mult, op1=ALU.mult),
                      r=[bx, b_rs[tt], b_fnb], w=[bo])
                P.dma("pool", final_dst, ot, r=[bo], w=[b_xd[tt]])
                return
            hb, bh = hb_ring.next()
            P.dve(lambda e: e.scalar_tensor_tensor(out=hb, in0=xt, scalar=rs[:, tt:tt + 1], in1=rowb[:, gcol:gcol + D],
                                                   op0=ALU.mult, op1=ALU.mult),
                  r=[bx, b_rs[tt], b_rowb], w=[bh])

            def part2():
                bk = tr_ring.next()
                for kc in range(8):
                    TR(psb[bk][:, kc * 128:(kc + 1) * 128], hb[:, kc * 128:(kc + 1) * 128], ident, r=[bh, b_cb], w=[PB[bk]])
                P.act(lambda e: e.copy(out=dstT[:, :, tt * 128:(tt + 1) * 128], in_=psb[bk].rearrange("p (k n) -> p k n", k=8)),
                      r=[PB[bk]], w=[b_dst[tt]])
            if defer:
                return part2
            part2()

        def zero_ss():
            P.dve(lambda e: e.memset(ss, 0.0), w=b_ss)

        def finalize(l, bi, yT, b_y, first, prefetch=None):
            mgT = MG["t"]
            A.mark()
            wb = A.alloc((4, D), BF16); b_wb = Buf("wb")
            wg = A.alloc((8, D), BF16); b_wg = Buf("wg")
            sg_ring = Ring([(A.alloc(512, F32), Buf(f"sg{i}")) for i in range(2)])
            tp_ring = Ring([(A.alloc(512, F32), Buf(f"tp{i}")) for i in range(2)])
            b_wbp = bufs(2, "wbp"); b_wgp = bufs(4, "wgp")
            for pz in range(2):
                load_w(wb[:, :, pz * 512:(pz + 1) * 512], w_br[l, bi][:, pz * 512:(pz + 1) * 512], w=[b_wbp[pz]])
                for pq in range(2):
                    g0 = (pz * 2 + pq) * 256
                    load_w(wg[:, :, g0:g0 + 256], w_in[l, :, C_GATE + bi * D + g0:C_GATE + bi * D + g0 + 256], w=[b_wgp[pz * 2 + pq]])
            if prefetch is not None:
                prefetch()
            zr = Ring([0, 1]); gr = Ring([2, 3])
            for oc in range(8):
                for tg in range(4):
                    tsl = slice(tg * 512, (tg + 1) * 512)
                    tb = [4 * tg + i for i in range(4)]
                    zb = zr.next(); gb = gr.next()
                    for c in range(4):
                        MM(ps[zb], wb[:, c, oc * 128:(oc + 1) * 128], yT[:, c, tsl], c == 0, c == 3,
                           r=[b_wbp[oc // 4]] + [b_y[t] for t in tb], w=[PB[zb]])
                    for kc in range(8):
                        MM(ps[gb], wg[:, kc, oc * 128:(oc + 1) * 128], hT[:, kc, tsl], kc == 0, kc == 7,
                           r=[b_wgp[oc // 2]] + [b_hT[t] for t in tb], w=[PB[gb]])
                    sg, bsg = sg_ring.next()
                    P.act(lambda e, sg=sg, gb=gb, oc=oc: e.activation(out=sg, in_=ps[gb], func=AF.Sigmoid,
                                                                     bias=pvt[:, bi * 8 + oc:bi * 8 + oc + 1]),
                          r=[PB[gb], b_pv], w=[bsg])
                    if first:
                        P.dve(lambda e, sg=sg, zb=zb, oc=oc, tsl=tsl: e.tensor_tensor(out=mgT[:, oc, tsl], in0=ps[zb], in1=sg, op=ALU.mult),
                              r=[PB[zb], bsg], w=[b_mg[t] for t in tb])
                    else:
                        tp, btp = tp_ring.next()
                        P.dve(lambda e, sg=sg, zb=zb, tp=tp: e.tensor_tensor(out=tp, in0=ps[zb], in1=sg, op=ALU.mult),
                              r=[PB[zb], bsg], w=[btp])
                        P.pool(lambda e, tp=tp, oc=oc, tsl=tsl: e.tensor_tensor(out=mgT[:, oc, tsl], in0=mgT[:, oc, tsl], in1=tp, op=ALU.add),
                               r=[btp] + [b_mg[t] for t in tb], w=[b_mg[t] for t in tb])
            A.release()

        def dump(name, ap, rb):
            if debug and name in dbg:
                P.dma("sp", dbg[name], ap, r=rb)

        def rope_tables(s):
            A.mark()
            posi = A.alloc(NT, I32); b_pi = Buf("posi")
            posf = A.alloc(NT, F32); b_pf = Buf("posf")
            P.dma("sp", posi, pos_in[s], w=[b_pi])
            P.dve(lambda e: e.tensor_copy(out=posf, in_=posi), r=[b_pi], w=[b_pf])
            for (nf, koff, ctab, stab) in ((32, K_IFR, cos_r, sin_r), (16, K_IFM, cos_m, sin_m)):
                ang = A.alloc((NT, nf), F32); t1 = A.alloc((NT, nf), F32); t2 = A.alloc((NT, nf), F32)
                b_a = Buf("ang"); b_t1 = Buf("t1"); b_t2 = Buf("t2")
                P.dve(lambda e, ang=ang, nf=nf, koff=koff: e.tensor_tensor(
                    out=ang, in0=posf.unsqueeze(2).broadcast_to([128, NT, nf]),
                    in1=cst[:, koff:koff + nf].unsqueeze(1).broadcast_to([128, NT, nf]), op=ALU.mult),
                    r=[b_pf, b_cst], w=[b_a])
                P.dve(lambda e, ang=ang, t1=t1: e.tensor_scalar(out=t1, in0=ang, scalar1=1.0 / TWO_PI, scalar2=MAGIC, op0=ALU.mult, op1=ALU.add),
                      r=[b_a], w=[b_t1])
                P.dve(lambda e, t1=t1, t2=t2: e.tensor_scalar(out=t2, in0=t1, scalar1=-MAGIC, scalar2=None, op0=ALU.add),
                      r=[b_t1], w=[b_t2])
                P.dve(lambda e, t1=t1, t2=t2, ang=ang: e.scalar_tensor_tensor(out=t1, in0=t2, scalar=-C1, in1=ang, op0=ALU.mult, op1=ALU.add),
                      r=[b_t2, b_a], w=[b_t1])
                P.dve(lambda e, t1=t1, t2=t2, ang=ang: e.scalar_tensor_tensor(out=ang, in0=t2, scalar=-C2, in1=t1, op0=ALU.mult, op1=ALU.add),
                      r=[b_t2, b_t1], w=[b_a])
                P.dve(lambda e, ang=ang: e.tensor_scalar(out=ang, in0=ang, scalar1=-PI_LO, scalar2=PI_LO, op0=ALU.max, op1=ALU.min),
                      r=[b_a], w=[b_a])
                P.act(lambda e, ang=ang, stab=stab: e.activation(out=stab, in_=ang, func=AF.Sin), r=[b_a], w=[b_rope])
                P.act(lambda e, ang=ang, t1=t1: e.activation(out=t1, in_=ang, func=AF.Abs), r=[b_a], w=[b_t1])
                P.act(lambda e, t1=t1, ctab=ctab: e.activation(out=ctab, in_=t1, func=AF.Sin, scale=-1.0, bias=math.pi / 2),
                      r=[b_t1], w=[b_rope])
            A.release()
            if debug and s == 0:
                dump("d_cos", cos_r, [b_rope]); dump("d_sin", sin_r, [b_rope])

        def rope_apply(dst1, dst2, x1, x2, ct, st, shape, tmp, r, w):
            ta, tb_ = tmp
            b_ta = Buf("ta"); b_tb = Buf("tb")
            P.dve(lambda e: e.tensor_tensor(out=ta, in0=x1, in1=ct, op=ALU.mult), r=r, w=[b_ta])
            P.dve(lambda e: e.tensor_tensor(out=tb_, in0=x2, in1=st, op=ALU.mult), r=r, w=[b_tb])
            P.dve(lambda e: e.tensor_tensor(out=dst1, in0=ta, in1=tb_, op=ALU.subtract), r=[b_ta, b_tb], w=w)
            P.dve(lambda e: e.tensor_tensor(out=ta, in0=x1, in1=st, op=ALU.mult), r=r, w=[b_ta])
            P.dve(lambda e: e.tensor_tensor(out=tb_, in0=x2, in1=ct, op=ALU.mult), r=r, w=[b_tb])
            P.dve(lambda e: e.tensor_tensor(out=dst2, in0=ta, in1=tb_, op=ALU.add), r=[b_ta, b_tb], w=w)

        def ck(n):
            if upto < n:
                raise _Stop()

        def body():
          for s in range(nseq):
            rope_tables(s)
            ck(0)
            for l in range(nlayers):
                dbg_on = debug and s == 0 and l == 0
                P.dma("sp", rowb, rowv[l:l + 1, :].partition_broadcast(128).squeeze(1), w=[b_rowb])
                P.dma("sp", pvt, pv_in[l], w=[b_pv])
                def m0_tile(tt, defer=False):
                    xt, bx = xt_ring.next()
                    P.dma("sp", xt, x_in[s, tt * 128:(tt + 1) * 128, :], w=[bx])
                    return norm_tile(xt, bx, R_NMIX, tt, hT, b_hT, defer=defer)
                if l == 0:
                    zero_ss()

                ck(1)
                A.mark()
                qnT = A.alloc((3, S), BF16); b_qnT = bufs(NT, "qnT")
                KT = A.alloc((8, S), BF16); b_KT = bufs(NT, "KT")
                VP = A.alloc((NT, 8, 65), BF16); b_VP = bufs(NT, "VP")
                ymT = yT; b_ym = b_y
                A.mark()
                wm = A.alloc((8, 672), BF16); b_wm = Buf("wm")
                wkv = A.alloc((2, 1024), BF16); b_wkv = Buf("wkv")
                kvnT = A.alloc((2, S), BF16); b_kvnT = bufs(NT, "kvnT")
                qn_ring = Ring([(A.alloc(384, BF16), Buf(f"qn{i}")) for i in range(2)])
                kvn_ring = Ring([(A.alloc(256, BF16), Buf(f"kvn{i}")) for i in range(2)])
                kpe_ring = Ring([(A.alloc(96, BF16), Buf(f"kpe{i}")) for i in range(2)])
                st8 = A.alloc((NT, 8), F32); b_st8 = bufs(NT, "st8")
                rtmp = (A.alloc(16, F32), A.alloc(16, F32))
                load_w(wm, w_in[l, :, C_MQ:C_MQ + 672], w=[b_wm])
                load_w(wkv, w_ukv[l], w=[b_wkv])
                for (kp, bkp) in kpe_ring.items:
                    P.dve(lambda e, kp=kp: e.memset(kp, 0.0), w=[bkp])
                P.dve(lambda e: e.memset(st8, 0.0), w=b_st8)
                P.pool(lambda e: e.memset(VP, 1.0), w=b_VP)
                pr_ring = Ring([(0, 1), (2, 3)])
                MPB = {}

                def mp_A(tt):
                    tsl = slice(tt * 128, (tt + 1) * 128)
                    ba, bb = pr_ring.next()
                    for kc in range(8):
                        MM(ps[ba], hT[:, kc, tsl], wm[:, kc, 0:512], kc == 0, kc == 7, r=[b_hT[tt], b_wm], w=[PB[ba]])
                    for kc in range(8):
                        MM(ps[bb][:, 0:160], hT[:, kc, tsl], wm[:, kc, 512:672], kc == 0, kc == 7, r=[b_hT[tt], b_wm], w=[PB[bb]])
                    c0 = st8[:, tt, 0:1]; c1 = st8[:, tt, 1:2]; c2 = st8[:, tt, 2:3]
                    P.act(lambda e, ba=ba, c0=c0: e.activation(out=junk[:, 0:384], in_=ps[ba][:, 0:384], func=AF.Square, accum_out=c0),
                          r=[PB[ba]], w=[b_junk, b_st8[tt]])
                    P.act(lambda e, ba=ba, c1=c1: e.activation(out=junk[:, 0:128], in_=ps[ba][:, 384:512], func=AF.Square, accum_out=c1),
                          r=[PB[ba]], w=[b_junk, b_st8[tt]])
                    P.act(lambda e, bb=bb, c2=c2: e.activation(out=junk[:, 0:128], in_=ps[bb][:, 0:128], func=AF.Square, accum_out=c2),
                          r=[PB[bb]], w=[b_junk, b_st8[tt]])
                    MPB[tt] = (ba, bb)

                def mp_D(tt):
                    P.dve(lambda e, tt=tt: e.tensor_tensor(out=st8[:, tt, 3:4], in0=st8[:, tt, 1:2], in1=st8[:, tt, 2:3], op=ALU.add),
                          r=[b_st8[tt]], w=[b_st8[tt]])

                def mp_B(tt):
                    ba, bb = MPB.pop(tt)
                    P.act(lambda e, tt=tt: e.activation(out=st8[:, tt, 4:5], in_=st8[:, tt, 0:1], func=AF.Sqrt, scale=1.0 / 384, bias=EPS),
                          r=[b_st8[tt]], w=[b_st8[tt]])
                    P.act(lambda e, tt=tt: e.activation(out=st8[:, tt, 5:6], in_=st8[:, tt, 3:4], func=AF.Sqrt, scale=1.0 / 256, bias=EPS),
                          r=[b_st8[tt]], w=[b_st8[tt]])
                    P.dve(lambda e, tt=tt: e.reciprocal(out=st8[:, tt, 6:8], in_=st8[:, tt, 4:6]), r=[b_st8[tt]], w=[b_st8[tt]])
                    qn, bqn = qn_ring.next(); kvn, bkvn = kvn_ring.next(); kp, bkp = kpe_ring.next()
                    P.dve(lambda e, qn=qn, ba=ba, tt=tt: e.scalar_tensor_tensor(
                        out=qn, in0=ps[ba][:, 0:384], scalar=st8[:, tt, 6:7], in1=rowb[:, R_QN:R_QN + 384], op0=ALU.mult, op1=ALU.mult),
                        r=[PB[ba], b_st8[tt], b_rowb], w=[bqn])
                    P.dve(lambda e, kvn=kvn, ba=ba, tt=tt: e.scalar_tensor_tensor(
                        out=kvn[:, 0:128], in0=ps[ba][:, 384:512], scalar=st8[:, tt, 7:8], in1=rowb[:, R_KVN:R_KVN + 128], op0=ALU.mult, op1=ALU.mult),
                        r=[PB[ba], b_st8[tt], b_rowb], w=[bkvn])
                    P.dve(lambda e, kvn=kvn, bb=bb, tt=tt: e.scalar_tensor_tensor(
                        out=kvn[:, 128:256], in0=ps[bb][:, 0:128], scalar=st8[:, tt, 7:8], in1=rowb[:, R_KVN + 128:R_KVN + 256], op0=ALU.mult, op1=ALU.mult),
                        r=[PB[bb], b_st8[tt], b_rowb], w=[bkvn])
                    rope_apply(kp[:, 64:80], kp[:, 80:96], ps[bb][:, 128:144], ps[bb][:, 144:160],
                               cos_m[:, tt, :], sin_m[:, tt, :], None, rtmp, r=[PB[bb], b_rope], w=[bkp])
                    return qn, bqn, kvn, bkvn, kp, bkp

                def mp_P2(tt, qn, bqn, kvn, bkvn, kp, bkp):
                    tsl = slice(tt * 128, (tt + 1) * 128)
                    bk = tr_ring.next()
                    for c in range(3):
                        TR(psb[bk][:, c * 128:(c + 1) * 128], qn[:, c * 128:(c + 1) * 128], ident, r=[bqn, b_cb], w=[PB[bk]])
                    for c in range(2):
                        TR(psb[bk][:, (3 + c) * 128:(4 + c) * 128], kvn[:, c * 128:(c + 1) * 128], ident, r=[bkvn, b_cb], w=[PB[bk]])
                    TR(psb[bk][0:96, 640:768], kp, ident, r=[bkp, b_cb], w=[PB[bk]])
                    P.act(lambda e, bk=bk, tsl=tsl: e.copy(out=qnT[:, :, tsl], in_=psb[bk][:, 0:384].rearrange("p (k n) -> p k n", k=3)),
                          r=[PB[bk]], w=[b_qnT[tt]])
                    P.act(lambda e, bk=bk, tsl=tsl: e.copy(out=kvnT[:, :, tsl], in_=psb[bk][:, 384:640].rearrange("p (k n) -> p k n", k=2)),
                          r=[PB[bk]], w=[b_kvnT[tt]])
                    P.dve(lambda e, bk=bk, tsl=tsl: e.tensor_copy(out=KT[64:96, :, tsl],
                                                                  in_=psb[bk][64:96, 640:768].unsqueeze(1).broadcast_to([32, 8, 128])),
                          r=[PB[bk]], w=[b_KT[tt]])
                    vb = 6 + (tt % 2)
                    for kc in range(2):
                        MM(ps[vb], kvnT[:, kc, tsl], wkv[:, kc, 512:1024], kc == 0, kc == 1, r=[b_kvnT[tt], b_wkv], w=[PB[vb]])
                    P.act(lambda e, vb=vb, tt=tt: e.copy(out=VP[:, tt, :, 0:64], in_=ps[vb].rearrange("p (h d) -> p h d", h=8)),
                          r=[PB[vb]], w=[b_VP[tt]])

                m0p = None
                if l == 0:
                    m0_tile(0)
                    m0_tile(1)
                    m0p = m0_tile(2, defer=True)
                mres = {}
                for k in range(NT + 2):
                    m0n = None
                    if l == 0 and k + 3 < NT:
                        m0n = m0_tile(k + 3, defer=True)
                    if k < NT:
                        mp_A(k)
                    if m0p is not None:
                        m0p()
                    m0p = m0n
                    if 0 <= k - 1 < NT:
                        mres[k - 1] = mp_B(k - 1)
                    if 0 <= k - 2 < NT:
                        mp_P2(k - 2, *mres.pop(k - 2))
                    if k < NT:
                        mp_D(k)
                if dbg_on:
                    dump("d_hT", hT, b_hT)
                kr = Ring([0, 1, 2, 3])
                for h in range(8):
                    for tg in range(4):
                        tsl = slice(tg * 512, (tg + 1) * 512)
                        tb = [4 * tg + i for i in range(4)]
                        kb = kr.next()
                        for kc in range(2):
                            MM(ps[kb][0:64, :], wkv[:, kc, h * 64:(h + 1) * 64], kvnT[:, kc, tsl], kc == 0, kc == 1,
                               r=[b_wkv] + [b_kvnT[t] for t in tb], w=[PB[kb]])
                        if (h * 4 + tg) % 2 == 0:
                            P.act(lambda e, kb=kb, h=h, tsl=tsl: e.copy(out=KT[0:64, h, tsl], in_=ps[kb][0:64, :]),
                                  r=[PB[kb]], w=[b_KT[t] for t in tb])
                        else:
                            P.dve(lambda e, kb=kb, h=h, tsl=tsl: e.tensor_copy(out=KT[0:64, h, tsl], in_=ps[kb][0:64, :]),
                                  r=[PB[kb]], w=[b_KT[t] for t in tb])
                A.release()
                ck(2)
                A.mark()
                wq = A.alloc((3, 768), BF16); b_wq = Buf("wq")
                load_w(wq, w_uq[l], w=[b_wq])
                qf_ring = Ring([(A.alloc((8, 96), BF16), Buf(f"qf{i}")) for i in range(3)])
                qi_ring = Ring([(A.alloc((8, 128), BF16), Buf(f"qi{i}")) for i in range(4)])
                p_ring = Ring([(A.alloc(512, BF16), Buf(f"p{i}")) for i in range(4)])
                yt_ring = Ring([(A.alloc(512, BF16), Buf(f"yt{i}")) for i in range(2)])
                rc_ring = Ring([(A.alloc(8, F32), Buf(f"rc{i}")) for i in range(2)])
                rtq = (A.alloc((8, 16), F32), A.alloc((8, 16), F32))
                qpe = A.alloc((8, 32), F32); bqpe = Buf("qpe")
                sc_ring = Ring([3, 4, 5])
                ASCALE = 96 ** -0.5
                def q_stage(i):
                    tsl = slice(i * 128, (i + 1) * 128)
                    qa, qb = (0, 1) if i % 2 == 0 else (6, 7)
                    for kc in range(3):
                        MM(ps[qa], qnT[:, kc, tsl], wq[:, kc, 0:512], kc == 0, kc == 2, r=[b_qnT[i], b_wq], w=[PB[qa]])
                    for kc in range(3):
                        MM(ps[qb][:, 0:256], qnT[:, kc, tsl], wq[:, kc, 512:768], kc == 0, kc == 2, r=[b_qnT[i], b_wq], w=[PB[qb]])
                    qf, bqf = qf_ring.next()
                    qff = qf.rearrange("p h d -> p (h d)")
                    P.dve(lambda e: e.tensor_copy(out=qff[:, 0:512], in_=ps[qa]), r=[PB[qa]], w=[bqf])
                    P.dve(lambda e: e.tensor_copy(out=qff[:, 512:768], in_=ps[qb][:, 0:256]), r=[PB[qb]], w=[bqf])
                    cb = cos_m[:, i, :].unsqueeze(1).broadcast_to([128, 8, 16])
                    sb_ = sin_m[:, i, :].unsqueeze(1).broadcast_to([128, 8, 16])
                    P.dve(lambda e: e.tensor_copy(out=qpe, in_=qf[:, :, 64:96]), r=[bqf], w=[bqpe])
                    rope_apply(qf[:, :, 64:80], qf[:, :, 80:96], qpe[:, :, 0:16], qpe[:, :, 16:32], cb, sb_, None, rtq,
                               r=[bqpe, b_rope], w=[bqf])
                    for h in range(8):
                        TR(psb[2][0:96, h * 128:(h + 1) * 128], qf[:, h, :], ident, r=[bqf, b_cb], w=[PB[2]])
                    qi, bqi = qi_ring.next()
                    P.dve(lambda e: e.tensor_copy(out=qi[0:96], in_=psb[2][0:96, :].rearrange("p (h n) -> p h n", h=8)),
                          r=[PB[2]], w=[bqi])
                    return qi, bqi

                def qk_stage(i, qi, bqi, h, js):
                    sb2 = sc_ring.next()
                    for jl, j in enumerate(js):
                        diag = (j == i)
                        MM(ps[sb2][:, jl * 128:(jl + 1) * 128], KT[0:96, h, j * 128:(j + 1) * 128], qi[0:96, h, :],
                           True, not diag, r=[b_KT[j], bqi], w=[PB[sb2]])
                        if diag:
                            MM(ps[sb2][:, jl * 128:(jl + 1) * 128], ident, negb, False, True, r=[b_cb], w=[PB[sb2]])
                    pt, bpt = p_ring.next()
                    nj = len(js)
                    P.act(lambda e: e.activation(out=pt[:, 0:nj * 128], in_=ps[sb2][:, 0:nj * 128], func=AF.Exp, scale=ASCALE),
                          r=[PB[sb2]], w=[bpt])
                    return pt, bpt

                def pv_stage(i, h, js, pt, bpt):
                    ob = (6 if i % 2 == 0 else 0) + h // 4
                    ocol = (h % 4) * 65
                    for jl, j in enumerate(js):
                        MM(ps[ob][:, ocol:ocol + 65], pt[:, jl * 128:(jl + 1) * 128], VP[:, j, h, :], j == 0, j == i,
                           r=[bpt, b_VP[j]], w=[PB[ob]])

                def epilogue(i):
                    tsl = slice(i * 128, (i + 1) * 128)
                    rc, brc = rc_ring.next()
                    yt, byt = yt_ring.next()
                    for hb2 in range(2):
                        ob = (6 if i % 2 == 0 else 0) + hb2
                        pv4 = ps[ob][:, 0:260].rearrange("p (h d) -> p h d", h=4)
                        P.dve(lambda e, pv4=pv4, hb2=hb2: e.reciprocal(out=rc[:, hb2 * 4:(hb2 + 1) * 4], in_=pv4[:, :, 64]),
                              r=[PB[ob]], w=[brc])
                        P.dve(lambda e, pv4=pv4, hb2=hb2: e.tensor_tensor(
                            out=yt[:, hb2 * 256:(hb2 + 1) * 256].rearrange("p (h d) -> p h d", h=4), in0=pv4[:, :, 0:64],
                            in1=rc[:, hb2 * 4:(hb2 + 1) * 4].unsqueeze(2).broadcast_to([128, 4, 64]), op=ALU.mult),
                            r=[PB[ob], brc], w=[byt])
                    for c in range(4):
                        TR(psb[2][:, c * 128:(c + 1) * 128], yt[:, c * 128:(c + 1) * 128], ident, r=[byt, b_cb], w=[PB[2]])
                    P.dve(lambda e: e.tensor_copy(out=ymT[:, :, tsl], in_=psb[2][:, 0:512].rearrange("p (k n) -> p k n", k=4)),
                          r=[PB[2]], w=[b_ym[i]])

                LOOK = 2
                qs = {0: q_stage(0), 1: q_stage(1)}
                for i in range(NT):
                    qi, bqi = qs.pop(i)
                    chunks = [(h, list(range(j0, min(j0 + 4, i + 1)))) for h in range(8) for j0 in range(0, i + 1, 4)]
                    pend = []
                    for (h, js) in chunks:
                        pt, bpt = qk_stage(i, qi, bqi, h, js)
                        pend.append((h, js, pt, bpt))
                        if len(pend) > LOOK:
                            pv_stage(i, *pend.pop(0))
                    if i + 2 < NT:
                        qs[i + 2] = q_stage(i + 2)
                    while pend:
                        pv_stage(i, *pend.pop(0))
                    epilogue(i)
                A.release()
                if dbg_on:
                    dump("d_ym", ymT, b_ym)
                A.release()
                A.mark()
                mgT = A.alloc((8, S), BF16)
                MG["t"] = mgT
                ck(3)
                A.mark()
                wa = A.alloc((8, 1536), BF16); b_wa = Buf("wa")
                finalize(l, 3, ymT, b_ym, first=True, prefetch=lambda: load_w(wa, w_in[l, :, 0:1536], w=[b_wa]))

                ck(4)
                A.mark()
                yaT = yT; b_ya = b_y
                pb_ring = Ring([(A.alloc(S + 2, F32), Buf(f"pbuf{i}")) for i in range(2)])
                ab_ring = Ring([(A.alloc(S, F32), Buf(f"abuf{i}")) for i in range(2)])
                cacc = A.alloc(S, F32); b_cacc = Buf("cacc")
                axr = Ring([(A.alloc(512, F32), Buf(f"ax{i}")) for i in range(2)])
                for (pb_, bpb_) in pb_ring.items:
                    P.dve(lambda e, pb_=pb_: e.memset(pb_[:, 0:2], 0.0), w=[bpb_])
                cr = Ring([(0, 1, 2), (3, 4, 5)])
                for c in range(4):
                    pbuf, b_pbuf = pb_ring.next()
                    abuf, b_abuf = ab_ring.next()
                    for tg in range(4):
                        tsl = slice(tg * 512, (tg + 1) * 512)
                        tb = [4 * tg + i for i in range(4)]
                        b0, b1, b2 = cr.next()
                        for (bk, coff) in ((b0, C_AB), (b1, C_AC), (b2, C_AX)):
                            for kc in range(8):
                                MM(ps[bk], wa[:, kc, coff + c * 128:coff + (c + 1) * 128], hT[:, kc, tsl], kc == 0, kc == 7,
                                   r=[b_wa] + [b_hT[t] for t in tb], w=[PB[bk]])
                        ax, bax = axr.next()
                        P.act(lambda e, ax=ax, b2=b2: e.copy(out=ax, in_=ps[b2]), r=[PB[b2]], w=[bax])
                        P.act(lambda e, b0=b0, tsl=tsl, abuf=abuf: e.copy(out=abuf[:, tsl], in_=ps[b0]), r=[PB[b0]], w=[b_abuf])
                        P.dve(lambda e, ax=ax, b1=b1, tg=tg, pbuf=pbuf: e.tensor_tensor(out=pbuf[:, 2 + tg * 512:2 + (tg + 1) * 512], in0=ps[b1], in1=ax, op=ALU.mult),
                              r=[PB[b1], bax], w=[b_pbuf])
                    w0 = pvt[:, 32 + c * 3 + 0:32 + c * 3 + 1]; w1 = pvt[:, 32 + c * 3 + 1:32 + c * 3 + 2]; w2 = pvt[:, 32 + c * 3 + 2:32 + c * 3 + 3]
                    P.dve(lambda e, w0=w0, pbuf=pbuf: e.tensor_scalar(out=cacc, in0=pbuf[:, 0:S], scalar1=w0, scalar2=None, op0=ALU.mult),
                          r=[b_pbuf, b_pv], w=[b_cacc])
                    P.dve(lambda e, w1=w1, pbuf=pbuf: e.scalar_tensor_tensor(out=cacc, in0=pbuf[:, 1:S + 1], scalar=w1, in1=cacc, op0=ALU.mult, op1=ALU.add),
                          r=[b_pbuf, b_pv, b_cacc], w=[b_cacc])
                    P.dve(lambda e, w2=w2, pbuf=pbuf: e.scalar_tensor_tensor(out=cacc, in0=pbuf[:, 2:S + 2], scalar=w2, in1=cacc, op0=ALU.mult, op1=ALU.add),
                          r=[b_pbuf, b_pv, b_cacc], w=[b_cacc])
                    P.pool(lambda e, c=c, abuf=abuf: e.tensor_tensor(out=yaT[:, c, :], in0=cacc, in1=abuf, op=ALU.mult),
                           r=[b_cacc, b_abuf], w=b_ya)
                if dbg_on:
                    dump("d_ya", yaT, b_ya)
                A.release()
                A.release()
                A.mark()
                wr = A.alloc((8, 1536), BF16); b_wr = Buf("wr")
                finalize(l, 0, yaT, b_ya, first=False, prefetch=lambda: load_w(wr, w_in[l, :, C_RQ:C_RQ + 1536], w=[b_wr]))
                ck(4.5)

                ck(5)
                yrT = yT; b_yr = b_y
                A.mark()
                Sst = A.alloc((2, 128), F32); b_S = Buf("S")
                Sbf = A.alloc((2, 128), BF16); b_Sbf = Buf("Sbf")
                P.dve(lambda e: e.memset(Sst, 0.0), w=[b_S])
                P.dve(lambda e: e.memset(Sbf, 0.0), w=[b_Sbf])
                qk_ring = Ring([(A.alloc(512, BF16), Buf(f"qk{i}")) for i in range(3)])
                kz_ring = Ring([(A.alloc(256, BF16), Buf(f"kz{i}")) for i in range(5)])
                vb_ring = Ring([(A.alloc(512, BF16), Buf(f"vb{i}")) for i in range(5)])
                sgr_ring = Ring([(A.alloc(512, BF16), Buf(f"sgr{i}")) for i in range(5)])
                qT_ring = Ring([(A.alloc((2, 2, 128), BF16), Buf(f"qTz{i}")) for i in range(3)])
                kT_ring = Ring([(A.alloc((2, 128), BF16), Buf(f"kT{i}")) for i in range(3)])
                qx_ring = Ring([(A.alloc((2, 2, 128), BF16), Buf(f"qx{i}")) for i in range(4)])
                for (qz, bqz) in qT_ring.items:
                    P.dve(lambda e, qz=qz: e.memset(qz, 0.0), w=[bqz])
                at_ring = Ring([(A.alloc((4, 128), BF16), Buf(f"at{i}")) for i in range(3)])
                on_ring = Ring([(A.alloc(512, F32), Buf(f"on{i}")) for i in range(2)])
                yr_ring = Ring([(A.alloc(512, BF16), Buf(f"yrt{i}")) for i in range(3)])
                bst = A.alloc((4, 6), F32); b_bst = Buf("bst")
                mv = A.alloc((4, 2), F32); b_mv = Buf("mv")
                rsd = A.alloc(12, F32); b_rsd = Buf("rsd")
                rtr = (A.alloc((8, 32), F32), A.alloc((8, 32), F32))
                RST = {}

                def ret_A1(tt):
                    tsl = slice(tt * 128, (tt + 1) * 128)
                    for (bk, coff) in ((0, 0), (1, 512), (2, 1024)):
                        for kc in range(8):
                            MM(ps[bk], hT[:, kc, tsl], wr[:, kc, coff:coff + 512], kc == 0, kc == 7, r=[b_hT[tt], b_wr], w=[PB[bk]])
                    qk, bqk = qk_ring.next()
                    src = ps[0].rearrange("p (h d) -> p h d", h=8)
                    dst = qk.rearrange("p (h d) -> p h d", h=8)
                    cb = cos_r[:, tt, :].unsqueeze(1).broadcast_to([128, 8, 32])
                    sb_ = sin_r[:, tt, :].unsqueeze(1).broadcast_to([128, 8, 32])
                    rope_apply(dst[:, :, 0:32], dst[:, :, 32:64], src[:, :, 0:32], src[:, :, 32:64], cb, sb_, None, rtr,
                               r=[PB[0], b_rope], w=[bqk])
                    kz, bkz = kz_ring.next()
                    P.pool(lambda e: e.tensor_tensor(
                        out=kz.rearrange("p (h d) -> p h d", h=4), in0=qk[:, 256:512].rearrange("p (h d) -> p h d", h=4),
                        in1=cst[:, K_ZS:K_ZS + 4].unsqueeze(2).broadcast_to([128, 4, 64]), op=ALU.mult),
                        r=[bqk, b_cst], w=[bkz])
                    vbt, bvb = vb_ring.next()
                    P.act(lambda e: e.copy(out=vbt, in_=ps[1]), r=[PB[1]], w=[bvb])
                    sgr, bsgr = sgr_ring.next()
                    P.act(lambda e: e.activation(out=sgr, in_=ps[2], func=AF.Silu), r=[PB[2]], w=[bsgr])
                    RST[tt] = dict(qk=qk, bqk=bqk, kz=kz, bkz=bkz, vbt=vbt, bvb=bvb, sgr=sgr, bsgr=bsgr)

                def ret_A3(tt):
                    d = RST[tt]
                    qk, bqk = d["qk"], d["bqk"]
                    for c in range(4):
                        TR(psb[3][:, c * 128:(c + 1) * 128], qk[:, c * 128:(c + 1) * 128], ident, r=[bqk, b_cb], w=[PB[3]])
                    qTz, bqT = qT_ring.next()
                    kT, bkT = kT_ring.next()
                    P.act(lambda e: e.copy(out=qTz[0:64, :, 0, :], in_=psb[3][0:64, 0:256].rearrange("p (c n) -> p c n", c=2)),
                          r=[PB[3]], w=[bqT])
                    P.act(lambda e: e.copy(out=qTz[64:128, :, 1, :], in_=psb[3][64:128, 0:256].rearrange("p (c n) -> p c n", c=2)),
                          r=[PB[3]], w=[bqT])
                    P.act(lambda e: e.copy(out=kT, in_=psb[3][:, 256:512].rearrange("p (c n) -> p c n", c=2)),
                          r=[PB[3]], w=[bkT])
                    qx, bqx = qx_ring.next()
                    P.pool(lambda e: e.tensor_tensor(
                        out=qx, in0=qTz,
                        in1=cst[:, K_XI:K_XI + 256].rearrange("p (c n) -> p c n", c=2).unsqueeze(2).broadcast_to([128, 2, 2, 128]), op=ALU.mult),
                        r=[bqT, b_cst], w=[bqx])
                    d.update(qTz=qTz, bqT=bqT, kT=kT, bkT=bkT, qx=qx, bqx=bqx)

                def ret_B1(tt):
                    d = RST[tt]
                    qTz, bqT, kT, bkT = d["qTz"], d["bqT"], d["kT"], d["bkT"]
                    for h in range(4):
                        c = h // 2; hf = h % 2
                        MM(ps[4][:, h * 128:(h + 1) * 128], kT[:, c, :], qTz[:, c, hf, :], True, True, r=[bkT, bqT], w=[PB[4]])
                    att, bat = at_ring.next()
                    P.dve(lambda e: e.tensor_tensor(out=att.rearrange("p h n -> p (h n)"), in0=ps[4], in1=cst[:, K_DT:K_DT + 512], op=ALU.mult),
                          r=[PB[4], b_cst], w=[bat])
                    d.update(att=att, bat=bat)

                def ret_B2(tt):
                    d = RST[tt]
                    kz, bkz, vbt, bvb, sgr, bsgr = d["kz"], d["bkz"], d["vbt"], d["bvb"], d["sgr"], d["bsgr"]
                    qx, bqx, att, bat = d["qx"], d["bqx"], d["att"], d["bat"]
                    for h in range(4):
                        c = h // 2
                        MM(ps[5][:, h * 128:(h + 1) * 128], att[:, h, :], vbt[:, h * 128:(h + 1) * 128], True, False, r=[bat, bvb], w=[PB[5]])
                        MM(ps[5][:, h * 128:(h + 1) * 128], qx[:, c, h % 2, :], Sbf[:, c, :], False, True, r=[bqx, b_Sbf], w=[PB[5]])
                    for c in range(2):
                        MM(ps[6 + c], kz[:, c * 128:(c + 1) * 128], vbt, True, True, r=[bkz, bvb], w=[PB[6 + c]])
                    for h in range(4):
                        c = h // 2; po = (h % 2) * 64
                        P.dve(lambda e, h=h, c=c, po=po: e.scalar_tensor_tensor(
                            out=Sst[po:po + 64, c, :], in0=Sst[po:po + 64, c, :], scalar=CD[h], in1=ps[6 + c][po:po + 64, h * 128:(h + 1) * 128],
                            op0=ALU.mult, op1=ALU.add), r=[b_S, PB[6 + c]], w=[b_S])
                    P.act(lambda e: e.copy(out=Sbf, in_=Sst), r=[b_S], w=[b_Sbf])
                    for h in range(4):
                        P.dve(lambda e, h=h: e.bn_stats(out=bst[:, h, :], in_=ps[5][:, h * 128:(h + 1) * 128]), r=[PB[5]], w=[b_bst])
                    for h in range(4):
                        P.dve(lambda e, h=h: e.bn_aggr(out=mv[:, h, :], in_=bst[:, h, :]), r=[b_bst], w=[b_mv])
                    P.pool(lambda e: e.tensor_scalar(out=rsd[:, 0:4], in0=mv[:, :, 1], scalar1=EPS, scalar2=None, op0=ALU.add), r=[b_mv], w=[b_rsd])
                    P.pool(lambda e: e.tensor_tensor(out=rsd[:, 4:8], in0=rsd[:, 0:4], in1=nhalf[:, 0:4], op=ALU.pow), r=[b_rsd, b_nh], w=[b_rsd])
                    on, bon = on_ring.next()
                    P.dve(lambda e: e.scalar_tensor_tensor(out=rsd[:, 8:12], in0=mv[:, :, 0], scalar=-1.0, in1=rsd[:, 4:8], op0=ALU.mult, op1=ALU.mult),
                          r=[b_mv, b_rsd], w=[b_rsd])
                    for h in range(4):
                        P.act(lambda e, h=h: e.activation(out=on[:, h * 128:(h + 1) * 128], in_=ps[5][:, h * 128:(h + 1) * 128],
                                                          func=AF.Identity, scale=rsd[:, 4 + h:5 + h], bias=rsd[:, 8 + h:9 + h]),
                              r=[PB[5], b_rsd], w=[bon])
                    yrt, byrt = yr_ring.next()
                    P.pool(lambda e: e.tensor_tensor(out=yrt, in0=on, in1=sgr, op=ALU.mult), r=[bon, bsgr], w=[byrt])
                    d.update(yrt=yrt, byrt=byrt)

                def ret_B3(tt):
                    d = RST.pop(tt)
                    tsl = slice(tt * 128, (tt + 1) * 128)
                    yrt, byrt = d["yrt"], d["byrt"]
                    for c in range(4):
                        TR(psb[3][:, c * 128:(c + 1) * 128], yrt[:, c * 128:(c + 1) * 128], ident, r=[byrt, b_cb], w=[PB[3]])
                    P.act(lambda e: e.copy(out=yrT[:, :, tsl], in_=psb[3][:, 0:512].rearrange("p (k n) -> p k n", k=4)),
                          r=[PB[3]], w=[b_yr[tt]])

                for k in range(NT + 4):
                    if k < NT:
                        ret_A1(k)
                    if 0 <= k - 1 < NT:
                        ret_A3(k - 1)
                    if 0 <= k - 2 < NT:
                        ret_B1(k - 2)
                    if 0 <= k - 3 < NT:
                        ret_B2(k - 3)
                    if 0 <= k - 4 < NT:
                        ret_B3(k - 4)
                A.release()
                if dbg_on:
                    dump("d_yr", yrT, b_yr)
                A.release()
                A.mark()
                wsg = A.alloc((8, 1024), BF16); b_wsg = Buf("wsg")
                finalize(l, 1, yrT, b_yr, first=False, prefetch=lambda: load_w(wsg, w_in[l, :, C_SU:C_SU + 1024], w=[b_wsg]))

                ck(6)
                ysT = yT; b_ys = b_y
                A.mark()
                guT = A.alloc((4, S), BF16); b_gu = bufs(NT, "gu")
                wsf = A.alloc((4, 128), F32); b_wsf = Buf("wsf")
                wsb = A.alloc((4, 128), BF16); b_wsb = Buf("wsb")
                P.dma("sp", wsf, wsT_in[l], w=[b_wsf])
                P.dve(lambda e: e.tensor_tensor(out=wsb, in0=wsf, in1=cst[:, K_M01:K_M01 + 128].unsqueeze(1).broadcast_to([128, 4, 128]), op=ALU.mult),
                      r=[b_wsf, b_cst], w=[b_wsb])
                g1_ring = Ring([(A.alloc(512, F32), Buf(f"g1{i}")) for i in range(5)])
                g2_ring = Ring([(A.alloc(512, F32), Buf(f"g2{i}")) for i in range(3)])
                gv_ring = Ring([(A.alloc(512, F32), Buf(f"gv{i}")) for i in range(3)])
                vn_ring = Ring([(A.alloc(512, BF16), Buf(f"vn{i}")) for i in range(3)])
                st6 = A.alloc(6, F32); b_st6 = Buf("st6")
                mv2 = A.alloc(4, F32); b_mv2 = Buf("mv2")
                GC = 2.0 * math.sqrt(2.0 / math.pi)

                def gelu_from_psum(bk, dst_ap, w_dst):
                    g1, bg1 = g1_ring.next(); g2, bg2 = g2_ring.next()
                    P.act(lambda e: e.activation(out=g1, in_=ps[bk], func=AF.Square), r=[PB[bk]], w=[bg1])
                    P.dve(lambda e: e.tensor_scalar(out=g1, in0=g1, scalar1=0.044715, scalar2=1.0, op0=ALU.mult, op1=ALU.add), r=[bg1], w=[bg1])
                    P.dve(lambda e: e.tensor_tensor(out=g1, in0=g1, in1=ps[bk], op=ALU.mult), r=[bg1, PB[bk]], w=[bg1])
                    P.act(lambda e: e.activation(out=g2, in_=g1, func=AF.Sigmoid, scale=GC), r=[bg1], w=[bg2])
                    P.dve(lambda e: e.tensor_tensor(out=dst_ap, in0=g2, in1=ps[bk], op=ALU.mult), r=[bg2, PB[bk]], w=w_dst)

                def gelu_G1(bk):
                    g1, bg1 = g1_ring.next()
                    P.act(lambda e: e.activation(out=g1, in_=ps[bk], func=AF.Square, scale=math.sqrt(0.044715)), r=[PB[bk]], w=[bg1])
                    P.dve(lambda e: e.scalar_tensor_tensor(out=g1, in0=g1, scalar=1.0, in1=ps[bk], op0=ALU.add, op1=ALU.mult), r=[bg1, PB[bk]], w=[bg1])
                    return g1, bg1

                def gelu_G2(bk, g1, bg1, dst_ap, w_dst):
                    g2, bg2 = g2_ring.next()
                    P.act(lambda e: e.activation(out=g2, in_=g1, func=AF.Sigmoid, scale=GC), r=[bg1], w=[bg2])
                    P.dve(lambda e: e.tensor_tensor(out=dst_ap, in0=g2, in1=ps[bk], op=ALU.mult), r=[bg2, PB[bk]], w=w_dst)

                sr = Ring([0, 1, 6])
                gpend = None
                for g in range(4):
                    for tg in range(4):
                        tsl = slice(tg * 512, (tg + 1) * 512)
                        tb = [4 * tg + i for i in range(4)]
                        bk = sr.next()
                        for kc in range(8):
                            MM(ps[bk], wsg[:, kc, g * 128:(g + 1) * 128], hT[:, kc, tsl], kc == 0, kc == 7,
                               r=[b_wsg] + [b_hT[t] for t in tb], w=[PB[bk]])
                        g1, bg1 = gelu_G1(bk)
                        if gpend is not None:
                            gelu_G2(*gpend)
                        gpend = (bk, g1, bg1, guT[:, g, tsl], [b_gu[t] for t in tb])
                gelu_G2(*gpend)
                vr = Ring([2, 3, 7]); orr = Ring([4, 5])
                mv_ring = Ring([(A.alloc(4, F32), Buf(f"mv2_{i}")) for i in range(3)])
                st_ring = Ring([(A.alloc(6, F32), Buf(f"st6_{i}")) for i in range(3)])
                SPT = {}

                def sp_S1a(tt):
                    tsl = slice(tt * 128, (tt + 1) * 128)
                    bk = vr.next()
                    for kc in range(8):
                        MM(ps[bk], hT[:, kc, tsl], wsg[:, kc, 512:1024], kc == 0, kc == 7, r=[b_hT[tt], b_wsg], w=[PB[bk]])
                    g1, bg1 = gelu_G1(bk)
                    SPT[tt] = dict(bk=bk, g1=g1, bg1=bg1)

                def sp_S1b(tt):
                    d = SPT[tt]
                    gv, bgv = gv_ring.next()
                    gelu_G2(d["bk"], d["g1"], d["bg1"], gv, [bgv])
                    st, bst_ = st_ring.next()
                    mvx, bmv = mv_ring.next()
                    P.dve(lambda e: e.bn_stats(out=st, in_=gv), r=[bgv], w=[bst_])
                    P.dve(lambda e: e.bn_aggr(out=mvx[:, 0:2], in_=st), r=[bst_], w=[bmv])
                    d.update(gv=gv, bgv=bgv, mvx=mvx, bmv=bmv)

                def sp_S1c(tt):
                    d = SPT[tt]
                    gv, bgv, mvx, bmv = d["gv"], d["bgv"], d["mvx"], d["bmv"]
                    P.pool(lambda e: e.tensor_scalar(out=mvx[:, 2:3], in0=mvx[:, 1:2], scalar1=EPS, scalar2=None, op0=ALU.add), r=[bmv], w=[bmv])
                    P.pool(lambda e: e.tensor_tensor(out=mvx[:, 3:4], in0=mvx[:, 2:3], in1=nhalf[:, 0:1], op=ALU.pow), r=[bmv, b_nh], w=[bmv])
                    P.dve(lambda e: e.tensor_scalar(out=gv, in0=gv, scalar1=mvx[:, 0:1], scalar2=mvx[:, 3:4], op0=ALU.subtract, op1=ALU.mult),
                          r=[bgv, bmv], w=[bgv])
                    P.pool(lambda e: e.tensor_tensor(out=gv, in0=gv, in1=rowb[:, R_LNG:R_LNG + 512], op=ALU.mult), r=[bgv, b_rowb], w=[bgv])
                    vn, bvn = vn_ring.next()
                    P.pool(lambda e: e.tensor_tensor(out=vn, in0=gv, in1=rowb[:, R_LNB:R_LNB + 512], op=ALU.add), r=[bgv, b_rowb], w=[bvn])
                    d.update(vn=vn, bvn=bvn)

                def sp_S2(tt):
                    d = SPT.pop(tt)
                    vn, bvn = d["vn"], d["bvn"]
                    tsl = slice(tt * 128, (tt + 1) * 128)
                    ob = orr.next()
                    for g in range(4):
                        MM(ps[ob][:, g * 128:(g + 1) * 128], vn[:, g * 128:(g + 1) * 128], wsb[:, g, :], True, True, r=[bvn, b_wsb], w=[PB[ob]])
                    g1, bg1 = g1_ring.next()
                    P.dve(lambda e: e.tensor_tensor(out=g1, in0=ps[ob], in1=rowb[:, R_BS:R_BS + 512], op=ALU.add),
                          r=[PB[ob], b_rowb], w=[bg1])
                    P.dve(lambda e: e.tensor_tensor(out=ysT[:, :, tsl], in0=g1.rearrange("p (g n) -> p g n", g=4), in1=guT[:, :, tsl], op=ALU.mult),
                          r=[bg1, b_gu[tt]], w=[b_ys[tt]])

                for k in range(NT + 3):
                    if k < NT:
                        sp_S1a(k)
                    if 0 <= k - 1 < NT:
                        sp_S1b(k - 1)
                    if 0 <= k - 2 < NT:
                        sp_S1c(k - 2)
                    if 0 <= k - 3 < NT:
                        sp_S2(k - 3)
                A.release()
                if dbg_on:
                    dump("d_ys", ysT, b_ys)
                A.release()
                A.mark()
                wo = A.alloc((8, D), BF16); b_wo = Buf("wo")
                finalize(l, 2, ysT, b_ys, first=False, prefetch=lambda: load_w(wo, w_o[l], w=[b_wo]))
                if dbg_on:
                    dump("d_mg", mgT, b_mg)

                ck(7)
                def update_pass(l, s, lhs_chunks, wts, b_wts, nk, b_lhs, src_x, norm_col, final, dbgname=None):
                    if norm_col is not None or final:
                        zero_ss()
                    ur = Ring([(0, 1), (2, 3)])
                    pend2 = None
                    for tt in range(NT):
                        tsl = slice(tt * 128, (tt + 1) * 128)
                        ba, bb = ur.next()
                        for (bk, coff) in ((ba, 0), (bb, 512)):
                            for kc in range(nk):
                                MM(ps[bk], lhs_chunks[:, kc, tsl], wts[:, kc, coff:coff + 512], kc == 0, kc == nk - 1,
                                   r=[b_lhs[tt], b_wts], w=[PB[bk]])
                        if pend2 is not None:
                            pend2()
                            pend2 = None
                        xt, bx = xt_ring.next()
                        P.dma("sp", xt, src_x[tt * 128:(tt + 1) * 128, :], r=[b_xd[tt]], w=[bx])
                        P.dve(lambda e, xt=xt, ba=ba: e.tensor_tensor(out=xt[:, 0:512], in0=xt[:, 0:512], in1=ps[ba], op=ALU.add), r=[bx, PB[ba]], w=[bx])
                        P.dve(lambda e, xt=xt, bb=bb: e.tensor_tensor(out=xt[:, 512:1024], in0=xt[:, 512:1024], in1=ps[bb], op=ALU.add), r=[bx, PB[bb]], w=[bx])
                        if not final:
                            P.dma("pool", out[s, tt * 128:(tt + 1) * 128, :], xt, r=[bx], w=[b_xd[tt]])
                        if dbgname is not None:
                            P.dma("sp", dbg[dbgname][tt * 128:(tt + 1) * 128, :], xt, r=[bx])
                        if final:
                            norm_tile(xt, bx, None, tt, None, None, final_dst=out[s, tt * 128:(tt + 1) * 128, :])
                        elif norm_col is not None:
                            pend2 = norm_tile(xt, bx, norm_col, tt, hT, b_hT, defer=True)
                    if pend2 is not None:
                        pend2()

                src_x = x_in[s] if l == 0 else out[s]
                update_pass(l, s, mgT, wo, b_wo, 8, b_mg, src_x, R_NFFN, False, "d_x1" if dbg_on else None)
                A.release()
                A.release()

                ck(8)
                A.mark()
                actT = A.alloc((11, S), BF16); b_act = bufs(NT, "act")
                w2 = A.alloc((11, D), BF16); b_w2 = Buf("w2")
                wblk = Ring([(A.alloc((8, 256), BF16), Buf(f"wblk{i}")) for i in range(3)])
                sl_ring = Ring([(A.alloc(512, F32), Buf(f"sl{i}")) for i in range(2)])
                for half in range(2):
                    fr = Ring([(0, 2), (1, 3)])
                    pend = []

                    def issue(fc):
                        wt, bwt = wblk.next()
                        gc = half * 1408 + fc * 128
                        P.dma("pool", wt[:, :, 0:128], w_f1[l, :, gc:gc + 128].rearrange("(k p) n -> p k n", p=128), w=[bwt])
                        P.dma("pool", wt[:, :, 128:256], w_f1[l, :, DFF + gc:DFF + gc + 128].rearrange("(k p) n -> p k n", p=128), w=[bwt])
                        pend.append((wt, bwt))
                    issue(0); issue(1)
                    load_w(w2, w_f2[l, half * 1408:(half + 1) * 1408, :], r=(), w=[b_w2])
                    for fc in range(11):
                        if fc + 2 < 11:
                            issue(fc + 2)
                        wt, bwt = pend[fc]
                        for tg in range(4):
                            tsl = slice(tg * 512, (tg + 1) * 512)
                            tb = [4 * tg + i for i in range(4)]
                            gb, ub = fr.next()
                            for kc in range(8):
                                MM(ps[gb], wt[:, kc, 0:128], hT[:, kc, tsl], kc == 0, kc == 7, r=[bwt] + [b_hT[t] for t in tb], w=[PB[gb]])
                            for kc in range(8):
                                MM(ps[ub], wt[:, kc, 128:256], hT[:, kc, tsl], kc == 0, kc == 7, r=[bwt] + [b_hT[t] for t in tb], w=[PB[ub]])
                            sl, bsl = sl_ring.next()
                            P.act(lambda e, sl=sl, gb=gb: e.activation(out=sl, in_=ps[gb], func=AF.Silu), r=[PB[gb]], w=[bsl])
                            P.dve(lambda e, sl=sl, ub=ub, fc=fc, tsl=tsl: e.tensor_tensor(out=actT[:, fc, tsl], in0=sl, in1=ps[ub], op=ALU.mult),
                                  r=[bsl, PB[ub]], w=[b_act[t] for t in tb])
                    last_layer = (l == nlayers - 1)
                    if half == 0:
                        update_pass(l, s, actT, w2, b_w2, 11, b_act, out[s], None, False)
                    else:
                        update_pass(l, s, actT, w2, b_w2, 11, b_act, out[s], None if last_layer else R_NNEXT, last_layer,
                                    "d_x2" if dbg_on else None)
                A.release()
        try:
            body()
        except _Stop:
            pass
        P.emit()
        print("ops/sig/per-engine:", P.stats, "arena peak KiB", A.peak / 1024.0, flush=True)
    return nc


_CACHE = {}


def prep_shared(inputs):
    f = lambda a: np.ascontiguousarray(np.asarray(a), dtype=np.float32)
    w_ukv = f(inputs["mla_w_ukv"]).reshape(L, 256, 8, 128)
    w_ukv_p = np.ascontiguousarray(np.concatenate([w_ukv[..., 0:64].reshape(L, 256, 512), w_ukv[..., 64:128].reshape(L, 256, 512)], axis=-1))
    wsT = np.ascontiguousarray(f(inputs["sg_ws"]).transpose(0, 3, 1, 2))
    rowvec = np.concatenate([f(inputs["norm_mix"]), f(inputs["norm_ffn"]), f(inputs["mla_q_norm"]), f(inputs["mla_kv_norm"]),
                             f(inputs["sg_ln_g"]), f(inputs["sg_ln_b"]), f(inputs["sg_b"]).reshape(L, 512),
                             np.concatenate([f(inputs["norm_mix"])[1:], np.zeros((1, D), np.float32)], axis=0)], axis=1)
    assert rowvec.shape == (L, NROW)
    bg = f(inputs["b_gate"]).reshape(L, 4, 8, 128).transpose(0, 3, 1, 2).reshape(L, 128, 32)
    cw = f(inputs["conv_w"]).reshape(L, 3, 4, 128).transpose(0, 3, 2, 1).reshape(L, 128, 12)
    pvec = np.ascontiguousarray(np.concatenate([bg, cw], axis=2))
    consts, _ = host_consts()
    return {
        "w_in": f(inputs["w_in"]), "w_branch": f(inputs["w_branch"]), "w_out": f(inputs["w_out"]),
        "w_ffn_in": f(inputs["w_ffn_in"]), "w_ffn_out": f(inputs["w_ffn_out"]),
        "w_uq": f(inputs["mla_w_uq"]), "w_ukv": w_ukv_p, "wsT": wsT, "rowvec": np.ascontiguousarray(rowvec),
        "fnorm": f(inputs["final_norm"]).reshape(1, D), "pvec": pvec, "consts": consts,
    }


def kernel(**inputs):
    ncores = 8
    x = np.ascontiguousarray(np.asarray(inputs["x"]), dtype=np.float32)
    pos = np.asarray(inputs["positions"]).astype(np.int32)
    B = x.shape[0]
    nseq = B // ncores
    shared = prep_shared(inputs)
    if "nc" not in _CACHE:
        _CACHE["nc"] = build(nseq=nseq)
    nc = _CACHE["nc"]
    in_maps = []
    for c in range(ncores):
        m = dict(shared)
        m["x"] = np.ascontiguousarray(x[c * nseq:(c + 1) * nseq])
        p = pos[c * nseq:(c + 1) * nseq].reshape(nseq, NT, 128).transpose(0, 2, 1)
        m["pos"] = np.ascontiguousarray(p)
        in_maps.append(m)
    res = run_bass_kernel_spmd(nc, in_maps, core_ids=list(range(ncores)))
    return np.concatenate([r["out"] for r in res.results], axis=0)
```

```python
import math
import contextlib
import numpy as np
import concourse.bass as bass
import concourse.mybir as mybir
from concourse.bass_utils import run_bass_kernel_spmd

dt = mybir.dt
F32, BF16, I32 = dt.float32, dt.bfloat16, dt.int32
AF = mybir.ActivationFunctionType
ALU = mybir.AluOpType

ENGS = ("pe", "act", "dve", "pool", "sp")
NDMASEM = 8


class Buf:
    __slots__ = ("name", "lw", "rd", "excl")

    def __init__(self, name="", excl=False):
        self.name = name
        self.lw = None
        self.rd = []
        self.excl = excl


def bufs(n, name=""):
    return [Buf(f"{name}{i}") for i in range(n)]


class Prog:
    def __init__(self, nc):
        self.nc = nc
        self.ops = []

    def add(self, eng, fn, reads=(), writes=(), dma=False):
        self.ops.append((eng, fn, tuple(reads), tuple(writes), dma))

    def pe(self, fn, r=(), w=()): self.add("pe", fn, r, w)
    def act(self, fn, r=(), w=()): self.add("act", fn, r, w)
    def dve(self, fn, r=(), w=()): self.add("dve", fn, r, w)
    def pool(self, fn, r=(), w=()): self.add("pool", fn, r, w)

    def dma(self, q, out, in_, r=(), w=()):
        self.add(q, lambda e: e.dma_start(out=out, in_=in_), r, w, dma=True)

    def fence(self):
        self.ops.append(("fence", None, (), (), False))

    def emit(self):
        nc = self.nc
        ops = self.ops
        n = len(ops)
        deps = [None] * n
        signaling = [False] * n
        last_eng = {}
        last_slot = {}
        dcount = {}
        fence_deps = []
        for k, (eng, fn, reads, writes, is_dma) in enumerate(ops):
            if eng == "fence":
                fence_deps = list(last_eng.values()) + list(last_slot.values())
                deps[k] = []
                continue
            raw = set()
            oth = set()
            if fence_deps and any((b.lw is None and not b.rd) for b in writes):
                oth.update(fence_deps)
            if is_dma:
                j = dcount.get(eng, 0)
                dcount[eng] = j + 1
                last_slot[(eng, j % NDMASEM)] = k
            else:
                last_eng[eng] = k
            for b in reads:
                if b.lw is not None:
                    raw.add(b.lw)
                if b.excl:
                    oth.update(b.rd)
            for b in writes:
                if b.lw is not None:
                    oth.add(b.lw)
                for r_ in b.rd:
                    oth.add(r_)
            for b in reads:
                b.rd.append(k)
            for b in writes:
                b.lw = k
                b.rd = []
            best = {}
            dl = []
            for p in raw | oth:
                if p == k:
                    continue
                peng, _, _, _, pdma = ops[p]
                if pdma:
                    dl.append(p)
                    continue
                if (not is_dma) and peng == eng:
                    if eng == "pe" or p not in raw:
                        continue
                if peng not in best or best[peng] < p:
                    best[peng] = p
            dl.extend(best.values())
            for p in dl:
                if not ops[p][4]:
                    signaling[p] = True
            deps[k] = dl
        cnt = {e: 0 for e in ENGS}
        sigval = [None] * n
        dmacount = {}
        slotuses = {}
        dma_prev = [None] * n
        for k, (eng, fn, reads, writes, is_dma) in enumerate(ops):
            if eng == "fence":
                continue
            if is_dma:
                j = dmacount.get(eng, 0)
                dmacount[eng] = j + 1
                slot = j % NDMASEM
                u = slotuses.get((eng, slot), 0)
                if u > 0:
                    dma_prev[k] = (("dma", eng, slot), 16 * u)
                slotuses[(eng, slot)] = u + 1
                sigval[k] = (("dma", eng, slot), 16 * (u + 1))
            elif signaling[k]:
                cnt[eng] += 1
                sigval[k] = (("eng", eng), cnt[eng])
        per = {e: [] for e in ENGS}
        seen = {e: {} for e in ENGS}
        for k, (eng, fn, reads, writes, is_dma) in enumerate(ops):
            if eng == "fence":
                continue
            waits = []
            cand = [sigval[p] for p in deps[k]]
            if dma_prev[k] is not None:
                cand.append(dma_prev[k])
            for (sk, v) in cand:
                if seen[eng].get(sk, 0) >= v:
                    continue
                seen[eng][sk] = v
                waits.append((sk, v))
            per[eng].append((waits, fn, sigval[k], is_dma))
        final_waits = [(("dma", q, s), 16 * u) for (q, s), u in slotuses.items()]
        self.stats = (n, sum(signaling), {e: len(per[e]) for e in ENGS})

        stack = contextlib.ExitStack()
        sems = {}
        with stack:
            for e in ENGS:
                sems[("eng", e)] = stack.enter_context(nc.semaphore("s_" + e))
            for q in dmacount:
                for s in range(NDMASEM):
                    sems[("dma", q, s)] = stack.enter_context(nc.semaphore(f"d_{q}_{s}"))
            block = stack.enter_context(nc.Block())

            def run(engname, e):
                for waits, fn, inc, is_dma in per[engname]:
                    for (sk, v) in waits:
                        e.wait_ge(sems[sk], v)
                    ins = fn(e)
                    if inc is not None:
                        ins.then_inc(sems[inc[0]], 16 if is_dma else 1)
                if engname == "sp":
                    for (sk, v) in final_waits:
                        e.wait_ge(sems[sk], v)

            @block.tensor
            def _(e): run("pe", e)

            @block.scalar
            def _(e): run("act", e)

            @block.vector
            def _(e): run("dve", e)

            @block.gpsimd
            def _(e): run("pool", e)

            @block.sync
            def _(e): run("sp", e)


class Arena:
    def __init__(self, t, nbytes, prog=None):
        self.t = t
        self.nbytes = nbytes
        self.off = 0
        self.marks = []
        self.peak = 0
        self.prog = prog

    def alloc(self, shape_free, dtype, parts=128):
        if isinstance(shape_free, int):
            shape_free = (shape_free,)
        esz = mybir.dt.size(dtype)
        nel = int(np.prod(shape_free))
        nb = (nel * esz + 63) // 64 * 64
        assert self.off + nb <= self.nbytes, f"arena overflow {self.off}+{nb}>{self.nbytes}"
        o = self.off
        self.off += nb
        self.peak = max(self.peak, self.off)
        ap = self.t[0:parts, o // 2:(o + nel * esz) // 2]
        if dtype != BF16:
            ap = ap.bitcast(dtype)
        if len(shape_free) > 1:
            names = " ".join(f"d{i}" for i in range(len(shape_free)))
            kw = {f"d{i}": int(s) for i, s in enumerate(shape_free)}
            ap = ap.rearrange(f"p ({names}) -> p {names}", **kw)
        return ap

    def mark(self):
        self.marks.append(self.off)

    def release(self):
        self.off = self.marks.pop()
        if self.prog is not None:
            self.prog.fence()


class Ring:
    def __init__(self, items):
        self.items = items
        self.i = 0

    def next(self):
        it = self.items[self.i % len(self.items)]
        self.i += 1
        return it


L = 2
D = 1024
S = 2048
NT = 16
NIN = 8864
DFF = 2816
C_AB, C_AC, C_AX = 0, 512, 1024
C_RQ, C_RK, C_RV, C_RG = 1536, 1792, 2048, 2560
C_SU, C_SV = 3072, 3584
C_MQ, C_MKV, C_MPE = 4096, 4480, 4736
C_GATE = 4768
EPS = 1e-6
R_NMIX, R_NFFN, R_QN, R_KVN, R_LNG, R_LNB, R_BS = 0, 1024, 2048, 2432, 2688, 3200, 3712
R_NNEXT = 4224
NROW = 5248
K_ID, K_NEG, K_M01, K_DT, K_ZS, K_XI, K_IFR, K_IFM = 0, 128, 256, 384, 896, 900, 1156, 1188
NCONST = 1204
NPV = 44
MAGIC = 12582912.0
TWO_PI = 2.0 * math.pi
C1 = float(np.float32(6.28125))
C2 = float(np.float32(TWO_PI - 6.28125))
PI_LO = 3.1415925


def host_consts():
    c = np.zeros((128, NCONST), np.float64)
    idx = np.arange(128)
    c[:, K_ID:K_ID + 128] = np.eye(128)
    kk, qq = np.meshgrid(idx, idx, indexing="ij")
    c[:, K_NEG:K_NEG + 128] = np.where(kk <= qq, 0.0, -30000.0)
    c[:, K_M01:K_M01 + 128] = np.where(kk <= qq, 1.0, 0.0)
    lg = np.log1p(-np.exp2(-5.0 - np.arange(4, dtype=np.float64)))
    scale = 64 ** -0.5
    for h in range(4):
        diff = (qq - kk).astype(np.float64)
        dtm = np.where(diff >= 0, np.exp(np.maximum(diff, 0) * lg[h]), 0.0) * scale
        c[:, K_DT + h * 128:K_DT + (h + 1) * 128] = dtm
        c[:, K_ZS + h] = np.exp((127 - idx) * lg[h]) * scale
    for cc in range(2):
        for p in range(128):
            h = 2 * cc + p // 64
            c[p, K_XI + cc * 128:K_XI + (cc + 1) * 128] = np.exp((idx + 1.0) * lg[h])
    ifr = (np.float32(10000.0) ** (-(np.arange(0, 64, 2, dtype=np.float32) / np.float32(64)))).astype(np.float32)
    ifm = (np.float32(10000.0) ** (-(np.arange(0, 32, 2, dtype=np.float32) / np.float32(32)))).astype(np.float32)
    c[:, K_IFR:K_IFR + 32] = ifr[None, :]
    c[:, K_IFM:K_IFM + 16] = ifm[None, :]
    cd = [float(np.exp(128 * lg[h])) for h in range(4)]
    return c.astype(np.float32), cd


class _Stop(Exception):
    pass


def build(nseq=2, nlayers=L, debug=False, upto=99):
    nc = bass.Bass("TRN2", target_bir_lowering=False)
    din = lambda name, shape, d=F32: nc.dram_tensor(name, list(shape), d, kind="ExternalInput").ap()
    x_in = din("x", [nseq, S, D])
    pos_in = din("pos", [nseq, 128, NT], I32)
    w_in = din("w_in", [L, D, NIN])
    w_br = din("w_branch", [L, 4, 512, D])
    w_o = din("w_out", [L, D, D])
    w_f1 = din("w_ffn_in", [L, D, 2 * DFF])
    w_f2 = din("w_ffn_out", [L, DFF, D])
    w_uq = din("w_uq", [L, 384, 768])
    w_ukv = din("w_ukv", [L, 256, 1024])
    wsT_in = din("wsT", [L, 128, 4, 128])
    rowv = din("rowvec", [L, NROW])
    fnorm = din("fnorm", [1, D])
    pv_in = din("pvec", [L, 128, NPV])
    cst_in = din("consts", [128, NCONST])
    out = nc.dram_tensor("out", [nseq, S, D], F32, kind="ExternalOutput").ap()
    dbg = {}
    if debug:
        for nm, shp in (("d_hT", [128, 8, S]), ("d_ym", [128, 4, S]), ("d_ya", [128, 4, S]), ("d_yr", [128, 4, S]),
                        ("d_ys", [128, 4, S]), ("d_mg", [128, 8, S])):
            dbg[nm] = nc.dram_tensor(nm, shp, BF16, kind="ExternalOutput").ap()
        dbg["d_x1"] = nc.dram_tensor("d_x1", [S, D], F32, kind="ExternalOutput").ap()
        dbg["d_x2"] = nc.dram_tensor("d_x2", [S, D], F32, kind="ExternalOutput").ap()
        dbg["d_cos"] = nc.dram_tensor("d_cos", [128, NT, 32], F32, kind="ExternalOutput").ap()
        dbg["d_sin"] = nc.dram_tensor("d_sin", [128, NT, 32], F32, kind="ExternalOutput").ap()

    _, CD = host_consts()
    NB = 204 * 1024
    stack = contextlib.ExitStack()
    with stack:
        at = stack.enter_context(nc.sbuf_tensor("arena", [128, NB // 2], BF16))
        pst = stack.enter_context(nc.psum_tensor("ps", [128, 8, 512], F32))
        P = Prog(nc)
        A = Arena(at, NB, P)
        PB = [Buf(f"bank{i}", excl=True) for i in range(8)]
        ps = [pst[:, i, :] for i in range(8)]
        psb = [pst[:, i, :].bitcast(BF16) for i in range(8)]

        def MM(out_ap, lhsT, rhs, start, stop, r, w):
            P.pe(lambda e: e.matmul(out_ap, lhsT=lhsT, rhs=rhs, start=start, stop=stop), r, w)

        def TR(out_ap, in_ap, ident, r, w):
            P.pe(lambda e: e.transpose(out=out_ap, in_=in_ap, identity=ident), r, w)

        cst = A.alloc(NCONST, F32); b_cst = Buf("cst")
        idb = A.alloc(128, BF16); negb = A.alloc(128, BF16); b_cb = Buf("cb")
        rowb = A.alloc(NROW, F32); b_rowb = Buf("rowb")
        fnb = A.alloc(D, F32); b_fnb = Buf("fnb")
        pvt = A.alloc(NPV, F32); b_pv = Buf("pv")
        cos_r = A.alloc((NT, 32), F32); sin_r = A.alloc((NT, 32), F32)
        cos_m = A.alloc((NT, 16), F32); sin_m = A.alloc((NT, 16), F32)
        b_rope = Buf("rope")
        hT = A.alloc((8, S), BF16); b_hT = bufs(NT, "hT")
        yT = A.alloc((4, S), BF16); b_y = bufs(NT, "y")
        b_mg = bufs(NT, "mg")
        MG = {}
        ss = A.alloc(NT, F32); sd = A.alloc(NT, F32); rs = A.alloc(NT, F32)
        b_ss = bufs(NT, "ss"); b_sd = bufs(NT, "sd"); b_rs = bufs(NT, "rs")
        junk = A.alloc(D, BF16); b_junk = Buf("junk")
        nhalf = A.alloc(4, F32); b_nh = Buf("nhalf")
        P.pool(lambda e: e.memset(nhalf, -0.5), w=[b_nh])
        xt_ring = Ring([(A.alloc(D, F32), Buf(f"xt{i}")) for i in range(3)])
        hb_ring = Ring([(A.alloc(D, BF16), Buf(f"hb{i}")) for i in range(2)])
        ident = idb
        b_xd = bufs(NT, "xd")

        P.dma("sp", cst, cst_in, w=[b_cst])
        P.dma("sp", fnb, fnorm[0:1, :].partition_broadcast(128).squeeze(1), w=[b_fnb])
        P.dve(lambda e: e.tensor_copy(out=idb, in_=cst[:, K_ID:K_ID + 128]), r=[b_cst], w=[b_cb])
        P.dve(lambda e: e.tensor_copy(out=negb, in_=cst[:, K_NEG:K_NEG + 128]), r=[b_cst], w=[b_cb])

        tr_ring = Ring([4, 5])

        def load_w(dst, src2d, r=(), w=()):
            P.dma("pool", dst, src2d.rearrange("(k p) n -> p k n", p=128), r=r, w=w)

        def norm_tile(xt, bx, gcol, tt, dstT, b_dst, final_dst=None, defer=False):
            P.act(lambda e: e.activation(out=junk, in_=xt, func=AF.Square, accum_out=ss[:, tt:tt + 1]),
                  r=[bx], w=[b_junk, b_ss[tt]])
            P.act(lambda e: e.activation(out=sd[:, tt:tt + 1], in_=ss[:, tt:tt + 1], func=AF.Sqrt, scale=1.0 / D, bias=EPS),
                  r=[b_ss[tt]], w=[b_sd[tt]])
            P.dve(lambda e: e.reciprocal(out=rs[:, tt:tt + 1], in_=sd[:, tt:tt + 1]), r=[b_sd[tt]], w=[b_rs[tt]])
            if final_dst is not None:
                ot, bo = xt_ring.next()
                P.dve(lambda e: e.scalar_tensor_tensor(out=ot, in0=xt, scalar=rs[:, tt:tt + 1], in1=fnb, op0=ALU.mult, op1=ALU.mult),
                      r=[bx, b_rs[tt], b_fnb], w=[bo])
                P.dma("pool", final_dst, ot, r=[bo], w=[b_xd[tt]])
                return
            hb, bh = hb_ring.next()
            P.dve(lambda e: e.scalar_tensor_tensor(out=hb, in0=xt, scalar=rs[:, tt:tt + 1], in1=rowb[:, gcol:gcol + D],
                                                   op0=ALU.mult, op1=ALU.mult),
                  r=[bx, b_rs[tt], b_rowb], w=[bh])

            def part2():
                bk = tr_ring.next()
                for kc in range(8):
                    TR(psb[bk][:, kc * 128:(kc + 1) * 128], hb[:, kc * 128:(kc + 1) * 128], ident, r=[bh, b_cb], w=[PB[bk]])
                P.act(lambda e: e.copy(out=dstT[:, :, tt * 128:(tt + 1) * 128], in_=psb[bk].rearrange("p (k n) -> p k n", k=8)),
                      r=[PB[bk]], w=[b_dst[tt]])
            if defer:
                return part2
            part2()

        def zero_ss():
            P.dve(lambda e: e.memset(ss, 0.0), w=b_ss)

        def finalize(l, bi, yT, b_y, first, prefetch=None):
            mgT = MG["t"]
            A.mark()
            wb = A.alloc((4, D), BF16); b_wb = Buf("wb")
            wg = A.alloc((8, D), BF16); b_wg = Buf("wg")
            sg_ring = Ring([(A.alloc(512, F32), Buf(f"sg{i}")) for i in range(3)])
            tp_ring = Ring([(A.alloc(512, F32), Buf(f"tp{i}")) for i in range(3)])
            b_wbp = bufs(2, "wbp"); b_wgp = bufs(4, "wgp")
            for pz in range(2):
                load_w(wb[:, :, pz * 512:(pz + 1) * 512], w_br[l, bi][:, pz * 512:(pz + 1) * 512], w=[b_wbp[pz]])
                for pq in range(2):
                    g0 = (pz * 2 + pq) * 256
                    load_w(wg[:, :, g0:g0 + 256], w_in[l, :, C_GATE + bi * D + g0:C_GATE + bi * D + g0 + 256], w=[b_wgp[pz * 2 + pq]])
            if prefetch is not None:
                prefetch()
            zr = Ring([0, 1, 4, 5]); gr = Ring([2, 3, 6, 7])
            for oc in range(8):
                for tg in range(4):
                    tsl = slice(tg * 512, (tg + 1) * 512)
                    tb = [4 * tg + i for i in range(4)]
                    zb = zr.next(); gb = gr.next()
                    for c in range(4):
                        MM(ps[zb], wb[:, c, oc * 128:(oc + 1) * 128], yT[:, c, tsl], c == 0, c == 3,
                           r=[b_wbp[oc // 4]] + [b_y[t] for t in tb], w=[PB[zb]])
                    for kc in range(8):
                        MM(ps[gb], wg[:, kc, oc * 128:(oc + 1) * 128], hT[:, kc, tsl], kc == 0, kc == 7,
                           r=[b_wgp[oc // 2]] + [b_hT[t] for t in tb], w=[PB[gb]])
                    sg, bsg = sg_ring.next()
                    P.act(lambda e, sg=sg, gb=gb, oc=oc: e.activation(out=sg, in_=ps[gb], func=AF.Sigmoid,
                                                                     bias=pvt[:, bi * 8 + oc:bi * 8 + oc + 1]),
                          r=[PB[gb], b_pv], w=[bsg])
                    if first:
                        P.dve(lambda e, sg=sg, zb=zb, oc=oc, tsl=tsl: e.tensor_tensor(out=mgT[:, oc, tsl], in0=ps[zb], in1=sg, op=ALU.mult),
                              r=[PB[zb], bsg], w=[b_mg[t] for t in tb])
                    else:
                        tp, btp = tp_ring.next()
                        P.dve(lambda e, sg=sg, zb=zb, tp=tp: e.tensor_tensor(out=tp, in0=ps[zb], in1=sg, op=ALU.mult),
                              r=[PB[zb], bsg], w=[btp])
                        P.pool(lambda e, tp=tp, oc=oc, tsl=tsl: e.tensor_tensor(out=mgT[:, oc, tsl], in0=mgT[:, oc, tsl], in1=tp, op=ALU.add),
                               r=[btp] + [b_mg[t] for t in tb], w=[b_mg[t] for t in tb])
            A.release()

        def dump(name, ap, rb):
            if debug and name in dbg:
                P.dma("sp", dbg[name], ap, r=rb)

        def rope_tables(s):
            A.mark()
            posi = A.alloc(NT, I32); b_pi = Buf("posi")
            posf = A.alloc(NT, F32); b_pf = Buf("posf")
            P.dma("sp", posi, pos_in[s], w=[b_pi])
            P.dve(lambda e: e.tensor_copy(out=posf, in_=posi), r=[b_pi], w=[b_pf])
            for (nf, koff, ctab, stab) in ((32, K_IFR, cos_r, sin_r), (16, K_IFM, cos_m, sin_m)):
                ang = A.alloc((NT, nf), F32); t1 = A.alloc((NT, nf), F32); t2 = A.alloc((NT, nf), F32)
                b_a = Buf("ang"); b_t1 = Buf("t1"); b_t2 = Buf("t2")
                P.dve(lambda e, ang=ang, nf=nf, koff=koff: e.tensor_tensor(
                    out=ang, in0=posf.unsqueeze(2).broadcast_to([128, NT, nf]),
                    in1=cst[:, koff:koff + nf].unsqueeze(1).broadcast_to([128, NT, nf]), op=ALU.mult),
                    r=[b_pf, b_cst], w=[b_a])
                P.dve(lambda e, ang=ang, t1=t1: e.tensor_scalar(out=t1, in0=ang, scalar1=1.0 / TWO_PI, scalar2=MAGIC, op0=ALU.mult, op1=ALU.add),
                      r=[b_a], w=[b_t1])
                P.dve(lambda e, t1=t1, t2=t2: e.tensor_scalar(out=t2, in0=t1, scalar1=-MAGIC, scalar2=None, op0=ALU.add),
                      r=[b_t1], w=[b_t2])
                P.dve(lambda e, t1=t1, t2=t2, ang=ang: e.scalar_tensor_tensor(out=t1, in0=t2, scalar=-C1, in1=ang, op0=ALU.mult, op1=ALU.add),
                      r=[b_t2, b_a], w=[b_t1])
                P.dve(lambda e, t1=t1, t2=t2, ang=ang: e.scalar_tensor_tensor(out=ang, in0=t2, scalar=-C2, in1=t1, op0=ALU.mult, op1=ALU.add),
                      r=[b_t2, b_t1], w=[b_a])
                P.dve(lambda e, ang=ang: e.tensor_scalar(out=ang, in0=ang, scalar1=-PI_LO, scalar2=PI_LO, op0=ALU.max, op1=ALU.min),
                      r=[b_a], w=[b_a])
                P.act(lambda e, ang=ang, stab=stab: e.activation(out=stab, in_=ang, func=AF.Sin), r=[b_a], w=[b_rope])
                P.act(lambda e, ang=ang, t1=t1: e.activation(out=t1, in_=ang, func=AF.Abs), r=[b_a], w=[b_t1])
                P.act(lambda e, t1=t1, ctab=ctab: e.activation(out=ctab, in_=t1, func=AF.Sin, scale=-1.0, bias=math.pi / 2),
                      r=[b_t1], w=[b_rope])
            A.release()
            if debug and s == 0:
                dump("d_cos", cos_r, [b_rope]); dump("d_sin", sin_r, [b_rope])

        def rope_apply(dst1, dst2, x1, x2, ct, st, shape, tmp, r, w):
            ta, tb_ = tmp
            b_ta = Buf("ta"); b_tb = Buf("tb")
            P.dve(lambda e: e.tensor_tensor(out=ta, in0=x1, in1=ct, op=ALU.mult), r=r, w=[b_ta])
            P.dve(lambda e: e.tensor_tensor(out=tb_, in0=x2, in1=st, op=ALU.mult), r=r, w=[b_tb])
            P.dve(lambda e: e.tensor_tensor(out=dst1, in0=ta, in1=tb_, op=ALU.subtract), r=[b_ta, b_tb], w=w)
            P.dve(lambda e: e.tensor_tensor(out=ta, in0=x1, in1=st, op=ALU.mult), r=r, w=[b_ta])
            P.dve(lambda e: e.tensor_tensor(out=tb_, in0=x2, in1=ct, op=ALU.mult), r=r, w=[b_tb])
            P.dve(lambda e: e.tensor_tensor(out=dst2, in0=ta, in1=tb_, op=ALU.add), r=[b_ta, b_tb], w=w)

        def ck(n):
            if upto < n:
                raise _Stop()

        def body():
          for s in range(nseq):
            rope_tables(s)
            ck(0)
            for l in range(nlayers):
                dbg_on = debug and s == 0 and l == 0
                P.dma("sp", rowb, rowv[l:l + 1, :].partition_broadcast(128).squeeze(1), w=[b_rowb])
                P.dma("sp", pvt, pv_in[l], w=[b_pv])
                def m0_tile(tt, defer=False):
                    xt, bx = xt_ring.next()
                    P.dma("sp", xt, x_in[s, tt * 128:(tt + 1) * 128, :], w=[bx])
                    return norm_tile(xt, bx, R_NMIX, tt, hT, b_hT, defer=defer)
                if l == 0:
                    zero_ss()

                ck(1)
                A.mark()
                qnT = A.alloc((3, S), BF16); b_qnT = bufs(NT, "qnT")
                KT = A.alloc((8, S), BF16); b_KT = bufs(NT, "KT")
                VP = A.alloc((NT, 8, 65), BF16); b_VP = bufs(NT, "VP")
                ymT = yT; b_ym = b_y
                A.mark()
                wm = A.alloc((8, 672), BF16); b_wm = Buf("wm")
                wkv = A.alloc((2, 1024), BF16); b_wkv = Buf("wkv")
                kvnT = A.alloc((2, S), BF16); b_kvnT = bufs(NT, "kvnT")
                qn_ring = Ring([(A.alloc(384, BF16), Buf(f"qn{i}")) for i in range(2)])
                kvn_ring = Ring([(A.alloc(256, BF16), Buf(f"kvn{i}")) for i in range(2)])
                kpe_ring = Ring([(A.alloc(96, BF16), Buf(f"kpe{i}")) for i in range(2)])
                st8 = A.alloc((NT, 8), F32); b_st8 = bufs(NT, "st8")
                rtmp = (A.alloc(16, F32), A.alloc(16, F32))
                load_w(wm, w_in[l, :, C_MQ:C_MQ + 672], w=[b_wm])
                load_w(wkv, w_ukv[l], w=[b_wkv])
                for (kp, bkp) in kpe_ring.items:
                    P.dve(lambda e, kp=kp: e.memset(kp, 0.0), w=[bkp])
                P.dve(lambda e: e.memset(st8, 0.0), w=b_st8)
                P.pool(lambda e: e.memset(VP, 1.0), w=b_VP)
                pr_ring = Ring([(0, 1), (2, 3)])
                MPB = {}

                def mp_A(tt):
                    tsl = slice(tt * 128, (tt + 1) * 128)
                    ba, bb = pr_ring.next()
                    for kc in range(8):
                        MM(ps[ba], hT[:, kc, tsl], wm[:, kc, 0:512], kc == 0, kc == 7, r=[b_hT[tt], b_wm], w=[PB[ba]])
                    for kc in range(8):
                        MM(ps[bb][:, 0:160], hT[:, kc, tsl], wm[:, kc, 512:672], kc == 0, kc == 7, r=[b_hT[tt], b_wm], w=[PB[bb]])
                    c0 = st8[:, tt, 0:1]; c1 = st8[:, tt, 1:2]; c2 = st8[:, tt, 2:3]
                    P.act(lambda e, ba=ba, c0=c0: e.activation(out=junk[:, 0:384], in_=ps[ba][:, 0:384], func=AF.Square, accum_out=c0),
                          r=[PB[ba]], w=[b_junk, b_st8[tt]])
                    P.act(lambda e, ba=ba, c1=c1: e.activation(out=junk[:, 0:128], in_=ps[ba][:, 384:512], func=AF.Square, accum_out=c1),
                          r=[PB[ba]], w=[b_junk, b_st8[tt]])
                    P.act(lambda e, bb=bb, c2=c2: e.activation(out=junk[:, 0:128], in_=ps[bb][:, 0:128], func=AF.Square, accum_out=c2),
                          r=[PB[bb]], w=[b_junk, b_st8[tt]])
                    MPB[tt] = (ba, bb)

                def mp_D(tt):
                    P.dve(lambda e, tt=tt: e.tensor_tensor(out=st8[:, tt, 3:4], in0=st8[:, tt, 1:2], in1=st8[:, tt, 2:3], op=ALU.add),
                          r=[b_st8[tt]], w=[b_st8[tt]])

                def mp_B(tt):
                    ba, bb = MPB.pop(tt)
                    P.act(lambda e, tt=tt: e.activation(out=st8[:, tt, 4:5], in_=st8[:, tt, 0:1], func=AF.Sqrt, scale=1.0 / 384, bias=EPS),
                          r=[b_st8[tt]], w=[b_st8[tt]])
                    P.act(lambda e, tt=tt: e.activation(out=st8[:, tt, 5:6], in_=st8[:, tt, 3:4], func=AF.Sqrt, scale=1.0 / 256, bias=EPS),
                          r=[b_st8[tt]], w=[b_st8[tt]])
                    P.dve(lambda e, tt=tt: e.reciprocal(out=st8[:, tt, 6:8], in_=st8[:, tt, 4:6]), r=[b_st8[tt]], w=[b_st8[tt]])
                    qn, bqn = qn_ring.next(); kvn, bkvn = kvn_ring.next(); kp, bkp = kpe_ring.next()
                    P.dve(lambda e, qn=qn, ba=ba, tt=tt: e.scalar_tensor_tensor(
                        out=qn, in0=ps[ba][:, 0:384], scalar=st8[:, tt, 6:7], in1=rowb[:, R_QN:R_QN + 384], op0=ALU.mult, op1=ALU.mult),
                        r=[PB[ba], b_st8[tt], b_rowb], w=[bqn])
                    P.dve(lambda e, kvn=kvn, ba=ba, tt=tt: e.scalar_tensor_tensor(
                        out=kvn[:, 0:128], in0=ps[ba][:, 384:512], scalar=st8[:, tt, 7:8], in1=rowb[:, R_KVN:R_KVN + 128], op0=ALU.mult, op1=ALU.mult),
                        r=[PB[ba], b_st8[tt], b_rowb], w=[bkvn])
                    P.dve(lambda e, kvn=kvn, bb=bb, tt=tt: e.scalar_tensor_tensor(
                        out=kvn[:, 128:256], in0=ps[bb][:, 0:128], scalar=st8[:, tt, 7:8], in1=rowb[:, R_KVN + 128:R_KVN + 256], op0=ALU.mult, op1=ALU.mult),
                        r=[PB[bb], b_st8[tt], b_rowb], w=[bkvn])
                    rope_apply(kp[:, 64:80], kp[:, 80:96], ps[bb][:, 128:144], ps[bb][:, 144:160],
                               cos_m[:, tt, :], sin_m[:, tt, :], None, rtmp, r=[PB[bb], b_rope], w=[bkp])
                    return qn, bqn, kvn, bkvn, kp, bkp

                def mp_P2(tt, qn, bqn, kvn, bkvn, kp, bkp):
                    tsl = slice(tt * 128, (tt + 1) * 128)
                    bk = tr_ring.next()
                    for c in range(3):
                        TR(psb[bk][:, c * 128:(c + 1) * 128], qn[:, c * 128:(c + 1) * 128], ident, r=[bqn, b_cb], w=[PB[bk]])
                    for c in range(2):
                        TR(psb[bk][:, (3 + c) * 128:(4 + c) * 128], kvn[:, c * 128:(c + 1) * 128], ident, r=[bkvn, b_cb], w=[PB[bk]])
                    TR(psb[bk][0:96, 640:768], kp, ident, r=[bkp, b_cb], w=[PB[bk]])
                    P.act(lambda e, bk=bk, tsl=tsl: e.copy(out=qnT[:, :, tsl], in_=psb[bk][:, 0:384].rearrange("p (k n) -> p k n", k=3)),
                          r=[PB[bk]], w=[b_qnT[tt]])
                    P.act(lambda e, bk=bk, tsl=tsl: e.copy(out=kvnT[:, :, tsl], in_=psb[bk][:, 384:640].rearrange("p (k n) -> p k n", k=2)),
                          r=[PB[bk]], w=[b_kvnT[tt]])
                    P.dve(lambda e, bk=bk, tsl=tsl: e.tensor_copy(out=KT[64:96, :, tsl],
                                                                  in_=psb[bk][64:96, 640:768].unsqueeze(1).broadcast_to([32, 8, 128])),
                          r=[PB[bk]], w=[b_KT[tt]])
                    vb = 6 + (tt % 2)
                    for kc in range(2):
                        MM(ps[vb], kvnT[:, kc, tsl], wkv[:, kc, 512:1024], kc == 0, kc == 1, r=[b_kvnT[tt], b_wkv], w=[PB[vb]])
                    P.act(lambda e, vb=vb, tt=tt: e.copy(out=VP[:, tt, :, 0:64], in_=ps[vb].rearrange("p (h d) -> p h d", h=8)),
                          r=[PB[vb]], w=[b_VP[tt]])

                m0p = None
                if l == 0:
                    m0_tile(0)
                    m0_tile(1)
                    m0p = m0_tile(2, defer=True)
                mres = {}
                for k in range(NT + 2):
                    m0n = None
                    if l == 0 and k + 3 < NT:
                        m0n = m0_tile(k + 3, defer=True)
                    if k < NT:
                        mp_A(k)
                    if m0p is not None:
                        m0p()
                    m0p = m0n
                    if 0 <= k - 1 < NT:
                        mres[k - 1] = mp_B(k - 1)
                    if 0 <= k - 2 < NT:
                        mp_P2(k - 2, *mres.pop(k - 2))
                    if k < NT:
                        mp_D(k)
                if dbg_on:
                    dump("d_hT", hT, b_hT)
                kr = Ring([0, 1, 2, 3])
                for h in range(8):
                    for tg in range(4):
                        tsl = slice(tg * 512, (tg + 1) * 512)
                        tb = [4 * tg + i for i in range(4)]
                        kb = kr.next()
                        for kc in range(2):
                            MM(ps[kb][0:64, :], wkv[:, kc, h * 64:(h + 1) * 64], kvnT[:, kc, tsl], kc == 0, kc == 1,
                               r=[b_wkv] + [b_kvnT[t] for t in tb], w=[PB[kb]])
                        if (h * 4 + tg) % 2 == 0:
                            P.act(lambda e, kb=kb, h=h, tsl=tsl: e.copy(out=KT[0:64, h, tsl], in_=ps[kb][0:64, :]),
                                  r=[PB[kb]], w=[b_KT[t] for t in tb])
                        else:
                            P.dve(lambda e, kb=kb, h=h, tsl=tsl: e.tensor_copy(out=KT[0:64, h, tsl], in_=ps[kb][0:64, :]),
                                  r=[PB[kb]], w=[b_KT[t] for t in tb])
                A.release()
                ck(2)
                A.mark()
                wq = A.alloc((3, 768), BF16); b_wq = Buf("wq")
                load_w(wq, w_uq[l], w=[b_wq])
                qf_ring = Ring([(A.alloc((8, 96), BF16), Buf(f"qf{i}")) for i in range(3)])
                qi_ring = Ring([(A.alloc((8, 128), BF16), Buf(f"qi{i}")) for i in range(4)])
                p_ring = Ring([(A.alloc(512, BF16), Buf(f"p{i}")) for i in range(4)])
                yt_ring = Ring([(A.alloc(512, BF16), Buf(f"yt{i}")) for i in range(2)])
                rc_ring = Ring([(A.alloc(8, F32), Buf(f"rc{i}")) for i in range(2)])
                rtq = (A.alloc((8, 16), F32), A.alloc((8, 16), F32))
                qpe = A.alloc((8, 32), F32); bqpe = Buf("qpe")
                sc_ring = Ring([3, 4, 5])
                ASCALE = 96 ** -0.5
                def q_stage(i):
                    tsl = slice(i * 128, (i + 1) * 128)
                    qa, qb = (0, 1) if i % 2 == 0 else (6, 7)
                    for kc in range(3):
                        MM(ps[qa], qnT[:, kc, tsl], wq[:, kc, 0:512], kc == 0, kc == 2, r=[b_qnT[i], b_wq], w=[PB[qa]])
                    for kc in range(3):
                        MM(ps[qb][:, 0:256], qnT[:, kc, tsl], wq[:, kc, 512:768], kc == 0, kc == 2, r=[b_qnT[i], b_wq], w=[PB[qb]])
                    qf, bqf = qf_ring.next()
                    qff = qf.rearrange("p h d -> p (h d)")
                    P.dve(lambda e: e.tensor_copy(out=qff[:, 0:512], in_=ps[qa]), r=[PB[qa]], w=[bqf])
                    P.dve(lambda e: e.tensor_copy(out=qff[:, 512:768], in_=ps[qb][:, 0:256]), r=[PB[qb]], w=[bqf])
                    cb = cos_m[:, i, :].unsqueeze(1).broadcast_to([128, 8, 16])
                    sb_ = sin_m[:, i, :].unsqueeze(1).broadcast_to([128, 8, 16])
                    P.dve(lambda e: e.tensor_copy(out=qpe, in_=qf[:, :, 64:96]), r=[bqf], w=[bqpe])
                    rope_apply(qf[:, :, 64:80], qf[:, :, 80:96], qpe[:, :, 0:16], qpe[:, :, 16:32], cb, sb_, None, rtq,
                               r=[bqpe, b_rope], w=[bqf])
                    for h in range(8):
                        TR(psb[2][0:96, h * 128:(h + 1) * 128], qf[:, h, :], ident, r=[bqf, b_cb], w=[PB[2]])
                    qi, bqi = qi_ring.next()
                    P.dve(lambda e: e.tensor_copy(out=qi[0:96], in_=psb[2][0:96, :].rearrange("p (h n) -> p h n", h=8)),
                          r=[PB[2]], w=[bqi])
                    return qi, bqi

                def qk_stage(i, qi, bqi, h, js):
                    sb2 = sc_ring.next()
                    for jl, j in enumerate(js):
                        diag = (j == i)
                        MM(ps[sb2][:, jl * 128:(jl + 1) * 128], KT[0:96, h, j * 128:(j + 1) * 128], qi[0:96, h, :],
                           True, not diag, r=[b_KT[j], bqi], w=[PB[sb2]])
                        if diag:
                            MM(ps[sb2][:, jl * 128:(jl + 1) * 128], ident, negb, False, True, r=[b_cb], w=[PB[sb2]])
                    pt, bpt = p_ring.next()
                    nj = len(js)
                    P.act(lambda e: e.activation(out=pt[:, 0:nj * 128], in_=ps[sb2][:, 0:nj * 128], func=AF.Exp, scale=ASCALE),
                          r=[PB[sb2]], w=[bpt])
                    return pt, bpt

                def pv_stage(i, h, js, pt, bpt):
                    ob = (6 if i % 2 == 0 else 0) + h // 4
                    ocol = (h % 4) * 65
                    for jl, j in enumerate(js):
                        MM(ps[ob][:, ocol:ocol + 65], pt[:, jl * 128:(jl + 1) * 128], VP[:, j, h, :], j == 0, j == i,
                           r=[bpt, b_VP[j]], w=[PB[ob]])

                def epilogue(i):
                    tsl = slice(i * 128, (i + 1) * 128)
                    rc, brc = rc_ring.next()
                    yt, byt = yt_ring.next()
                    for hb2 in range(2):
                        ob = (6 if i % 2 == 0 else 0) + hb2
                        pv4 = ps[ob][:, 0:260].rearrange("p (h d) -> p h d", h=4)
                        P.dve(lambda e, pv4=pv4, hb2=hb2: e.reciprocal(out=rc[:, hb2 * 4:(hb2 + 1) * 4], in_=pv4[:, :, 64]),
                              r=[PB[ob]], w=[brc])
                        P.dve(lambda e, pv4=pv4, hb2=hb2: e.tensor_tensor(
                            out=yt[:, hb2 * 256:(hb2 + 1) * 256].rearrange("p (h d) -> p h d", h=4), in0=pv4[:, :, 0:64],
                            in1=rc[:, hb2 * 4:(hb2 + 1) * 4].unsqueeze(2).broadcast_to([128, 4, 64]), op=ALU.mult),
                            r=[PB[ob], brc], w=[byt])
                    for c in range(4):
                        TR(psb[2][:, c * 128:(c + 1) * 128], yt[:, c * 128:(c + 1) * 128], ident, r=[byt, b_cb], w=[PB[2]])
                    P.dve(lambda e: e.tensor_copy(out=ymT[:, :, tsl], in_=psb[2][:, 0:512].rearrange("p (k n) -> p k n", k=4)),
                          r=[PB[2]], w=[b_ym[i]])

                LOOK = 2
                qs = {0: q_stage(0), 1: q_stage(1)}
                for i in range(NT):
                    qi, bqi = qs.pop(i)
                    chunks = [(h, list(range(j0, min(j0 + 4, i + 1)))) for h in range(8) for j0 in range(0, i + 1, 4)]
                    pend = []
                    for (h, js) in chunks:
                        pt, bpt = qk_stage(i, qi, bqi, h, js)
                        pend.append((h, js, pt, bpt))
                        if len(pend) > LOOK:
                            pv_stage(i, *pend.pop(0))
                    if i + 2 < NT:
                        qs[i + 2] = q_stage(i + 2)
                    while pend:
                        pv_stage(i, *pend.pop(0))
                    epilogue(i)
                A.release()
                if dbg_on:
                    dump("d_ym", ymT, b_ym)
                A.release()
                A.mark()
                mgT = A.alloc((8, S), BF16)
                MG["t"] = mgT
                ck(3)
                A.mark()
                wa = A.alloc((8, 1536), BF16); b_wa = Buf("wa")
                finalize(l, 3, ymT, b_ym, first=True, prefetch=lambda: load_w(wa, w_in[l, :, 0:1536], w=[b_wa]))

                ck(4)
                A.mark()
                yaT = yT; b_ya = b_y
                pb_ring = Ring([(A.alloc(S + 2, F32), Buf(f"pbuf{i}")) for i in range(2)])
                ab_ring = Ring([(A.alloc(S, F32), Buf(f"abuf{i}")) for i in range(2)])
                cacc = A.alloc(S, F32); b_cacc = Buf("cacc")
                axr = Ring([(A.alloc(512, F32), Buf(f"ax{i}")) for i in range(2)])
                for (pb_, bpb_) in pb_ring.items:
                    P.dve(lambda e, pb_=pb_: e.memset(pb_[:, 0:2], 0.0), w=[bpb_])
                cr = Ring([(0, 1, 2), (3, 4, 5)])
                for c in range(4):
                    pbuf, b_pbuf = pb_ring.next()
                    abuf, b_abuf = ab_ring.next()
                    for tg in range(4):
                        tsl = slice(tg * 512, (tg + 1) * 512)
                        tb = [4 * tg + i for i in range(4)]
                        b0, b1, b2 = cr.next()
                        for (bk, coff) in ((b0, C_AB), (b1, C_AC), (b2, C_AX)):
                            for kc in range(8):
                                MM(ps[bk], wa[:, kc, coff + c * 128:coff + (c + 1) * 128], hT[:, kc, tsl], kc == 0, kc == 7,
                                   r=[b_wa] + [b_hT[t] for t in tb], w=[PB[bk]])
                        ax, bax = axr.next()
                        P.act(lambda e, ax=ax, b2=b2: e.copy(out=ax, in_=ps[b2]), r=[PB[b2]], w=[bax])
                        P.act(lambda e, b0=b0, tsl=tsl, abuf=abuf: e.copy(out=abuf[:, tsl], in_=ps[b0]), r=[PB[b0]], w=[b_abuf])
                        P.dve(lambda e, ax=ax, b1=b1, tg=tg, pbuf=pbuf: e.tensor_tensor(out=pbuf[:, 2 + tg * 512:2 + (tg + 1) * 512], in0=ps[b1], in1=ax, op=ALU.mult),
                              r=[PB[b1], bax], w=[b_pbuf])
                    w0 = pvt[:, 32 + c * 3 + 0:32 + c * 3 + 1]; w1 = pvt[:, 32 + c * 3 + 1:32 + c * 3 + 2]; w2 = pvt[:, 32 + c * 3 + 2:32 + c * 3 + 3]
                    P.dve(lambda e, w0=w0, pbuf=pbuf: e.tensor_scalar(out=cacc, in0=pbuf[:, 0:S], scalar1=w0, scalar2=None, op0=ALU.mult),
                          r=[b_pbuf, b_pv], w=[b_cacc])
                    P.dve(lambda e, w1=w1, pbuf=pbuf: e.scalar_tensor_tensor(out=cacc, in0=pbuf[:, 1:S + 1], scalar=w1, in1=cacc, op0=ALU.mult, op1=ALU.add),
                          r=[b_pbuf, b_pv, b_cacc], w=[b_cacc])
                    P.dve(lambda e, w2=w2, pbuf=pbuf: e.scalar_tensor_tensor(out=cacc, in0=pbuf[:, 2:S + 2], scalar=w2, in1=cacc, op0=ALU.mult, op1=ALU.add),
                          r=[b_pbuf, b_pv, b_cacc], w=[b_cacc])
                    P.pool(lambda e, c=c, abuf=abuf: e.tensor_tensor(out=yaT[:, c, :], in0=cacc, in1=abuf, op=ALU.mult),
                           r=[b_cacc, b_abuf], w=b_ya)
                if dbg_on:
                    dump("d_ya", yaT, b_ya)
                A.release()
                A.release()
                A.mark()
                wr = A.alloc((8, 1536), BF16); b_wr = Buf("wr")
                finalize(l, 0, yaT, b_ya, first=False, prefetch=lambda: load_w(wr, w_in[l, :, C_RQ:C_RQ + 1536], w=[b_wr]))
                ck(4.5)

                ck(5)
                yrT = yT; b_yr = b_y
                A.mark()
                Sst = A.alloc((2, 128), F32); b_S = Buf("S")
                Sbf = A.alloc((2, 128), BF16); b_Sbf = Buf("Sbf")
                P.dve(lambda e: e.memset(Sst, 0.0), w=[b_S])
                P.dve(lambda e: e.memset(Sbf, 0.0), w=[b_Sbf])
                qk_ring = Ring([(A.alloc(512, BF16), Buf(f"qk{i}")) for i in range(3)])
                kz_ring = Ring([(A.alloc(256, BF16), Buf(f"kz{i}")) for i in range(5)])
                vb_ring = Ring([(A.alloc(512, BF16), Buf(f"vb{i}")) for i in range(5)])
                sgr_ring = Ring([(A.alloc(512, BF16), Buf(f"sgr{i}")) for i in range(5)])
                qT_ring = Ring([(A.alloc((2, 2, 128), BF16), Buf(f"qTz{i}")) for i in range(3)])
                kT_ring = Ring([(A.alloc((2, 128), BF16), Buf(f"kT{i}")) for i in range(3)])
                qx_ring = Ring([(A.alloc((2, 2, 128), BF16), Buf(f"qx{i}")) for i in range(4)])
                for (qz, bqz) in qT_ring.items:
                    P.dve(lambda e, qz=qz: e.memset(qz, 0.0), w=[bqz])
                at_ring = Ring([(A.alloc((4, 128), BF16), Buf(f"at{i}")) for i in range(3)])
                on_ring = Ring([(A.alloc(512, F32), Buf(f"on{i}")) for i in range(2)])
                yr_ring = Ring([(A.alloc(512, BF16), Buf(f"yrt{i}")) for i in range(3)])
                bst = A.alloc((4, 6), F32); b_bst = Buf("bst")
                mv = A.alloc((4, 2), F32); b_mv = Buf("mv")
                rsd = A.alloc(12, F32); b_rsd = Buf("rsd")
                rtr = (A.alloc((8, 32), F32), A.alloc((8, 32), F32))
                RST = {}

                def ret_A1(tt):
                    tsl = slice(tt * 128, (tt + 1) * 128)
                    for (bk, coff) in ((0, 0), (1, 512), (2, 1024)):
                        for kc in range(8):
                            MM(ps[bk], hT[:, kc, tsl], wr[:, kc, coff:coff + 512], kc == 0, kc == 7, r=[b_hT[tt], b_wr], w=[PB[bk]])
                    qk, bqk = qk_ring.next()
                    src = ps[0].rearrange("p (h d) -> p h d", h=8)
                    dst = qk.rearrange("p (h d) -> p h d", h=8)
                    cb = cos_r[:, tt, :].unsqueeze(1).broadcast_to([128, 8, 32])
                    sb_ = sin_r[:, tt, :].unsqueeze(1).broadcast_to([128, 8, 32])
                    rope_apply(dst[:, :, 0:32], dst[:, :, 32:64], src[:, :, 0:32], src[:, :, 32:64], cb, sb_, None, rtr,
                               r=[PB[0], b_rope], w=[bqk])
                    kz, bkz = kz_ring.next()
                    P.pool(lambda e: e.tensor_tensor(
                        out=kz.rearrange("p (h d) -> p h d", h=4), in0=qk[:, 256:512].rearrange("p (h d) -> p h d", h=4),
                        in1=cst[:, K_ZS:K_ZS + 4].unsqueeze(2).broadcast_to([128, 4, 64]), op=ALU.mult),
                        r=[bqk, b_cst], w=[bkz])
                    vbt, bvb = vb_ring.next()
                    P.act(lambda e: e.copy(out=vbt, in_=ps[1]), r=[PB[1]], w=[bvb])
                    sgr, bsgr = sgr_ring.next()
                    P.act(lambda e: e.activation(out=sgr, in_=ps[2], func=AF.Silu), r=[PB[2]], w=[bsgr])
                    RST[tt] = dict(qk=qk, bqk=bqk, kz=kz, bkz=bkz, vbt=vbt, bvb=bvb, sgr=sgr, bsgr=bsgr)

                def ret_A3(tt):
                    d = RST[tt]
                    qk, bqk = d["qk"], d["bqk"]
                    for c in range(4):
                        TR(psb[3][:, c * 128:(c + 1) * 128], qk[:, c * 128:(c + 1) * 128], ident, r=[bqk, b_cb], w=[PB[3]])
                    qTz, bqT = qT_ring.next()
                    kT, bkT = kT_ring.next()
                    P.act(lambda e: e.copy(out=qTz[0:64, :, 0, :], in_=psb[3][0:64, 0:256].rearrange("p (c n) -> p c n", c=2)),
                          r=[PB[3]], w=[bqT])
                    P.act(lambda e: e.copy(out=qTz[64:128, :, 1, :], in_=psb[3][64:128, 0:256].rearrange("p (c n) -> p c n", c=2)),
                          r=[PB[3]], w=[bqT])
                    P.act(lambda e: e.copy(out=kT, in_=psb[3][:, 256:512].rearrange("p (c n) -> p c n", c=2)),
                          r=[PB[3]], w=[bkT])
                    qx, bqx = qx_ring.next()
                    P.pool(lambda e: e.tensor_tensor(
                        out=qx, in0=qTz,
                        in1=cst[:, K_XI:K_XI + 256].rearrange("p (c n) -> p c n", c=2).unsqueeze(2).broadcast_to([128, 2, 2, 128]), op=ALU.mult),
                        r=[bqT, b_cst], w=[bqx])
                    d.update(qTz=qTz, bqT=bqT, kT=kT, bkT=bkT, qx=qx, bqx=bqx)

                def ret_B1(tt):
                    d = RST[tt]
                    qTz, bqT, kT, bkT = d["qTz"], d["bqT"], d["kT"], d["bkT"]
                    for h in range(4):
                        c = h // 2; hf = h % 2
                        MM(ps[4][:, h * 128:(h + 1) * 128], kT[:, c, :], qTz[:, c, hf, :], True, True, r=[bkT, bqT], w=[PB[4]])
                    att, bat = at_ring.next()
                    P.dve(lambda e: e.tensor_tensor(out=att.rearrange("p h n -> p (h n)"), in0=ps[4], in1=cst[:, K_DT:K_DT + 512], op=ALU.mult),
                          r=[PB[4], b_cst], w=[bat])
                    d.update(att=att, bat=bat)

                def ret_B2(tt):
                    d = RST[tt]
                    kz, bkz, vbt, bvb, sgr, bsgr = d["kz"], d["bkz"], d["vbt"], d["bvb"], d["sgr"], d["bsgr"]
                    qx, bqx, att, bat = d["qx"], d["bqx"], d["att"], d["bat"]
                    for h in range(4):
                        c = h // 2
                        MM(ps[5][:, h * 128:(h + 1) * 128], att[:, h, :], vbt[:, h * 128:(h + 1) * 128], True, False, r=[bat, bvb], w=[PB[5]])
                        MM(ps[5][:, h * 128:(h + 1) * 128], qx[:, c, h % 2, :], Sbf[:, c, :], False, True, r=[bqx, b_Sbf], w=[PB[5]])
                    for c in range(2):
                        MM(ps[6 + c], kz[:, c * 128:(c + 1) * 128], vbt, True, True, r=[bkz, bvb], w=[PB[6 + c]])
                    for h in range(4):
                        c = h // 2; po = (h % 2) * 64
                        P.dve(lambda e, h=h, c=c, po=po: e.scalar_tensor_tensor(
                            out=Sst[po:po + 64, c, :], in0=Sst[po:po + 64, c, :], scalar=CD[h], in1=ps[6 + c][po:po + 64, h * 128:(h + 1) * 128],
                            op0=ALU.mult, op1=ALU.add), r=[b_S, PB[6 + c]], w=[b_S])
                    P.act(lambda e: e.copy(out=Sbf, in_=Sst), r=[b_S], w=[b_Sbf])
                    for h in range(4):
                        P.dve(lambda e, h=h: e.bn_stats(out=bst[:, h, :], in_=ps[5][:, h * 128:(h + 1) * 128]), r=[PB[5]], w=[b_bst])
                    for h in range(4):
                        P.dve(lambda e, h=h: e.bn_aggr(out=mv[:, h, :], in_=bst[:, h, :]), r=[b_bst], w=[b_mv])
                    P.pool(lambda e: e.tensor_scalar(out=rsd[:, 0:4], in0=mv[:, :, 1], scalar1=EPS, scalar2=None, op0=ALU.add), r=[b_mv], w=[b_rsd])
                    P.pool(lambda e: e.tensor_tensor(out=rsd[:, 4:8], in0=rsd[:, 0:4], in1=nhalf[:, 0:4], op=ALU.pow), r=[b_rsd, b_nh], w=[b_rsd])
                    on, bon = on_ring.next()
                    P.dve(lambda e: e.scalar_tensor_tensor(out=rsd[:, 8:12], in0=mv[:, :, 0], scalar=-1.0, in1=rsd[:, 4:8], op0=ALU.mult, op1=ALU.mult),
                          r=[b_mv, b_rsd], w=[b_rsd])
                    for h in range(4):
                        P.act(lambda e, h=h: e.activation(out=on[:, h * 128:(h + 1) * 128], in_=ps[5][:, h * 128:(h + 1) * 128],
                                                          func=AF.Identity, scale=rsd[:, 4 + h:5 + h], bias=rsd[:, 8 + h:9 + h]),
                              r=[PB[5], b_rsd], w=[bon])
                    yrt, byrt = yr_ring.next()
                    P.pool(lambda e: e.tensor_tensor(out=yrt, in0=on, in1=sgr, op=ALU.mult), r=[bon, bsgr], w=[byrt])
                    d.update(yrt=yrt, byrt=byrt)

                def ret_B3(tt):
                    d = RST.pop(tt)
                    tsl = slice(tt * 128, (tt + 1) * 128)
                    yrt, byrt = d["yrt"], d["byrt"]
                    for c in range(4):
                        TR(psb[3][:, c * 128:(c + 1) * 128], yrt[:, c * 128:(c + 1) * 128], ident, r=[byrt, b_cb], w=[PB[3]])
                    P.act(lambda e: e.copy(out=yrT[:, :, tsl], in_=psb[3][:, 0:512].rearrange("p (k n) -> p k n", k=4)),
                          r=[PB[3]], w=[b_yr[tt]])

                for k in range(NT + 4):
                    if k < NT:
                        ret_A1(k)
                    if 0 <= k - 1 < NT:
                        ret_A3(k - 1)
                    if 0 <= k - 2 < NT:
                        ret_B1(k - 2)
                    if 0 <= k - 3 < NT:
                        ret_B2(k - 3)
                    if 0 <= k - 4 < NT:
                        ret_B3(k - 4)
                A.release()
                if dbg_on:
                    dump("d_yr", yrT, b_yr)
                A.release()
                A.mark()
                wsg = A.alloc((8, 1024), BF16); b_wsg = Buf("wsg")
                finalize(l, 1, yrT, b_yr, first=False, prefetch=lambda: load_w(wsg, w_in[l, :, C_SU:C_SU + 1024], w=[b_wsg]))

                ck(6)
                ysT = yT; b_ys = b_y
                A.mark()
                guT = A.alloc((4, S), BF16); b_gu = bufs(NT, "gu")
                wsf = A.alloc((4, 128), F32); b_wsf = Buf("wsf")
                wsb = A.alloc((4, 128), BF16); b_wsb = Buf("wsb")
                P.dma("sp", wsf, wsT_in[l], w=[b_wsf])
                P.dve(lambda e: e.tensor_tensor(out=wsb, in0=wsf, in1=cst[:, K_M01:K_M01 + 128].unsqueeze(1).broadcast_to([128, 4, 128]), op=ALU.mult),
                      r=[b_wsf, b_cst], w=[b_wsb])
                g1_ring = Ring([(A.alloc(512, F32), Buf(f"g1{i}")) for i in range(5)])
                g2_ring = Ring([(A.alloc(512, F32), Buf(f"g2{i}")) for i in range(3)])
                gv_ring = Ring([(A.alloc(512, F32), Buf(f"gv{i}")) for i in range(3)])
                vn_ring = Ring([(A.alloc(512, BF16), Buf(f"vn{i}")) for i in range(3)])
                st6 = A.alloc(6, F32); b_st6 = Buf("st6")
                mv2 = A.alloc(4, F32); b_mv2 = Buf("mv2")
                GC = 2.0 * math.sqrt(2.0 / math.pi)

                def gelu_from_psum(bk, dst_ap, w_dst):
                    g1, bg1 = g1_ring.next(); g2, bg2 = g2_ring.next()
                    P.act(lambda e: e.activation(out=g1, in_=ps[bk], func=AF.Square), r=[PB[bk]], w=[bg1])
                    P.dve(lambda e: e.tensor_scalar(out=g1, in0=g1, scalar1=0.044715, scalar2=1.0, op0=ALU.mult, op1=ALU.add), r=[bg1], w=[bg1])
                    P.dve(lambda e: e.tensor_tensor(out=g1, in0=g1, in1=ps[bk], op=ALU.mult), r=[bg1, PB[bk]], w=[bg1])
                    P.act(lambda e: e.activation(out=g2, in_=g1, func=AF.Sigmoid, scale=GC), r=[bg1], w=[bg2])
                    P.dve(lambda e: e.tensor_tensor(out=dst_ap, in0=g2, in1=ps[bk], op=ALU.mult), r=[bg2, PB[bk]], w=w_dst)

                def gelu_G1(bk):
                    g1, bg1 = g1_ring.next()
                    P.act(lambda e: e.activation(out=g1, in_=ps[bk], func=AF.Square, scale=math.sqrt(0.044715)), r=[PB[bk]], w=[bg1])
                    P.dve(lambda e: e.scalar_tensor_tensor(out=g1, in0=g1, scalar=1.0, in1=ps[bk], op0=ALU.add, op1=ALU.mult), r=[bg1, PB[bk]], w=[bg1])
                    return g1, bg1

                def gelu_G2(bk, g1, bg1, dst_ap, w_dst):
                    g2, bg2 = g2_ring.next()
                    P.act(lambda e: e.activation(out=g2, in_=g1, func=AF.Sigmoid, scale=GC), r=[bg1], w=[bg2])
                    P.dve(lambda e: e.tensor_tensor(out=dst_ap, in0=g2, in1=ps[bk], op=ALU.mult), r=[bg2, PB[bk]], w=w_dst)

                sr = Ring([0, 1, 6])
                gpend = None
                for g in range(4):
                    for tg in range(4):
                        tsl = slice(tg * 512, (tg + 1) * 512)
                        tb = [4 * tg + i for i in range(4)]
                        bk = sr.next()
                        for kc in range(8):
                            MM(ps[bk], wsg[:, kc, g * 128:(g + 1) * 128], hT[:, kc, tsl], kc == 0, kc == 7,
                               r=[b_wsg] + [b_hT[t] for t in tb], w=[PB[bk]])
                        g1, bg1 = gelu_G1(bk)
                        if gpend is not None:
                            gelu_G2(*gpend)
                        gpend = (bk, g1, bg1, guT[:, g, tsl], [b_gu[t] for t in tb])
                gelu_G2(*gpend)
                vr = Ring([2, 3, 7]); orr = Ring([4, 5])
                mv_ring = Ring([(A.alloc(4, F32), Buf(f"mv2_{i}")) for i in range(3)])
                st_ring = Ring([(A.alloc(6, F32), Buf(f"st6_{i}")) for i in range(3)])
                SPT = {}

                def sp_S1a(tt):
                    tsl = slice(tt * 128, (tt + 1) * 128)
                    bk = vr.next()
                    for kc in range(8):
                        MM(ps[bk], hT[:, kc, tsl], wsg[:, kc, 512:1024], kc == 0, kc == 7, r=[b_hT[tt], b_wsg], w=[PB[bk]])
                    g1, bg1 = gelu_G1(bk)
                    SPT[tt] = dict(bk=bk, g1=g1, bg1=bg1)

                def sp_S1b(tt):
                    d = SPT[tt]
                    gv, bgv = gv_ring.next()
                    gelu_G2(d["bk"], d["g1"], d["bg1"], gv, [bgv])
                    st, bst_ = st_ring.next()
                    mvx, bmv = mv_ring.next()
                    P.dve(lambda e: e.bn_stats(out=st, in_=gv), r=[bgv], w=[bst_])
                    P.dve(lambda e: e.bn_aggr(out=mvx[:, 0:2], in_=st), r=[bst_], w=[bmv])
                    d.update(gv=gv, bgv=bgv, mvx=mvx, bmv=bmv)

                def sp_S1c(tt):
                    d = SPT[tt]
                    gv, bgv, mvx, bmv = d["gv"], d["bgv"], d["mvx"], d["bmv"]
                    P.pool(lambda e: e.tensor_scalar(out=mvx[:, 2:3], in0=mvx[:, 1:2], scalar1=EPS, scalar2=None, op0=ALU.add), r=[bmv], w=[bmv])
                    P.pool(lambda e: e.tensor_tensor(out=mvx[:, 3:4], in0=mvx[:, 2:3], in1=nhalf[:, 0:1], op=ALU.pow), r=[bmv, b_nh], w=[bmv])
                    P.dve(lambda e: e.tensor_scalar(out=gv, in0=gv, scalar1=mvx[:, 0:1], scalar2=mvx[:, 3:4], op0=ALU.subtract, op1=ALU.mult),
                          r=[bgv, bmv], w=[bgv])
                    P.pool(lambda e: e.tensor_tensor(out=gv, in0=gv, in1=rowb[:, R_LNG:R_LNG + 512], op=ALU.mult), r=[bgv, b_rowb], w=[bgv])
                    vn, bvn = vn_ring.next()
                    P.pool(lambda e: e.tensor_tensor(out=vn, in0=gv, in1=rowb[:, R_LNB:R_LNB + 512], op=ALU.add), r=[bgv, b_rowb], w=[bvn])
                    d.update(vn=vn, bvn=bvn)

                def sp_S2(tt):
                    d = SPT.pop(tt)
                    vn, bvn = d["vn"], d["bvn"]
                    tsl = slice(tt * 128, (tt + 1) * 128)
                    ob = orr.next()
                    for g in range(4):
                        MM(ps[ob][:, g * 128:(g + 1) * 128], vn[:, g * 128:(g + 1) * 128], wsb[:, g, :], True, True, r=[bvn, b_wsb], w=[PB[ob]])
                    g1, bg1 = g1_ring.next()
                    P.dve(lambda e: e.tensor_tensor(out=g1, in0=ps[ob], in1=rowb[:, R_BS:R_BS + 512], op=ALU.add),
                          r=[PB[ob], b_rowb], w=[bg1])
                    P.dve(lambda e: e.tensor_tensor(out=ysT[:, :, tsl], in0=g1.rearrange("p (g n) -> p g n", g=4), in1=guT[:, :, tsl], op=ALU.mult),
                          r=[bg1, b_gu[tt]], w=[b_ys[tt]])

                for k in range(NT + 3):
                    if k < NT:
                        sp_S1a(k)
                    if 0 <= k - 1 < NT:
                        sp_S1b(k - 1)
                    if 0 <= k - 2 < NT:
                        sp_S1c(k - 2)
                    if 0 <= k - 3 < NT:
                        sp_S2(k - 3)
                A.release()
                if dbg_on:
                    dump("d_ys", ysT, b_ys)
                A.release()
                A.mark()
                wo = A.alloc((8, D), BF16); b_wo = Buf("wo")
                finalize(l, 2, ysT, b_ys, first=False, prefetch=lambda: load_w(wo, w_o[l], w=[b_wo]))
                if dbg_on:
                    dump("d_mg", mgT, b_mg)

                ck(7)
                def update_pass(l, s, lhs_chunks, wts, b_wts, nk, b_lhs, src_x, norm_col, final, dbgname=None):
                    if norm_col is not None or final:
                        zero_ss()
                    ur = Ring([(0, 1), (2, 3)])
                    pend2 = None
                    for tt in range(NT):
                        tsl = slice(tt * 128, (tt + 1) * 128)
                        ba, bb = ur.next()
                        for (bk, coff) in ((ba, 0), (bb, 512)):
                            for kc in range(nk):
                                MM(ps[bk], lhs_chunks[:, kc, tsl], wts[:, kc, coff:coff + 512], kc == 0, kc == nk - 1,
                                   r=[b_lhs[tt], b_wts], w=[PB[bk]])
                        if pend2 is not None:
                            pend2()
                            pend2 = None
                        xt, bx = xt_ring.next()
                        P.dma("sp", xt, src_x[tt * 128:(tt + 1) * 128, :], r=[b_xd[tt]], w=[bx])
                        P.dve(lambda e, xt=xt, ba=ba: e.tensor_tensor(out=xt[:, 0:512], in0=xt[:, 0:512], in1=ps[ba], op=ALU.add), r=[bx, PB[ba]], w=[bx])
                        P.dve(lambda e, xt=xt, bb=bb: e.tensor_tensor(out=xt[:, 512:1024], in0=xt[:, 512:1024], in1=ps[bb], op=ALU.add), r=[bx, PB[bb]], w=[bx])
                        if not final:
                            P.dma("pool", out[s, tt * 128:(tt + 1) * 128, :], xt, r=[bx], w=[b_xd[tt]])
                        if dbgname is not None:
                            P.dma("sp", dbg[dbgname][tt * 128:(tt + 1) * 128, :], xt, r=[bx])
                        if final:
                            norm_tile(xt, bx, None, tt, None, None, final_dst=out[s, tt * 128:(tt + 1) * 128, :])
                        elif norm_col is not None:
                            pend2 = norm_tile(xt, bx, norm_col, tt, hT, b_hT, defer=True)
                    if pend2 is not None:
                        pend2()

                src_x = x_in[s] if l == 0 else out[s]
                update_pass(l, s, mgT, wo, b_wo, 8, b_mg, src_x, R_NFFN, False, "d_x1" if dbg_on else None)
                A.release()
                A.release()

                ck(8)
                A.mark()
                actT = A.alloc((11, S), BF16); b_act = bufs(NT, "act")
                w2 = A.alloc((11, D), BF16); b_w2 = Buf("w2")
                wblk = Ring([(A.alloc((8, 256), BF16), Buf(f"wblk{i}")) for i in range(3)])
                sl_ring = Ring([(A.alloc(512, F32), Buf(f"sl{i}")) for i in range(2)])
                for half in range(2):
                    fr = Ring([(0, 2), (1, 3)])
                    pend = []

                    def issue(fc):
                        wt, bwt = wblk.next()
                        gc = half * 1408 + fc * 128
                        P.dma("pool", wt[:, :, 0:128], w_f1[l, :, gc:gc + 128].rearrange("(k p) n -> p k n", p=128), w=[bwt])
                        P.dma("pool", wt[:, :, 128:256], w_f1[l, :, DFF + gc:DFF + gc + 128].rearrange("(k p) n -> p k n", p=128), w=[bwt])
                        pend.append((wt, bwt))
                    issue(0); issue(1)
                    load_w(w2, w_f2[l, half * 1408:(half + 1) * 1408, :], r=(), w=[b_w2])
                    for fc in range(11):
                        if fc + 2 < 11:
                            issue(fc + 2)
                        wt, bwt = pend[fc]
                        for tg in range(4):
                            tsl = slice(tg * 512, (tg + 1) * 512)
                            tb = [4 * tg + i for i in range(4)]
                            gb, ub = fr.next()
                            for kc in range(8):
                                MM(ps[gb], wt[:, kc, 0:128], hT[:, kc, tsl], kc == 0, kc == 7, r=[bwt] + [b_hT[t] for t in tb], w=[PB[gb]])
                            for kc in range(8):
                                MM(ps[ub], wt[:, kc, 128:256], hT[:, kc, tsl], kc == 0, kc == 7, r=[bwt] + [b_hT[t] for t in tb], w=[PB[ub]])
                            sl, bsl = sl_ring.next()
                            P.act(lambda e, sl=sl, gb=gb: e.activation(out=sl, in_=ps[gb], func=AF.Silu), r=[PB[gb]], w=[bsl])
                            P.dve(lambda e, sl=sl, ub=ub, fc=fc, tsl=tsl: e.tensor_tensor(out=actT[:, fc, tsl], in0=sl, in1=ps[ub], op=ALU.mult),
                                  r=[bsl, PB[ub]], w=[b_act[t] for t in tb])
                    last_layer = (l == nlayers - 1)
                    if half == 0:
                        update_pass(l, s, actT, w2, b_w2, 11, b_act, out[s], None, False)
                    else:
                        update_pass(l, s, actT, w2, b_w2, 11, b_act, out[s], None if last_layer else R_NNEXT, last_layer,
                                    "d_x2" if dbg_on else None)
                A.release()
        try:
            body()
        except _Stop:
            pass
        P.emit()
        print("ops/sig/per-engine:", P.stats, "arena peak KiB", A.peak / 1024.0, flush=True)
    return nc


_CACHE = {}


def prep_shared(inputs):
    f = lambda a: np.ascontiguousarray(np.asarray(a), dtype=np.float32)
    w_ukv = f(inputs["mla_w_ukv"]).reshape(L, 256, 8, 128)
    w_ukv_p = np.ascontiguousarray(np.concatenate([w_ukv[..., 0:64].reshape(L, 256, 512), w_ukv[..., 64:128].reshape(L, 256, 512)], axis=-1))
    wsT = np.ascontiguousarray(f(inputs["sg_ws"]).transpose(0, 3, 1, 2))
    rowvec = np.concatenate([f(inputs["norm_mix"]), f(inputs["norm_ffn"]), f(inputs["mla_q_norm"]), f(inputs["mla_kv_norm"]),
                             f(inputs["sg_ln_g"]), f(inputs["sg_ln_b"]), f(inputs["sg_b"]).reshape(L, 512),
                             np.concatenate([f(inputs["norm_mix"])[1:], np.zeros((1, D), np.float32)], axis=0)], axis=1)
    assert rowvec.shape == (L, NROW)
    bg = f(inputs["b_gate"]).reshape(L, 4, 8, 128).transpose(0, 3, 1, 2).reshape(L, 128, 32)
    cw = f(inputs["conv_w"]).reshape(L, 3, 4, 128).transpose(0, 3, 2, 1).reshape(L, 128, 12)
    pvec = np.ascontiguousarray(np.concatenate([bg, cw], axis=2))
    consts, _ = host_consts()
    return {
        "w_in": f(inputs["w_in"]), "w_branch": f(inputs["w_branch"]), "w_out": f(inputs["w_out"]),
        "w_ffn_in": f(inputs["w_ffn_in"]), "w_ffn_out": f(inputs["w_ffn_out"]),
        "w_uq": f(inputs["mla_w_uq"]), "w_ukv": w_ukv_p, "wsT": wsT, "rowvec": np.ascontiguousarray(rowvec),
        "fnorm": f(inputs["final_norm"]).reshape(1, D), "pvec": pvec, "consts": consts,
    }


def kernel(**inputs):
    ncores = 8
    x = np.ascontiguousarray(np.asarray(inputs["x"]), dtype=np.float32)
    pos = np.asarray(inputs["positions"]).astype(np.int32)
    B = x.shape[0]
    nseq = B // ncores
    shared = prep_shared(inputs)
    if "nc" not in _CACHE:
        _CACHE["nc"] = build(nseq=nseq)
    nc = _CACHE["nc"]
    in_maps = []
    for c in range(ncores):
        m = dict(shared)
        m["x"] = np.ascontiguousarray(x[c * nseq:(c + 1) * nseq])
        p = pos[c * nseq:(c + 1) * nseq].reshape(nseq, NT, 128).transpose(0, 2, 1)
        m["pos"] = np.ascontiguousarray(p)
        in_maps.append(m)
    res = run_bass_kernel_spmd(nc, in_maps, core_ids=list(range(ncores)))
    return np.concatenate([r["out"] for r in res.results], axis=0)
```
